# Optimizing a Trainium2 kernel written in Bass

```python
import math
import jax, jax.numpy as jnp
from jax import lax
import numpy as np

D_MODEL = 1024
BATCH = 32
SEQ = 256
DEPTH = 1
DEC_BATCH = 8
DEC_SEQ = 1024
PAST_LEN = 256

GRID_W = 64
A_WIDTH = D_MODEL // 2
B_WIDTH = D_MODEL - A_WIDTH
H_A = 4
DV_A = A_WIDTH // H_A
DK_A = DV_A // 2
H_B = 4
DV_B = B_WIDTH // H_B
DK_B = DV_B // 2
GATE_RANK = 16
GATE_NORM = 16.0
GLA_CHUNK = 64
D_FF = 4 * D_MODEL
N_MOD = 6
Q_BLOCK = 128
ROPE_BASE = 10000.0
EPS = 1e-6

kernel_name = "hybrid_diffattn_gla_prefix_dit_step"


def rmsnorm(x, g):
    xf = x.astype(jnp.float32)
    y = xf * lax.rsqrt(jnp.mean(xf * xf, axis=-1, keepdims=True) + EPS)
    return (y * g.astype(jnp.float32)).astype(x.dtype)


def rope_1d(x, pos):
    half = x.shape[-1] // 2
    inv = ROPE_BASE ** (-jnp.arange(half, dtype=jnp.float32) / half)
    ang = pos.astype(jnp.float32)[:, None] * inv[None, :]
    cos, sin = jnp.cos(ang).astype(x.dtype), jnp.sin(ang).astype(x.dtype)
    x1, x2 = x[..., :half], x[..., half:]
    return jnp.concatenate([x1 * cos - x2 * sin, x1 * sin + x2 * cos], axis=-1)


def rope2d(x, row_pos, col_pos):
    d = x.shape[-1] // 2
    return jnp.concatenate([rope_1d(x[..., :d], row_pos), rope_1d(x[..., d:], col_pos)], axis=-1)


def diff_attention(q1, q2, k1, k2, v, lam):
    B, H, Lq, d = q1.shape
    nb = Lq // Q_BLOCK
    scale = d ** -0.5

    def to_blocks(q):
        return jnp.moveaxis(q.reshape(B, H, nb, Q_BLOCK, d), 2, 0)

    def one_block(qs):
        a1, a2 = qs
        s1 = jnp.einsum('bhqd,bhkd->bhqk', a1, k1).astype(jnp.float32) * scale
        s2 = jnp.einsum('bhqd,bhkd->bhqk', a2, k2).astype(jnp.float32) * scale
        p = jax.nn.softmax(s1, axis=-1) - lam * jax.nn.softmax(s2, axis=-1)
        return jnp.einsum('bhqk,bhkv->bhqv', p.astype(v.dtype), v)

    o = lax.map(one_block, (to_blocks(q1), to_blocks(q2)))
    return jnp.moveaxis(o, 0, 2).reshape(B, H, Lq, v.shape[-1])


def gla_scan(q, k, v, g, s0):
    B, H, L, dk = q.shape
    dv = v.shape[-1]
    n = L // GLA_CHUNK

    def chunks(t):
        return jnp.moveaxis(t.astype(jnp.float32).reshape(B, H, n, GLA_CHUNK, t.shape[-1]), 2, 0)

    causal = jnp.tril(jnp.ones((GLA_CHUNK, GLA_CHUNK), dtype=bool))[:, :, None]

    def step(S, inp):
        qc, kc, vc, gc = inp
        b = jnp.cumsum(gc, axis=-2)
        o_inter = jnp.einsum('bhtk,bhkv->bhtv', qc * jnp.exp(b), S)
        diff = b[..., :, None, :] - b[..., None, :, :]
        decay = jnp.where(causal, jnp.exp(jnp.where(causal, diff, 0.0)), 0.0)
        att = jnp.einsum('bhtk,bhsk,bhtsk->bhts', qc, kc, decay)
        o = o_inter + jnp.einsum('bhts,bhsv->bhtv', att, vc)
        b_last = b[..., -1:, :]
        S_new = jnp.exp(b_last)[..., 0, :, None] * S + jnp.einsum('bhsk,bhsv->bhkv', kc * jnp.exp(b_last - b), vc)
        return S_new, o

    S_fin, o = lax.scan(step, s0.astype(jnp.float32), (chunks(q), chunks(k), chunks(v), chunks(g)))
    return jnp.moveaxis(o, 0, 2).reshape(B, H, L, dv), S_fin


def adaln(cvec, lw):
    m = jax.nn.silu(cvec) @ lw['w_ada'] + lw['b_ada']
    return [t[:, None, :] for t in jnp.split(m, N_MOD, axis=-1)]


def token_mixers(h, lw, lam_init, pos, ctx_k, ctx_v, s0_f, s0_b):
    B, L, _ = h.shape
    sizes = [H_A * 2 * DK_A, H_A * 2 * DK_A, H_A * DV_A, H_B * DK_B, H_B * DK_B,
             H_B * DV_B, H_B * DV_B, GATE_RANK, GATE_RANK]
    cuts = [int(s) for s in np.cumsum(sizes)[:-1]]
    qa, ka, va, qb, kb, vb, rb, glf, glb = jnp.split(h @ lw['w_in'], cuts, axis=-1)

    def heads(t, nh):
        return t.reshape(B, L, nh, -1).transpose(0, 2, 1, 3)

    qa, ka, va = heads(qa, H_A), heads(ka, H_A), heads(va, H_A)
    q1, q2, k1, k2 = qa[..., :DK_A], qa[..., DK_A:], ka[..., :DK_A], ka[..., DK_A:]
    if pos is not None:
        row_pos, col_pos = pos
        q1, q2, k1, k2 = (rope2d(t, row_pos, col_pos) for t in (q1, q2, k1, k2))
    own_k = jnp.concatenate([k1, k2], axis=-1)
    if ctx_k is not None:
        keys = jnp.concatenate([ctx_k, own_k], axis=2)
        values = jnp.concatenate([ctx_v, va], axis=2)
    else:
        keys, values = own_k, va
    f32 = jnp.float32
    lam = (jnp.exp(jnp.sum(lw['lam_q1'].astype(f32) * lw['lam_k1'].astype(f32)))
           - jnp.exp(jnp.sum(lw['lam_q2'].astype(f32) * lw['lam_k2'].astype(f32))) + lam_init)
    o_a = diff_attention(q1, q2, keys[..., :DK_A], keys[..., DK_A:], values, lam)
    o_a = rmsnorm(o_a, lw['diff_norm']) * (1.0 - lam_init)
    o_a = o_a.transpose(0, 2, 1, 3).reshape(B, L, H_A * DV_A)

    qb = heads(qb, H_B) * (DK_B ** -0.5)
    kb, vb = heads(kb, H_B), heads(vb, H_B)
    g_f = heads(jax.nn.log_sigmoid((glf @ lw['w_gate_fwd'] + lw['b_gate_fwd']).astype(f32)) / GATE_NORM, H_B)
    g_b = heads(jax.nn.log_sigmoid((glb @ lw['w_gate_bwd'] + lw['b_gate_bwd']).astype(f32)) / GATE_NORM, H_B)
    if s0_f is None:
        s0_f = jnp.zeros((B, H_B, DK_B, DV_B), f32)
        s0_b = jnp.zeros((B, H_B, DK_B, DV_B), f32)
    o_f, s_f = gla_scan(qb, kb, vb, g_f, s0_f)
    flip = lambda t: jnp.flip(t, axis=2)
    o_b, s_b = gla_scan(flip(qb), flip(kb), flip(vb), flip(g_b), s0_b)
    o_g = rmsnorm(o_f + flip(o_b), lw['gla_norm'])
    o_g = (o_g.transpose(0, 2, 1, 3).reshape(B, L, H_B * DV_B) * jax.nn.silu(rb.astype(f32))).astype(h.dtype)

    out = jnp.concatenate([o_a, o_g], axis=-1) @ lw['w_out']
    return out, (own_k, va, s_f, s_b)


def layer(x, cvec, lw, lam_init, pos, ctx_k, ctx_v, s0_f, s0_b):
    shift1, scale1, gate1, shift2, scale2, gate2 = adaln(cvec, lw)
    h = rmsnorm(x, lw['norm_attn_pre']) * (1.0 + scale1) + shift1
    mix, ctx_tensors = token_mixers(h, lw, lam_init, pos, ctx_k, ctx_v, s0_f, s0_b)
    x = x + gate1 * rmsnorm(mix, lw['norm_attn_post'])
    h2 = rmsnorm(x, lw['norm_mlp_pre']) * (1.0 + scale2) + shift2
    f = jnp.square(jax.nn.relu(h2 @ lw['w_mlp1'])) @ lw['w_mlp2']
    x = x + gate2 * rmsnorm(f, lw['norm_mlp_post'])
    return x, ctx_tensors


def setup_inputs(seed: int = 0) -> dict:
    key = jax.random.key(seed)
    ks = jax.random.split(key, 32)
    nrm = lambda k, shape, s=1.0: jax.random.normal(k, shape, jnp.float32) * s
    gain = lambda k: 1.0 + nrm(k, (DEPTH, D_MODEL), 0.05)
    in_cols = 2 * H_A * 2 * DK_A + H_A * DV_A + 2 * H_B * DK_B + 2 * H_B * DV_B + 2 * GATE_RANK
    return {
        "x_prompt": nrm(ks[0], (BATCH, SEQ, D_MODEL)),
        "x_sample": nrm(ks[1], (DEC_BATCH, DEC_SEQ, D_MODEL)),
        "c": nrm(ks[2], (DEC_BATCH, D_MODEL)),
        "cache_k": nrm(ks[3], (DEC_BATCH, DEPTH, H_A, PAST_LEN, 2 * DK_A)),
        "cache_v": nrm(ks[4], (DEC_BATCH, DEPTH, H_A, PAST_LEN, DV_A)),
        "state_fwd": nrm(ks[5], (DEC_BATCH, DEPTH, H_B, DK_B, DV_B), 0.5),
        "state_bwd": nrm(ks[6], (DEC_BATCH, DEPTH, H_B, DK_B, DV_B), 0.5),
        "c_ctx": nrm(ks[7], (D_MODEL,)),
        "w_ada": nrm(ks[8], (DEPTH, D_MODEL, N_MOD * D_MODEL), 0.5 * D_MODEL ** -0.5),
        "b_ada": nrm(ks[9], (DEPTH, N_MOD * D_MODEL), 0.02),
        "norm_attn_pre": gain(ks[10]),
        "norm_attn_post": gain(ks[11]),
        "norm_mlp_pre": gain(ks[12]),
        "norm_mlp_post": gain(ks[13]),
        "w_in": nrm(ks[14], (DEPTH, D_MODEL, in_cols), D_MODEL ** -0.5),
        "w_gate_fwd": nrm(ks[15], (DEPTH, GATE_RANK, H_B * DK_B), GATE_RANK ** -0.5),
        "b_gate_fwd": nrm(ks[16], (DEPTH, H_B * DK_B), 0.1),
        "w_gate_bwd": nrm(ks[17], (DEPTH, GATE_RANK, H_B * DK_B), GATE_RANK ** -0.5),
        "b_gate_bwd": nrm(ks[18], (DEPTH, H_B * DK_B), 0.1),
        "lam_q1": nrm(ks[19], (DEPTH, DK_A), 0.1),
        "lam_k1": nrm(ks[20], (DEPTH, DK_A), 0.1),
        "lam_q2": nrm(ks[21], (DEPTH, DK_A), 0.1),
        "lam_k2": nrm(ks[22], (DEPTH, DK_A), 0.1),
        "diff_norm": 1.0 + nrm(ks[23], (DEPTH, DV_A), 0.05),
        "gla_norm": 1.0 + nrm(ks[24], (DEPTH, DV_B), 0.05),
        "w_out": nrm(ks[25], (DEPTH, D_MODEL, D_MODEL), D_MODEL ** -0.5),
        "w_mlp1": nrm(ks[26], (DEPTH, D_MODEL, D_FF), D_MODEL ** -0.5),
        "w_mlp2": nrm(ks[27], (DEPTH, D_FF, D_MODEL), D_FF ** -0.5),
    }


def reference(x_prompt, x_sample, c, cache_k, cache_v, state_fwd, state_bwd, c_ctx,
              w_ada, b_ada, norm_attn_pre, norm_attn_post, norm_mlp_pre, norm_mlp_post,
              w_in, w_gate_fwd, b_gate_fwd, w_gate_bwd, b_gate_bwd,
              lam_q1, lam_k1, lam_q2, lam_k2, diff_norm, gla_norm, w_out, w_mlp1, w_mlp2):
    n_lat = x_sample.shape[1]
    ROWS = n_lat // GRID_W
    row_pos = jnp.repeat(jnp.arange(ROWS, dtype=jnp.int32), GRID_W)
    col_pos = jnp.arange(ROWS * GRID_W, dtype=jnp.int32) % GRID_W
    pos = (row_pos, col_pos)

    y_prompt, y_sample = x_prompt, x_sample
    new_k, new_v, new_sf, new_sb = [], [], [], []
    for l in range(DEPTH):
        lw = {
            'w_ada': w_ada[l], 'b_ada': b_ada[l],
            'norm_attn_pre': norm_attn_pre[l], 'norm_attn_post': norm_attn_post[l],
            'norm_mlp_pre': norm_mlp_pre[l], 'norm_mlp_post': norm_mlp_post[l],
            'w_in': w_in[l], 'w_gate_fwd': w_gate_fwd[l], 'b_gate_fwd': b_gate_fwd[l],
            'w_gate_bwd': w_gate_bwd[l], 'b_gate_bwd': b_gate_bwd[l],
            'lam_q1': lam_q1[l], 'lam_k1': lam_k1[l], 'lam_q2': lam_q2[l], 'lam_k2': lam_k2[l],
            'diff_norm': diff_norm[l], 'gla_norm': gla_norm[l], 'w_out': w_out[l],
            'w_mlp1': w_mlp1[l], 'w_mlp2': w_mlp2[l],
        }
        lam_init = 0.8 - 0.6 * math.exp(-0.3 * l)
        y_prompt, (k_c, v_c, s_f, s_b) = layer(y_prompt, c_ctx[None, :], lw, lam_init,
                                                None, None, None, None, None)
        new_k.append(k_c)
        new_v.append(v_c)
        new_sf.append(s_f)
        new_sb.append(s_b)
        y_sample, _ = layer(y_sample, c, lw, lam_init, pos,
                            cache_k[:, l], cache_v[:, l], state_fwd[:, l], state_bwd[:, l])
    new_cache_k = jnp.stack(new_k, axis=1)
    new_cache_v = jnp.stack(new_v, axis=1)
    new_state_fwd = jnp.stack(new_sf, axis=1)
    new_state_bwd = jnp.stack(new_sb, axis=1)
    return (y_prompt, y_sample, new_cache_k, new_cache_v, new_state_fwd, new_state_bwd)
```

```python
import math
from contextlib import ExitStack

import numpy as np
import concourse.bass as bass
import concourse.mybir as mybir
from concourse.bass_utils import run_bass_kernel_spmd

F32 = mybir.dt.float32
BF16 = mybir.dt.bfloat16
AF = mybir.ActivationFunctionType
ALU = mybir.AluOpType
AX = mybir.AxisListType

D = 1024
T = 1024
NT = 8
EPS = 1e-6
LAM_INIT = 0.8 - 0.6 * math.exp(0.0)
N_CORES = 8
STOP = [0]
DBG = {}


class _Stop(Exception):
    pass


class Buf:
    __slots__ = ("name", "w", "r", "excl", "same_ok")

    def __init__(self, name="", excl=False, same_ok=False):
        self.name = name
        self.w = None
        self.r = {}
        self.same_ok = same_ok
        self.excl = excl

    def inherit(self, olds):
        for o in olds:
            if o.w is not None:
                self.r[o.w[0]] = max(self.r.get(o.w[0], 0), o.w[1])
            for k, v in o.r.items():
                self.r[k] = max(self.r.get(k, 0), v)
        return self


class Sched:
    COMPUTE = ("pe", "act", "dve", "pool")

    def __init__(self, nc, stack, n_dma_sems=10):
        self.nc = nc
        self.items = {e: [] for e in ("pe", "act", "dve", "pool", "sp")}
        self.sems = {}
        for e in self.COMPUTE:
            self.sems[e] = stack.enter_context(nc.semaphore("s_" + e))
        self.cnt = {e: 0 for e in self.COMPUTE}
        self.dq = {}
        for q in ("sp", "pool"):
            lst = []
            for i in range(n_dma_sems):
                key = "d_%s_%d" % (q, i)
                self.sems[key] = stack.enter_context(nc.semaphore(key))
                lst.append(key)
            self.dq[q] = {"keys": lst, "n": 0}
        self.known = {e: {} for e in self.items}
        self.stopped = False

    def _deps(self, eng, reads, writes):
        deps = {}

        def add(ev, b, is_write):
            if ev is None:
                return
            k, v = ev
            if k == eng and (eng == "pe" or (is_write and b.same_ok)):
                return
            if deps.get(k, 0) < v:
                deps[k] = v
        for b in reads:
            add(b.w, b, False)
            if b.excl:
                for k, v in b.r.items():
                    if k != eng:
                        add((k, v), b, False)
        for b in writes:
            add(b.w, b, True)
            for k, v in b.r.items():
                add((k, v), b, True)
        out = []
        kn = self.known[eng]
        for k, v in deps.items():
            if kn.get(k, 0) < v:
                kn[k] = v
                out.append((k, v))
        return out

    def _mark(self, ev, reads, writes):
        k, v = ev
        for b in reads:
            if b.r.get(k, 0) < v:
                b.r[k] = v
        for b in writes:
            b.w = ev
            b.r = {}

    def op(self, eng, fn, reads=(), writes=()):
        if self.stopped:
            return None
        waits = self._deps(eng, reads, writes)
        self.cnt[eng] += 1
        ev = (eng, self.cnt[eng])
        self.items[eng].append((waits, fn, (eng, 1)))
        self._mark(ev, reads, writes)
        return ev

    def dma(self, q, fn, reads=(), writes=()):
        if self.stopped:
            return None
        d = self.dq[q]
        i = d["n"]
        d["n"] += 1
        nk = len(d["keys"])
        key = d["keys"][i % nk]
        val = 16 * (i // nk + 1)
        waits = self._deps(q, reads, writes)
        if i >= nk and self.known[q].get(key, 0) < val - 16:
            self.known[q][key] = val - 16
            waits.append((key, val - 16))
        ev = (key, val)
        self.items[q].append((waits, fn, (key, 16)))
        self._mark(ev, reads, writes)
        return ev

    def finish(self, eng="sp"):
        waits = []
        for e in self.COMPUTE:
            if self.cnt[e] > 0:
                waits.append((e, self.cnt[e]))
        for q, d in self.dq.items():
            nk = len(d["keys"])
            for j, key in enumerate(d["keys"]):
                n = (d["n"] - j + nk - 1) // nk if d["n"] > j else 0
                if n > 0:
                    waits.append((key, 16 * n))
        self.items[eng].append((waits, None, None))

    def emit(self, block):
        sems = self.sems
        needed = {e: set() for e in self.COMPUTE}
        for lst in self.items.values():
            for waits, fn, inc in lst:
                for k, v in waits:
                    if k in needed:
                        needed[k].add(v)
        rank = {}
        for e in self.COMPUTE:
            rank[e] = {v: i + 1 for i, v in enumerate(sorted(needed[e]))}

        def run(engobj, lst):
            idx = 0
            for waits, fn, inc in lst:
                ws = [(k, rank[k][v] if k in rank else v) for k, v in waits]
                if fn is None:
                    for k, v in ws:
                        engobj.wait_ge(sems[k], v)
                    continue
                for k, v in ws[:-1]:
                    engobj.wait_ge(sems[k], v)
                ins = fn(engobj)
                if ws:
                    ins._wait_ge(sems[ws[-1][0]], ws[-1][1])
                if inc[0] in rank:
                    idx += 1
                    if idx in rank[inc[0]]:
                        ins.then_inc(sems[inc[0]], 1)
                else:
                    ins.then_inc(sems[inc[0]], inc[1])

        @block.tensor
        def _(e):
            run(e, self.items["pe"])

        @block.scalar
        def _(e):
            run(e, self.items["act"])

        @block.vector
        def _(e):
            run(e, self.items["dve"])

        @block.gpsimd
        def _(e):
            run(e, self.items["pool"])

        @block.sync
        def _(e):
            run(e, self.items["sp"])
        self.stats = {e: (self.cnt[e], len(rank[e])) for e in self.COMPUTE}


def _host_consts():
    c = {}
    c["ident_f"] = np.eye(128, dtype=np.float32)
    p = np.arange(128)
    same = (p[:, None] // 64) == (p[None, :] // 64)
    le = p[:, None] <= p[None, :]
    ge = p[:, None] >= p[None, :]
    lt = p[:, None] < p[None, :]
    gt = p[:, None] > p[None, :]
    sc = np.float32(-1.0 / 16.0)
    masks = np.zeros((6, 128, 128), np.float32)
    masks[0] = (same & le)
    masks[1] = (same & ge)
    masks[2] = (same & le) * sc
    masks[3] = (same & ge) * sc
    masks[4] = (same & gt) * sc
    masks[5] = (same & lt) * sc
    c["masks"] = np.ascontiguousarray(masks.transpose(1, 0, 2))
    d = p % 64
    is_col = (d >= 32)
    dd = d % 32
    fi = dd % 16
    second = dd >= 16
    inv = (np.float32(10000.0) ** (-(np.arange(16, dtype=np.float32)) / np.float32(16.0))).astype(np.float32)
    t = np.arange(T)
    rowpos = (t // 64).astype(np.float32)
    colpos = (t % 64).astype(np.float32)
    pos = np.where(is_col[:, None], colpos[None, :], rowpos[None, :]).astype(np.float32)
    ang = (pos * inv[fi][:, None]).astype(np.float32)
    cos = np.cos(ang).astype(np.float32)
    sin = np.sin(ang).astype(np.float32)
    sg = np.where(second[:, None], sin, -sin).astype(np.float32)
    c["rope"] = np.ascontiguousarray(np.stack([cos, sg], axis=1))
    swap = np.where(second, p - 16, p + 16)
    perm = np.zeros((128, 128), np.float32)
    perm[swap, p] = 1.0
    c["perm"] = perm
    sel = np.zeros((2, 2, 128), np.float32)
    sel[0, 0, :] = 1.0
    sel[1, 1, :] = 1.0
    c["sel"] = sel
    return c


def build_nc(debug=False):
    nc = bass.Bass("TRN2", target_bir_lowering=False)

    def din(name, shape):
        return nc.dram_tensor(name, list(shape), F32, kind="ExternalInput").ap()

    def dout(name, shape):
        return nc.dram_tensor(name, list(shape), F32, kind="ExternalOutput").ap()

    x_d = din("x", [2 * T, D])
    cvec_d = din("cvec", [2, D])
    cache_k_d = din("cache_k", [4, 256, 128])
    cache_v_d = din("cache_v", [4, 256, 128])
    state_d = [din("state_f", [4, 64, 128]), din("state_b", [4, 64, 128])]
    w_ada_d = din("w_ada", [D, 6 * D])
    b_ada_d = din("b_ada", [6 * D])
    npre1_d = din("norm_attn_pre", [D])
    npost1_d = din("norm_attn_post", [D])
    npre2_d = din("norm_mlp_pre", [D])
    npost2_d = din("norm_mlp_post", [D])
    w_in_d = din("w_in", [D, 3104])
    wg_d = [din("w_gate_fwd", [16, 256]), din("w_gate_bwd", [16, 256])]
    bg_d = [din("b_gate_fwd", [256]), din("b_gate_bwd", [256])]
    lam_d = [din("lam_q1", [64]), din("lam_k1", [64]), din("lam_q2", [64]), din("lam_k2", [64])]
    dnorm_d = din("diff_norm", [128])
    gnorm_d = din("gla_norm", [128])
    w_out_d = din("w_out", [D, D])
    w1_d = din("w_mlp1", [D, 4 * D])
    w2_d = din("w_mlp2", [4 * D, D])
    identf_d = din("ident_f", [128, 128])
    masks_d = din("masks", [128, 6, 128])
    rope_d = din("rope", [128, 2, T])
    perm_d = din("perm", [128, 128])
    sel_d = din("sel", [2, 2, 128])

    y_d = dout("y", [2 * T, D])
    newk_d = dout("new_k", [4, 4, 256, 128])
    newv_d = dout("new_v", [4, 4, 256, 128])
    news_d = [dout("new_sf", [4, 4, 64, 128]), dout("new_sb", [4, 4, 64, 128])]

    st = ExitStack()
    with st:
        S = Sched(nc, st)

        def stage(k):
            if STOP[0] == k:
                S.stopped = True

        def sb(name, shape, dt):
            return st.enter_context(nc.sbuf_tensor(name, list(shape), dt))

        IDB = sb("IDB", [128, 128], BF16)
        IDF = sb("IDF", [128, 128], F32)
        MASKS = sb("MASKS", [128, 6, 128], F32)
        PERM = sb("PERM", [128, 128], BF16)
        ROPE = sb("ROPE", [128, 2, T], F32)
        SEL = sb("SEL", [2, 2, 128], F32)
        SC = sb("SC", [128, 8, 2], BF16)
        MODC = sb("MODC", [128, 4, 8, 2], F32)
        AB = sb("AB", [128, 4, 8, 2], F32)
        GB = sb("GB", [128, 2, 2, D], F32)
        DG = sb("DG", [128, 4, 128], F32)
        GG = sb("GG", [128, 4, 128], F32)
        BG = sb("BG", [128, 2, 256], F32)
        WG = sb("WG", [48, 256], BF16)
        LAMT = sb("LAMT", [128, 4, 64], F32)
        LAMS = sb("LAMS", [128, 8], F32)
        SMALL = sb("SMALL", [128, 96], F32)
        XT = sb("XT", [128, 2, D], F32)
        XN = sb("XN", [128, 3, D], BF16)
        HT = sb("HT", [128, 8, T], BF16)
        OT = sb("OT", [128, 8, T], BF16)
        WS = sb("WS", [128, 3, 4096], BF16)
        RBYTES = 96 * 1024
        R = sb("R", [128, RBYTES // 2], BF16)
        G = st.enter_context(nc.psum_tensor("G", [128, 8, 512], F32))
        PTR = G[:, 6:8, :].bitcast(BF16)
        block = st.enter_context(nc.Block())

        def rview(off, shape, dt):
            esz = 2 if dt == BF16 else 4
            n = 1
            for s_ in shape[1:]:
                n *= s_
            assert off % 4 == 0 and off + n * esz <= RBYTES, (off, shape)
            ap = R[0:shape[0], off // 2:(off + n * esz) // 2]
            if dt != BF16:
                ap = ap.bitcast(dt)
            if len(shape) == 3:
                ap = ap.rearrange("p (a b) -> p a b", a=shape[1])
            elif len(shape) == 4:
                ap = ap.rearrange("p (a b c) -> p a b c", a=shape[1], b=shape[2])
            return ap

        bG = [Buf("G%d" % i, excl=True) for i in range(8)]
        bPTR = bG[6:8]
        bWS = [Buf("WS%d" % i) for i in range(3)]
        bXT = [Buf("XT%d" % i) for i in range(2)]
        bXN = [Buf("XN0"), Buf("XN1"), Buf("XNjunk", same_ok=True)]
        bHT = [Buf("HT%d" % i, same_ok=True) for i in range(NT)]
        bOT = [Buf("OT%d" % i, same_ok=True) for i in range(NT)]
        bC = Buf("consts")
        bSM = {}

        def smb(name):
            if name not in bSM:
                bSM[name] = Buf(name)
            return bSM[name]

        R_live = []

        def new_phase(names):
            nonlocal R_live
            out = {}
            for n_ in names:
                out[n_] = Buf(n_).inherit(R_live)
            R_live = list(out.values())
            return out

        small_next = [0]

        def small(n):
            a = small_next[0]
            small_next[0] += n
            assert small_next[0] <= 64
            return SMALL[:, a:a + n]

        ws_n = [0]

        def ws_load(src_list):
            slot = ws_n[0] % 3
            ws_n[0] += 1
            view = WS[:, slot, :].rearrange("p (k c) -> p k c", k=8)
            for c0, ncol, src in src_list:
                S.dma("pool", lambda e, c0=c0, ncol=ncol, src=src, view=view:
                      e.dma_start(out=view[:, :, c0:c0 + ncol], in_=src), writes=[bWS[slot]])
            return slot, view

        def wcols(w_ap, c0, ncol):
            return w_ap.rearrange("(kc p) c -> p kc c", p=128)[:, :, c0:c0 + ncol]

        def wrows(w_ap, r0, c0, ncol):
            return w_ap[r0:r0 + 1024, :].rearrange("(kc p) c -> p kc c", p=128)[:, :, c0:c0 + ncol]

        const_bufs = []

        def cw():
            b_ = Buf("c%d" % len(const_bufs))
            const_bufs.append(b_)
            return b_

        def cr():
            return list(const_bufs)

        S.dma("sp", lambda e: e.dma_start(out=IDF[:], in_=identf_d), writes=[cw()])
        S.dma("pool", lambda e: e.dma_start(out=IDB[:], in_=identf_d), writes=[cw()])
        S.dma("sp", lambda e: e.dma_start(out=MASKS[:], in_=masks_d), writes=[cw()])
        S.dma("pool", lambda e: e.dma_start(out=PERM[:], in_=perm_d), writes=[cw()])
        S.dma("sp", lambda e: e.dma_start(out=ROPE[:], in_=rope_d), writes=[cw()])
        S.dma("sp", lambda e: e.dma_start(out=SEL[:], in_=sel_d), writes=[cw()])
        ROWS = sb("ROWS", [64, 128], F32)
        COLS = sb("COLS", [128, 64], F32)
        S.dma("sp", lambda e: e.dma_start(out=ROWS[0:16, :], in_=cvec_d.rearrange("t (k p) -> (t k) p", p=128)), writes=[cw()])
        for mi, m in enumerate((0, 1, 3, 4)):
            S.dma("sp", lambda e, mi=mi, m=m: e.dma_start(
                out=ROWS[16 + 8 * mi:24 + 8 * mi, :], in_=b_ada_d[m * D:(m + 1) * D].rearrange("(j p) -> j p", p=128)), writes=[cw()])
        S.dma("sp", lambda e: e.dma_start(out=ROWS[48:56, :], in_=npre1_d.rearrange("(j p) -> j p", p=128)), writes=[cw()])
        S.dma("sp", lambda e: e.dma_start(out=ROWS[56:64, :], in_=npre2_d.rearrange("(j p) -> j p", p=128)), writes=[cw()])
        S.op("pe", lambda e: e.transpose(G[:, 2, 0:64], ROWS[:], IDF[0:64, 0:64]), reads=cr(), writes=[bG[2]])
        S.op("dve", lambda e: e.tensor_copy(COLS[:], G[:, 2, 0:64]), reads=[bG[2]], writes=[cw()])
        for h in range(4):
            S.dma("sp", lambda e, h=h: e.dma_start(out=DG[:, h, :], in_=dnorm_d.partition_broadcast(128)), writes=[cw()])
            S.dma("sp", lambda e, h=h: e.dma_start(out=GG[:, h, :], in_=gnorm_d.partition_broadcast(128)), writes=[cw()])
            S.dma("sp", lambda e, h=h: e.dma_start(out=LAMT[:, h, :], in_=lam_d[h].partition_broadcast(128)), writes=[cw()])
        for d_ in range(2):
            S.dma("sp", lambda e, d_=d_: e.dma_start(out=BG[:, d_, :], in_=bg_d[d_].partition_broadcast(128)), writes=[cw()])
            S.dma("pool", lambda e, d_=d_: e.dma_start(out=WG[32 * d_:32 * d_ + 16, :], in_=wg_d[d_]), writes=[cw()])
        stage(101)
        S.op("dve", lambda e: e.tensor_scalar_mul(DG[:], DG[:], float(1.0 - LAM_INIT)), reads=cr(), writes=[cw()])
        S.op("dve", lambda e: e.tensor_tensor(out=LAMT[:, 0, :], in0=LAMT[:, 0, :], in1=LAMT[:, 1, :], op=ALU.mult), reads=cr(), writes=[cw()])
        S.op("dve", lambda e: e.tensor_tensor(out=LAMT[:, 2, :], in0=LAMT[:, 2, :], in1=LAMT[:, 3, :], op=ALU.mult), reads=cr(), writes=[cw()])
        S.op("dve", lambda e: e.reduce_sum(out=LAMS[:, 0:1], in_=LAMT[:, 0, :], axis=AX.X), reads=cr(), writes=[cw()])
        S.op("dve", lambda e: e.reduce_sum(out=LAMS[:, 1:2], in_=LAMT[:, 2, :], axis=AX.X), reads=cr(), writes=[cw()])
        S.op("act", lambda e: e.activation(out=LAMS[:, 2:4], in_=LAMS[:, 0:2], func=AF.Exp), reads=cr(), writes=[cw()])
        S.op("dve", lambda e: e.memset(LAMS[:, 4:5], 1.0), reads=cr(), writes=[cw()])
        S.op("dve", lambda e: e.scalar_tensor_tensor(out=LAMS[:, 5:6], in0=LAMS[:, 3:4], scalar=float(-LAM_INIT),
                                                     in1=LAMS[:, 2:3], op0=ALU.add, op1=ALU.subtract), reads=cr(), writes=[cw()])

        stage(102)
        JOIN = sb("JOIN", [128, 2], F32)
        S.op("dve", lambda e: e.memset(JOIN[:], 0.0), reads=cr(), writes=[bC])
        def rstd_from_ss(ss_ap, out_ap, n, bufs):
            S.op("act", lambda e: e.activation(out=out_ap, in_=ss_ap, func=AF.Ln, scale=1.0 / n, bias=EPSC[:, 0:1]),
                 reads=bufs + [bC], writes=bufs)
            S.op("act", lambda e: e.activation(out=out_ap, in_=out_ap, func=AF.Exp, scale=-0.5), reads=bufs, writes=bufs)

        EPSC = sb("EPSC", [128, 2], F32)
        S.op("dve", lambda e: e.memset(EPSC[:, 0:1], EPS), writes=[bC])
        S.op("dve", lambda e: e.memset(EPSC[:, 1:2], 1.0), writes=[bC])

        stat_i = [0]

        def stat_cols(n):
            k = stat_i[0] % 8
            stat_i[0] += 1
            return SMALL[:, k * 8:k * 8 + n], smb("stat%d" % k)

        xp0 = [(rview(32768 + i * 4096, [128, D], F32), Buf("XP%d" % i)) for i in range(NT)]
        XNA = rview(65536, [128, 8, D], BF16)
        bXNA = [Buf("XNA%d" % i) for i in range(NT)]
        R_live = R_live + [b_ for _, b_ in xp0] + bXNA
        for i in range(NT):
            S.dma("sp", lambda e, i=i: e.dma_start(out=xp0[i][0], in_=x_d[i * 128:(i + 1) * 128, :]), writes=[xp0[i][1]])
        for i in range(NT):
            st_ap, st_b = stat_cols(2)
            S.op("act", lambda e, i=i, st_ap=st_ap: e.activation(out=XNA[:, i, :], in_=xp0[i][0], func=AF.Square, accum_out=st_ap[:, 0:1]),
                 reads=[xp0[i][1]], writes=[bXNA[i], st_b])
            rstd_from_ss(st_ap[:, 0:1], st_ap[:, 1:2], float(D), [st_b])
            S.op("dve", lambda e, i=i, st_ap=st_ap: e.tensor_scalar_mul(XNA[:, i, :], xp0[i][0], st_ap[:, 1:2]),
                 reads=[xp0[i][1], st_b], writes=[bXNA[i]])

        CT = COLS[:, 0:16].rearrange("p (t k) -> p k t", t=2)
        BADAC = COLS[:, 16:48].rearrange("p (m j) -> p m j", m=4)
        GPRE = COLS[:, 48:64].rearrange("p (n j) -> p n j", n=2)
        S.op("act", lambda e: e.activation(out=SC[:], in_=CT, func=AF.Silu), reads=[bC], writes=[bC])
        GROWX = XT[0:2, :, :]
        PMC = G[:, 0, 0:64].rearrange("p (a b c) -> p a b c", a=4, b=8)
        col_mods = {0: 0, 1: 1, 3: 2, 4: 3}
        bAB = [Buf("AB0"), Buf("AB1")]

        def adaln_block(m, half, cb=0, rb=1):
            slot, wv = ws_load([(0, 512, wcols(w_ada_d, m * D + half * 512, 512))])
            pmc = G[:, cb, 0:64].rearrange("p (a b c) -> p a b c", a=4, b=8)
            if m in col_mods:
                mi = col_mods[m]
                for j in range(4):
                    jj = half * 4 + j
                    for kc in range(8):
                        S.op("pe", lambda e, jj=jj, kc=kc, j=j: e.matmul(
                            pmc[:, mi, jj, :], wv[:, kc, j * 128:(j + 1) * 128], SC[:, kc, :],
                            start=(kc == 0), stop=(kc == 7)), reads=[bWS[slot], bC], writes=[bG[cb]])
                S.op("dve", lambda e: e.tensor_copy(MODC[:, mi, half * 4:half * 4 + 4, :], pmc[:, mi, half * 4:half * 4 + 4, :]),
                     reads=[bG[cb]], writes=[bAB[mi // 2]])
            else:
                gi = 0 if m == 2 else 1
                for kc in range(8):
                    S.op("pe", lambda e, kc=kc: e.matmul(
                        G[0:2, rb, :], SC[:, kc, :], wv[:, kc, :], start=(kc == 0), stop=(kc == 7)),
                        reads=[bWS[slot], bC], writes=[bG[rb]])
                S.op("dve", lambda e: e.tensor_copy(GROWX[:, gi, half * 512:(half + 1) * 512], G[0:2, rb, :]),
                     reads=[bG[rb]], writes=[bXT[0], bXT[1]])

        ada_tasks = [(m, half) for m in (3, 4, 2, 5) for half in range(2)]

        def ada_step(cb=0, rb=1):
            if ada_tasks:
                adaln_block(*ada_tasks.pop(0), cb=cb, rb=rb)


        def adaln_cols(n_):
            S.op("dve", lambda e: e.tensor_tensor(out=MODC[:, 2 * n_:2 * n_ + 2], in0=MODC[:, 2 * n_:2 * n_ + 2],
                                                  in1=BADAC[:, 2 * n_:2 * n_ + 2, :].unsqueeze(3).to_broadcast([128, 2, 8, 2]), op=ALU.add),
                 reads=[bAB[n_], bC], writes=[bAB[n_]])
            S.op("dve", lambda e: e.scalar_tensor_tensor(
                out=AB[:, 2 * n_, :, :], in0=MODC[:, 2 * n_ + 1, :, :], scalar=1.0,
                in1=GPRE[:, n_, :].unsqueeze(2).to_broadcast([128, 8, 2]), op0=ALU.add, op1=ALU.mult),
                reads=[bC, bAB[n_]], writes=[bAB[n_]])
            S.op("dve", lambda e: e.tensor_copy(AB[:, 2 * n_ + 1, :, :], MODC[:, 2 * n_, :, :]), reads=[bAB[n_]], writes=[bAB[n_]])

        for half in range(2):
            adaln_block(0, half)
        for half in range(2):
            adaln_block(1, half)
        adaln_cols(0)

        def adaln_finish(TB, bTB):
            while ada_tasks:
                ada_step()
            adaln_cols(1)
            gb_i = 0
            for ty in range(2):
                for gi in range(2):
                    for half in range(2):
                        bank = 2 + (gb_i % 2)
                        gb_i += 1
                        S.op("pe", lambda e, ty=ty, gi=gi, half=half, bank=bank: e.matmul(
                            G[:, bank, :], SEL[:, ty, :], GROWX[:, gi, half * 512:(half + 1) * 512], start=True, stop=True),
                            reads=[bC, bXT[0], bXT[1]], writes=[bG[bank]])
                        S.op("act", lambda e, ty=ty, gi=gi, half=half, bank=bank: e.copy(
                            GB[:, ty, gi, half * 512:(half + 1) * 512], G[:, bank, :]), reads=[bG[bank]], writes=[bGB])
            for gi, (m, np_d) in enumerate(((2, npost1_d), (5, npost2_d))):
                S.dma("sp", lambda e, m=m: e.dma_start(out=TB[0], in_=b_ada_d[m * D:(m + 1) * D].partition_broadcast(128)), writes=[bTB[0]])
                S.dma("sp", lambda e, np_d=np_d: e.dma_start(out=TB[1], in_=np_d.partition_broadcast(128)), writes=[bTB[1]])
                for ty in range(2):
                    S.op("dve", lambda e, ty=ty, gi=gi: e.tensor_tensor(out=GB[:, ty, gi, :], in0=GB[:, ty, gi, :], in1=TB[0], op=ALU.add),
                         reads=[bGB, bTB[0]], writes=[bGB])
                    S.op("dve", lambda e, ty=ty, gi=gi: e.tensor_tensor(out=GB[:, ty, gi, :], in0=GB[:, ty, gi, :], in1=TB[1], op=ALU.mult),
                         reads=[bGB, bTB[1]], writes=[bGB])

        bGB = Buf("GB")

        def norm_stats(src_ap, src_bufs, xn_slot):
            st_ap, st_b = stat_cols(2)
            xn = XN[:, xn_slot, :]
            S.op("act", lambda e: e.activation(out=xn, in_=src_ap, func=AF.Square, accum_out=st_ap[:, 0:1]),
                 reads=src_bufs, writes=[bXN[xn_slot], st_b])
            rstd_from_ss(st_ap[:, 0:1], st_ap[:, 1:2], float(D), [st_b])
            return st_ap, st_b

        def norm_xn_T(src_ap, src_bufs, st, tile_i, xn_slot):
            st_ap, st_b = st
            xn = XN[:, xn_slot, :]
            S.op("dve", lambda e: e.tensor_scalar_mul(xn, src_ap, st_ap[:, 1:2]), reads=src_bufs + [st_b], writes=[bXN[xn_slot]])
            pslot = tile_i % 2
            ptv = PTR[:, pslot, :].rearrange("p (k c) -> p k c", k=8)
            for kc in range(8):
                S.op("pe", lambda e, kc=kc: e.transpose(ptv[:, kc, :], xn[:, kc * 128:(kc + 1) * 128], IDB[:]),
                     reads=[bXN[xn_slot], bC], writes=[bPTR[pslot]])

        def norm_evac(tile_i, ty, which, eng="act"):
            pslot = tile_i % 2
            ptv = PTR[:, pslot, :].rearrange("p (k c) -> p k c", k=8)
            for kc in range(8):
                if eng == "act" or (eng == "mix" and kc < 4):
                    S.op("act", lambda e, kc=kc: e.activation(
                        out=HT[:, kc, tile_i * 128:(tile_i + 1) * 128], in_=ptv[:, kc, :], func=AF.Identity,
                        bias=AB[:, 2 * which + 1, kc, ty:ty + 1], scale=AB[:, 2 * which, kc, ty:ty + 1]),
                        reads=[bPTR[pslot], bAB[which]], writes=[bHT[tile_i]])
                else:
                    S.op("dve", lambda e, kc=kc: e.tensor_scalar(
                        HT[:, kc, tile_i * 128:(tile_i + 1) * 128], ptv[:, kc, :],
                        AB[:, 2 * which, kc, ty:ty + 1], AB[:, 2 * which + 1, kc, ty:ty + 1], ALU.mult, ALU.add),
                        reads=[bPTR[pslot], bAB[which]], writes=[bHT[tile_i]])

        pp_i = [0]

        def proj_fm(wv, wslot, col0, M, half, banks=(0, 1, 2, 3)):
            bank = banks[pp_i[0] % len(banks)]
            pp_i[0] += 1
            for kc in range(8):
                S.op("pe", lambda e, kc=kc: e.matmul(G[0:M, bank, :], wv[:, kc, col0:col0 + M],
                                                     HT[:, kc, half * 512:(half + 1) * 512], start=(kc == 0), stop=(kc == 7)),
                     reads=[bWS[wslot]] + bHT[half * 4:half * 4 + 4], writes=[bG[bank]])
            return bank

        def proj_tm(wv, wslot, col0, N, tile_i, banks=(0, 1, 2, 3)):
            bank = banks[pp_i[0] % len(banks)]
            pp_i[0] += 1
            for kc in range(8):
                S.op("pe", lambda e, kc=kc: e.matmul(G[:, bank, 0:N], HT[:, kc, tile_i * 128:(tile_i + 1) * 128],
                                                     wv[:, kc, col0:col0 + N], start=(kc == 0), stop=(kc == 7)),
                     reads=[bWS[wslot], bHT[tile_i]], writes=[bG[bank]])
            return bank

        def run_sg(sg):
            ty = sg
            row0 = sg * T
            is_sample = (sg == 0)

            nonlocal R_live
            xp = []
            if sg == 0:
                for i in range(NT + 1):
                    if i < NT:
                        pslot = i % 2
                        ptv = PTR[:, pslot, :].rearrange("p (k c) -> p k c", k=8)
                        for kc in range(8):
                            S.op("pe", lambda e, kc=kc, i=i, ptv=ptv: e.transpose(ptv[:, kc, :], XNA[:, i, kc * 128:(kc + 1) * 128], IDB[:]),
                                 reads=[bXNA[i], bC], writes=[bPTR[pslot]])
                    if i >= 1:
                        norm_evac(i - 1, ty, 0, eng="dve")
            else:
                otf = OT[:].rearrange("p k t -> p (k t)").bitcast(F32).rearrange("p (a b) -> p a b", a=4)
                xp_ot = [Buf("XPOT%d" % i).inherit(bOT) for i in range(4)]
                for i in range(4):
                    xp.append((otf[:, i, :], xp_ot[i]))
                for i in range(2):
                    xp.append((rview(88064 + i * 4096, [128, D], F32), Buf("XPR%d" % i).inherit(R_live)))
                R_live = R_live + [xp[4][1], xp[5][1]]
                for i in range(2):
                    xp.append((XT[:, i, :], bXT[i]))
                for i in range(NT):
                    S.dma("sp", lambda e, i=i: e.dma_start(out=xp[i][0], in_=x_d[row0 + i * 128: row0 + (i + 1) * 128, :]),
                          writes=[xp[i][1]])
                for i in range(NT + 1):
                    if i < NT:
                        st_ = norm_stats(xp[i][0], [xp[i][1]], i % 2)
                        norm_xn_T(xp[i][0], [xp[i][1]], st_, i, i % 2)
                    if i >= 1:
                        norm_evac(i - 1, ty, 0, eng="dve")
            if sg == 1:
                for b_ in bOT:
                    b_.inherit(xp_ot)
            stage(2 + 10 * sg)
            pa = new_phase(["QAT0", "QAT1", "QAT2", "QAT3", "KAT0", "KAT1", "KAT2", "KAT3", "VA", "PT0", "PT1", "PT2",
                            "OA00", "OA01", "OA10", "OA11", "ACCS", "OB16b", "CK32", "T1", "T2", "XB16", "NK0", "NK1", "SQ", "ON", "OB16"])
            QAT = rview(0, [128, 4, T], BF16)
            KAT = [rview(8192, [128, 4, 1280], BF16), rview(59392, [128, 4, 1280], BF16)]
            VA = rview(18432, [128, 10, 4, 129], BF16)
            PT = rview(28800, [128, 3, 512], BF16)
            OA4 = rview(32768, [128, 2, 2, 512], F32)
            CK32 = rview(40960, [128, 2, 512], F32)
            T1 = rview(45056, [128, 512], F32)
            T2 = rview(47104, [128, 512], F32)
            XB16 = rview(49152, [128, 512], BF16)
            NK = rview(50176, [128, 2, 512], F32)
            SQ = rview(54272, [128, 512], F32)
            ON = rview(56320, [128, 512], F32)
            OB16 = rview(58368, [128, 512], BF16)
            bQAT = [pa["QAT%d" % h] for h in range(4)]
            bKAT = [pa["KAT%d" % h] for h in range(4)]
            bPT = [pa["PT%d" % h] for h in range(3)]
            bOAq = [[pa["OA%d%d" % (a, b)] for b in range(2)] for a in range(2)]
            bNK = [pa["NK0"], pa["NK1"]]
            KO = 256 if is_sample else 0
            VO = 2 if is_sample else 0

            S.op("dve", lambda e: e.memset(VA[:, :, :, 128:129], 1.0), writes=[pa["VA"]])
            S.op("dve", lambda e: e.memset(KAT[0][64:128, :, :], 0.0), writes=bKAT)
            S.op("dve", lambda e: e.memset(KAT[1][0:64, :, :], 0.0), writes=bKAT)
            if is_sample:
                for kc in range(2):
                    S.dma("sp", lambda e, kc=kc: e.dma_start(
                        out=CK32[:, kc, :].rearrange("p (h d) -> p h d", h=4),
                        in_=cache_k_d[:, kc * 128:(kc + 1) * 128, :].rearrange("h p d -> p h d")), writes=[pa["CK32"]])
                    S.dma("pool", lambda e, kc=kc: e.dma_start(
                        out=VA[:, kc, :, 0:128],
                        in_=cache_v_d[:, kc * 128:(kc + 1) * 128, :].rearrange("h p d -> p h d")), writes=[pa["VA"]])
                stage(21)
                for kc in range(2):
                    bank = 4 + kc
                    for h in range(4):
                        S.op("pe", lambda e, kc=kc, h=h, bank=bank: e.transpose(
                            G[:, bank, h * 128:(h + 1) * 128], CK32[:, kc, h * 128:(h + 1) * 128], IDF[:]),
                            reads=[pa["CK32"], bC], writes=[bG[bank]])
                    for s_ in range(2):
                        S.op("act", lambda e, kc=kc, bank=bank, s_=s_: e.copy(
                            KAT[s_][s_ * 64:(s_ + 1) * 64, :, kc * 128:(kc + 1) * 128],
                            G[s_ * 64:(s_ + 1) * 64, bank, :].rearrange("p (h k) -> p h k", h=4)),
                            reads=[bG[bank]], writes=bKAT)

            def fm_evac(bank, dst_ap, dst_bufs, half):
                if not is_sample or DBG.get('norope'):
                    for (p0, p1, dap) in dst_ap:
                        S.op("act", lambda e, p0=p0, p1=p1, dap=dap: e.copy(dap, G[p0:p1, bank, :]), reads=[bG[bank]], writes=dst_bufs)
                    return
                S.op("act", lambda e: e.copy(XB16, G[:, bank, :]), reads=[bG[bank]], writes=[pa["XB16"]])
                if not DBG.get('nomm'):
                    S.op("pe", lambda e: e.matmul(G[:, 4, :], PERM[:], XB16, start=True, stop=True),
                         reads=[pa["XB16"], bC], writes=[bG[4]])
                if DBG.get('nodve'):
                    return
                S.op("dve", lambda e: e.tensor_tensor(out=T1, in0=G[:, bank, :], in1=ROPE[:, 0, half * 512:(half + 1) * 512], op=ALU.mult),
                     reads=[bG[bank], bC], writes=[pa["T1"]])
                if DBG.get('nodve2'):
                    return
                b4 = bank if DBG.get('nomm') else 4
                S.op("dve", lambda e: e.tensor_tensor(out=T2, in0=G[:, b4, :], in1=ROPE[:, 1, half * 512:(half + 1) * 512], op=ALU.mult),
                     reads=[bG[b4], bC], writes=[pa["T2"]])
                if DBG.get('nodve3'):
                    return
                for (p0, p1, dap) in dst_ap:
                    S.op("dve", lambda e, p0=p0, p1=p1, dap=dap: e.tensor_tensor(out=dap, in0=T1[p0:p1, :], in1=T2[p0:p1, :], op=ALU.add),
                         reads=[pa["T1"], pa["T2"]], writes=dst_bufs)

            stage(22)
            slot, wv = ws_load([(0, 512, wcols(w_in_d, 0, 512))])
            prev = None
            for h in range(4):
                for half in range(2):
                    bank = proj_fm(wv, slot, h * 128, 128, half)
                    if prev is not None:
                        fm_evac(*prev)
                    prev = (bank, [(0, 128, QAT[:, h, half * 512:(half + 1) * 512])], [bQAT[h]], half)
            fm_evac(*prev)
            stage(23)
            slot, wv = ws_load([(0, 512, wcols(w_in_d, 512, 512))])
            prev = None
            for h in range(4):
                for half in range(2):
                    bank = proj_fm(wv, slot, h * 128, 128, half)
                    if prev is not None:
                        fm_evac(*prev)
                    prev = (bank, [(0, 64, KAT[0][0:64, h, KO + half * 512:KO + (half + 1) * 512]),
                                   (64, 128, KAT[1][64:128, h, KO + half * 512:KO + (half + 1) * 512])], [bKAT[h]], half)
            fm_evac(*prev)
            if not is_sample:
                for i in range(NT):
                    bank = proj_tm(wv, slot, 0, 512, i)
                    ns = i % 2
                    S.op("act", lambda e, bank=bank, ns=ns: e.copy(NK[:, ns, :], G[:, bank, :]), reads=[bG[bank]], writes=[bNK[ns]])
                    S.dma("sp", lambda e, i=i, ns=ns: e.dma_start(
                        out=newk_d[i // 2, :, (i % 2) * 128:(i % 2 + 1) * 128, :].rearrange("h t d -> t h d"),
                        in_=NK[:, ns, :].rearrange("p (h d) -> p h d", h=4)), reads=[bNK[ns]])
            stage(24)
            slot, wv = ws_load([(0, 512, wcols(w_in_d, 1024, 512))])
            for i in range(NT):
                bank = proj_tm(wv, slot, 0, 512, i)
                S.op("act", lambda e, bank=bank, i=i: e.copy(VA[:, VO + i, :, 0:128], G[:, bank, :].rearrange("p (h d) -> p h d", h=4)),
                     reads=[bG[bank]], writes=[pa["VA"]])
                if not is_sample:
                    ns = i % 2
                    S.op("dve", lambda e, bank=bank, ns=ns: e.tensor_copy(NK[:, ns, :], G[:, bank, :]), reads=[bG[bank]], writes=[bNK[ns]])
                    S.dma("sp", lambda e, i=i, ns=ns: e.dma_start(
                        out=newv_d[i // 2, :, (i % 2) * 128:(i % 2 + 1) * 128, :].rearrange("h t d -> t h d"),
                        in_=NK[:, ns, :].rearrange("p (h d) -> p h d", h=4)), reads=[bNK[ns]])

            stage(3 + 10 * sg)
            if is_sample:
                qblocks = [(qb_ * 256, [(kc * 128, kc) for kc in range(10)]) for qb_ in range(4)]
            else:
                qblocks = [(s_ * 256, [(s_ * 256 + kc * 128, 2 * s_ + kc) for kc in range(2)]) for s_ in range(4)]
            ACCS = rview(69632, [128, 4, 129], F32)
            its = []
            for qbi, (q0, keys) in enumerate(qblocks):
                for h in range(4):
                    for ki, (kcol, vch) in enumerate(keys):
                        its.append((qbi, q0, h, ki, len(keys), kcol, vch))

            head_no = {}
            for (qbi_, q0_, h_, ki_, nk_, kcol_, vch_) in its:
                if (qbi_, h_) not in head_no:
                    head_no[(qbi_, h_)] = len(head_no)

            def att_scores(n):
                qbi, q0, h, ki, nk, kcol, vch = its[n]
                sbank = n % 2
                psc = G[:, sbank, :].rearrange("p (s q) -> p s q", s=2)
                for s_ in range(2):
                    S.op("pe", lambda e, s_=s_: e.matmul(
                        psc[:, s_, :], KAT[s_][:, h, kcol:kcol + 128], QAT[:, h, q0:q0 + 256], start=True, stop=True),
                        reads=[bKAT[h], bQAT[h]], writes=[bG[sbank]])

            def att_exp_pv(n):
                qbi, q0, h, ki, nk, kcol, vch = its[n]
                sbank = n % 2
                pts = n % 3
                oas = qbi % 2
                S.op("act", lambda e: e.activation(out=PT[:, pts, :], in_=G[:, sbank, :], func=AF.Exp, scale=0.125),
                     reads=[bG[sbank]], writes=[bPT[pts]])
                ptv = PT[:, pts, :].rearrange("p (s q) -> p s q", s=2)
                hn = head_no[(qbi, h)]
                ab = 2 + 2 * (hn % 2)
                for qt in range(2):
                    for s_ in range(2):
                        bank = ab + qt
                        S.op("pe", lambda e, qt=qt, s_=s_, bank=bank: e.matmul(
                            G[:, bank, s_ * 129:(s_ + 1) * 129], ptv[:, s_, qt * 128:(qt + 1) * 128], VA[:, vch, h, :],
                            start=(ki == 0 and s_ == 0), stop=(ki == nk - 1), skip_group_check=True),
                            reads=[bPT[pts], pa["VA"]], writes=[bG[bank]])
                if ki != nk - 1:
                    return
                accv = G[:, ab:ab + 2, 0:258].rearrange("p q (s c) -> p q s c", s=2)
                acc_b = [bG[ab], bG[ab + 1]]
                st_ap, st_b = stat_cols(8)
                S.op("dve", lambda e: e.reciprocal(st_ap[:, 0:4].rearrange("p (q s) -> p q s", q=2), accv[:, :, :, 128]),
                     reads=acc_b, writes=[st_b])
                S.op("dve", lambda e: e.tensor_tensor(
                    out=st_ap[:, 4:8].rearrange("p (q s) -> p q s", q=2), in0=st_ap[:, 0:4].rearrange("p (q s) -> p q s", q=2),
                    in1=LAMS[:, 4:6].unsqueeze(1).to_broadcast([128, 2, 2]), op=ALU.mult), reads=[st_b, bC], writes=[st_b])
                for qt in range(2):
                    S.op("dve", lambda e, qt=qt: e.tensor_scalar_mul(T1[:, qt * 128:(qt + 1) * 128], accv[:, qt, 1, 0:128],
                                                                   st_ap[:, 4 + 2 * qt + 1:4 + 2 * qt + 2]),
                         reads=[bG[ab + qt], st_b], writes=[pa["T1"]])
                    S.op("dve", lambda e, qt=qt: e.scalar_tensor_tensor(
                        out=OA4[:, oas, qt, h * 128:(h + 1) * 128], in0=accv[:, qt, 0, 0:128],
                        scalar=st_ap[:, 4 + 2 * qt:4 + 2 * qt + 1], in1=T1[:, qt * 128:(qt + 1) * 128], op0=ALU.mult, op1=ALU.add),
                        reads=[bG[ab + qt], st_b, pa["T1"]], writes=[bOAq[oas][qt]])
                if h != 3:
                    return
                for qt in range(2):
                    tile_i = (q0 // 128) + qt
                    oav = OA4[:, oas, qt, :]
                    st_ap2, st_b2 = SMALL[:, 64 + 8 * (ep_i[0] % 2):72 + 8 * (ep_i[0] % 2)], smb("ep%d" % (ep_i[0] % 2))
                    ep_i[0] += 1
                    ob = qt
                    pslot = tile_i % 2
                    ptv2 = PTR[:, pslot, 0:512].rearrange("p (h c) -> p h c", h=4)
                    bo = bOAq[oas][qt]

                    def t_sq(oav=oav, bo=bo):
                        S.op("act", lambda e: e.activation(out=SQ, in_=oav, func=AF.Square), reads=[bo], writes=[pa["SQ"]])

                    def t_red(st_ap2=st_ap2, st_b2=st_b2):
                        S.op("dve", lambda e: e.reduce_sum(out=st_ap2[:, 0:4], in_=SQ.rearrange("p (h d) -> p h d", h=4), axis=AX.X),
                             reads=[pa["SQ"]], writes=[st_b2])

                    def t_rstd(st_ap2=st_ap2, st_b2=st_b2):
                        rstd_from_ss(st_ap2[:, 0:4], st_ap2[:, 4:8], 128.0, [st_b2])

                    def t_mul(oav=oav, bo=bo, st_ap2=st_ap2, st_b2=st_b2, ob=ob):
                        S.op("dve", lambda e: e.tensor_tensor(
                            out=ON.rearrange("p (h d) -> p h d", h=4), in0=oav.rearrange("p (h d) -> p h d", h=4),
                            in1=st_ap2[:, 4:8].unsqueeze(2).to_broadcast([128, 4, 128]), op=ALU.mult),
                            reads=[bo, st_b2], writes=[pa["ON"]])
                        S.op("dve", lambda e: e.tensor_tensor(out=OB16r[ob], in0=ON, in1=DG[:].rearrange("p h d -> p (h d)"), op=ALU.mult),
                             reads=[pa["ON"], bC], writes=[bOB16r[ob]])

                    def t_tr(ob=ob, ptv2=ptv2, pslot=pslot):
                        for hh_ in range(4):
                            S.op("pe", lambda e, hh_=hh_: e.transpose(ptv2[:, hh_, :], OB16r[ob][:, hh_ * 128:(hh_ + 1) * 128], IDB[:]),
                                 reads=[bOB16r[ob], bC], writes=[bPTR[pslot]])

                    def t_cp(ptv2=ptv2, pslot=pslot, tile_i=tile_i):
                        S.op("act", lambda e: e.copy(OT[:, 0:4, tile_i * 128:(tile_i + 1) * 128], ptv2),
                             reads=[bPTR[pslot]], writes=[bOT[tile_i]])

                    pending.extend([t_sq, t_red, t_rstd, t_mul, t_tr, t_cp])

            ep_i = [0]
            pending = []
            OB16r = [OB16, rview(71936, [128, 512], BF16)]
            bOB16r = [pa["OB16"], pa["OB16b"]]
            for n in range(len(its) + 1):
                if n < len(its):
                    att_scores(n)
                if n >= 1:
                    att_exp_pv(n - 1)
                for _ in range(1 if is_sample else 2):
                    if pending:
                        pending.pop(0)()
                if sg == 0 and n % 18 == 9:
                    ada_step(cb=6, rb=7)
            while pending:
                pending.pop(0)()

            stage(4 + 10 * sg)
            names = (["GT%d" % i for i in range(NT)] + ["QT0", "QT1", "KT0", "KT1", "KH0", "KH1", "RBG", "S160", "S161", "SQ", "ON", "GLT",
                     "EB0", "EB1"] + ["VB%d" % i for i in range(NT)]
                     + ["ETP0", "ETP1", "ETM0", "ETM1", "EH0", "EH1", "XG0", "XG1", "RBT0", "RBT1", "OB0", "OB1"]
                     + ["ATT%d%d" % (r_, d_) for r_ in range(2) for d_ in range(2)]
                     + ["S32_%d%d%d%d" % (d_, r_, pr, hh) for d_ in range(2) for r_ in range(2) for pr in range(2) for hh in range(2)])
            pg = new_phase(names)
            GT = rview(0, [128, 8, 512], F32)
            QTt = [rview(16384 + d_ * 4096, [128, 2, T], BF16) for d_ in range(2)]
            KTt = [rview(24576 + d_ * 4096, [128, 2, T], BF16) for d_ in range(2)]
            KH = [rview(32768 + d_ * 4096, [128, 8, 256], BF16) for d_ in range(2)]
            VB = rview(40960, [128, 8, 512], BF16)
            RBG = rview(49152, [128, 8, 512], BF16)
            S16 = [rview(57344 + d_ * 8192, [128, 2, 16, 128], BF16) for d_ in range(2)]
            RBT = [rview(57344 + r_ * 2048, [128, 512], F32) for r_ in range(2)]
            ETP = [rview(73728 + r_ * 1024, [128, 2, 128], F32) for r_ in range(2)]
            ETM = [rview(75776 + r_ * 1024, [128, 2, 128], F32) for r_ in range(2)]
            EH = [rview(77824 + r_ * 1024, [128, 256], F32) for r_ in range(2)]
            XG = [rview(79872 + r_ * 2048, [128, 512], F32) for r_ in range(2)]
            S32 = [[rview(79872 + (d_ * 2 + r_) * 1024, [128, 2, 128], F32) for r_ in range(2)] for d_ in range(2)]
            ATT = [[rview(83968 + (r_ * 2 + d_) * 1024, [128, 4, 128], BF16) for d_ in range(2)] for r_ in range(2)]
            SQ2 = rview(88064, [128, 512], F32)
            ON2 = rview(90112, [128, 512], F32)
            OB2 = [rview(92160 + r_ * 1024, [128, 512], BF16) for r_ in range(2)]
            GLT = rview(94208, [64, T], BF16)
            EB = [rview(96256 + d_ * 128, [128, 2, 16], F32) for d_ in range(2)]
            bGT = [pg["GT%d" % i] for i in range(NT)]
            bVB = [pg["VB%d" % i] for i in range(NT)]
            bQT = [pg["QT0"], pg["QT1"]]
            bKT = [pg["KT0"], pg["KT1"]]
            bKH = [pg["KH0"], pg["KH1"]]
            bS16 = [pg["S160"], pg["S161"]]
            bEB = [pg["EB0"], pg["EB1"]]
            bETP = [pg["ETP0"], pg["ETP1"]]
            bETM = [pg["ETM0"], pg["ETM1"]]
            bEH = [pg["EH0"], pg["EH1"]]
            bXG = [pg["XG0"], pg["XG1"]]
            bRBT = [pg["RBT0"], pg["RBT1"]]
            bOB = [pg["OB0"], pg["OB1"]]
            bATT = [[pg["ATT%d%d" % (r_, d_)] for d_ in range(2)] for r_ in range(2)]
            bS32 = [[[[pg["S32_%d%d%d%d" % (d_, r_, pr, hh)] for hh in range(2)] for pr in range(2)] for r_ in range(2)] for d_ in range(2)]
            for b_ in bQT + bKT + bKH + bS16 + bEB + [pg["RBG"]]:
                b_.same_ok = True

            slot, wv = ws_load([(0, 16, wcols(w_in_d, 3072, 16)), (32, 16, wcols(w_in_d, 3088, 16))])
            for half in range(2):
                bank = proj_fm(wv, slot, 0, 64, half)
                S.op("act", lambda e, bank=bank, half=half: e.copy(GLT[:, half * 512:(half + 1) * 512], G[0:64, bank, :]),
                     reads=[bG[bank]], writes=[pg["GLT"]])
            for i in range(NT):
                r_ = i % 2
                gb0 = 4 if r_ == 0 else 2
                for d_ in range(2):
                    S.op("pe", lambda e, i=i, d_=d_, gb0=gb0: e.matmul(
                        G[:, gb0 + d_, 0:256], GLT[32 * d_:32 * d_ + 16, i * 128:(i + 1) * 128],
                        WG[32 * d_:32 * d_ + 16, :], start=True, stop=True), reads=[pg["GLT"], bC], writes=[bG[gb0 + d_]])
                S.op("dve", lambda e, r_=r_, gb0=gb0: e.tensor_tensor(out=XG[r_].rearrange("p (a b) -> p a b", a=2),
                                                                      in0=G[:, gb0:gb0 + 2, 0:256], in1=BG[:], op=ALU.add),
                     reads=[bG[gb0], bG[gb0 + 1], bC], writes=[bXG[r_]])
                S.op("act", lambda e, r_=r_: e.activation(out=XG[r_], in_=XG[r_], func=AF.Exp, scale=-1.0), reads=[bXG[r_]], writes=[bXG[r_]])
                S.op("act", lambda e, i=i, r_=r_: e.activation(out=GT[:, i, :], in_=XG[r_], func=AF.Ln, bias=EPSC[:, 1:2]),
                     reads=[bXG[r_], bC], writes=[bGT[i]])
            slot, wv = ws_load([(0, 512, wcols(w_in_d, 2048, 512))])
            for i in range(NT):
                bank = proj_tm(wv, slot, 0, 512, i)
                S.op("act", lambda e, bank=bank, i=i: e.copy(VB[:, i, :], G[:, bank, :]), reads=[bG[bank]], writes=[bVB[i]])
            slot, wv = ws_load([(0, 512, wcols(w_in_d, 2560, 512))])
            for i in range(NT):
                bank = proj_tm(wv, slot, 0, 512, i)
                r_ = i % 2
                S.op("act", lambda e, bank=bank, r_=r_: e.activation(out=RBT[r_], in_=G[:, bank, :], func=AF.Silu), reads=[bG[bank]], writes=[bRBT[r_]])
                S.op("dve", lambda e, i=i, r_=r_: e.tensor_tensor(out=RBG[:, i, :], in0=RBT[r_], in1=GG[:].rearrange("p h d -> p (h d)"), op=ALU.mult),
                     reads=[bRBT[r_], bC], writes=[pg["RBG"]])
            slot, wv = ws_load([(0, 512, wcols(w_in_d, 1536, 512))])
            pi = 0
            for half in range(2):
                for c_ in range(4):
                    for kc in range(8):
                        S.op("pe", lambda e, c_=c_, kc=kc, half=half, wv=wv: e.matmul(
                            G[:, c_, :], wv[:, kc, c_ * 128:(c_ + 1) * 128], HT[:, kc, half * 512:(half + 1) * 512],
                            start=(kc == 0), stop=(kc == 7)), reads=[bWS[slot]] + bHT[half * 4:half * 4 + 4], writes=[bG[c_]])
                for ti in range(4):
                    i = half * 4 + ti
                    kb_bank = 4 + (i % 2)
                    for kc in range(8):
                        S.op("pe", lambda e, kc=kc, i=i, wv=wv, kb_bank=kb_bank: e.matmul(
                            G[:, kb_bank, 0:256], HT[:, kc, i * 128:(i + 1) * 128], wv[:, kc, 256:512], start=(kc == 0), stop=(kc == 7)),
                            reads=[bWS[slot], bHT[i]], writes=[bG[kb_bank]])
                    for d_ in range(2):
                        r_ = pi % 2
                        cb = 6 + r_
                        pi += 1
                        for pr in range(2):
                            S.op("pe", lambda e, i=i, d_=d_, pr=pr, cb=cb: e.matmul(
                                G[:, cb, pr * 128:(pr + 1) * 128], GT[:, i, d_ * 256 + pr * 128:d_ * 256 + (pr + 1) * 128],
                                MASKS[:, 2 + d_, :], start=True, stop=True), reads=[bGT[i], bC], writes=[bG[cb]])
                        S.op("pe", lambda e, i=i, d_=d_, cb=cb: e.matmul(
                            G[:, cb, 256:512], MASKS[:, 4 + d_, :], GT[:, i, d_ * 256:(d_ + 1) * 256], start=True, stop=True),
                            reads=[bGT[i], bC], writes=[bG[cb]])
                        S.op("act", lambda e, r_=r_, cb=cb: e.activation(out=ETP[r_].rearrange("p a b -> p (a b)"), in_=G[:, cb, 0:256], func=AF.Exp),
                             reads=[bG[cb]], writes=[bETP[r_]])
                        S.op("act", lambda e, r_=r_, cb=cb: e.activation(out=ETM[r_].rearrange("p a b -> p (a b)"), in_=G[:, cb, 0:256], func=AF.Exp, scale=-1.0),
                             reads=[bG[cb]], writes=[bETM[r_]])
                        S.op("act", lambda e, r_=r_, cb=cb: e.activation(out=EH[r_], in_=G[:, cb, 256:512], func=AF.Exp), reads=[bG[cb]], writes=[bEH[r_]])
                        S.op("dve", lambda e, d_=d_, i=i, ti=ti, r_=r_: e.scalar_tensor_tensor(
                            out=QTt[d_][:, :, i * 128:(i + 1) * 128], in0=G[:, 0:2, ti * 128:(ti + 1) * 128], scalar=0.125,
                            in1=ETP[r_], op0=ALU.mult, op1=ALU.mult), reads=[bG[0], bG[1], bETP[r_]], writes=[bQT[d_]])
                        S.op("dve", lambda e, d_=d_, i=i, ti=ti, r_=r_: e.tensor_tensor(
                            out=KTt[d_][:, :, i * 128:(i + 1) * 128], in0=G[:, 2:4, ti * 128:(ti + 1) * 128], in1=ETM[r_], op=ALU.mult),
                            reads=[bG[2], bG[3], bETM[r_]], writes=[bKT[d_]])
                        S.op("dve", lambda e, d_=d_, i=i, r_=r_, kb_bank=kb_bank: e.tensor_tensor(
                            out=KH[d_][:, i, :], in0=G[:, kb_bank, 0:256], in1=EH[r_], op=ALU.mult),
                            reads=[bG[kb_bank], bEH[r_]], writes=[bKH[d_]])
                        col = 63 if d_ == 0 else 0
                        S.op("dve", lambda e, d_=d_, i=i, col=col, r_=r_: e.tensor_copy(
                            EB[d_][:, :, 2 * i:2 * i + 2], ETP[r_].rearrange("p a (c t) -> p a c t", c=2)[:, :, :, col]),
                            reads=[bETP[r_]], writes=[bEB[d_]])
            stage(5 + 10 * sg)
            if sg == 0:
                htf = HT[:].rearrange("p k t -> p (k t)").bitcast(F32).rearrange("p (a b) -> p a b", a=4)
                bTB = [Buf("TB0").inherit(bHT), Buf("TB1").inherit(bHT)]
                adaln_finish([htf[:, 0, :], htf[:, 1, :]], bTB)
                for b_ in bHT:
                    b_.inherit(bTB)
            for d_ in range(2):
                bS16[d_].inherit(bRBT)
                for r_ in range(2):
                    for pr in range(2):
                        for hh in range(2):
                            bS32[d_][r_][pr][hh].inherit(bXG)
            nseq = 1 if is_sample else 4
            cps = 16 // nseq
            kv_i = 0
            for k_ in range(16):
                for d_ in range(2):
                    c_ = k_ if d_ == 0 else 15 - k_
                    cur, nxt = k_ % 2, (k_ + 1) % 2
                    seq = c_ // cps
                    first = (c_ % cps == 0) if d_ == 0 else (c_ % cps == cps - 1)
                    last = (c_ % cps == cps - 1) if d_ == 0 else (c_ % cps == 0)
                    cur_b = [bS32[d_][cur][pr][0] for pr in range(2)]
                    nxt_b = [bS32[d_][nxt][pr][0] for pr in range(2)]
                    if first:
                        if is_sample:
                            S.dma("sp", lambda e, d_=d_, cur=cur: e.dma_start(
                                out=S32[d_][cur], in_=state_d[d_].rearrange("(pr hh) k v -> (hh k) pr v", hh=2)), writes=cur_b)
                        else:
                            S.op("dve", lambda e, d_=d_, cur=cur: e.memset(S32[d_][cur], 0.0), writes=cur_b)
                    S.op("act", lambda e, d_=d_, c_=c_, cur=cur: e.copy(S16[d_][:, :, c_, :], S32[d_][cur]), reads=cur_b, writes=[bS16[d_]])
                    i, ch = c_ // 2, c_ % 2
                    bank = kv_i % 4
                    kv_i += 1
                    for pr in range(2):
                        for hh in range(2):
                            h = 2 * pr + hh
                            S.op("pe", lambda e, d_=d_, i=i, ch=ch, pr=pr, hh=hh, h=h, bank=bank: e.matmul(
                                G[hh * 64:(hh + 1) * 64, bank, pr * 128:(pr + 1) * 128],
                                KH[d_][ch * 64:(ch + 1) * 64, i, h * 64:(h + 1) * 64],
                                VB[ch * 64:(ch + 1) * 64, i, h * 128:(h + 1) * 128], start=True, stop=True),
                                reads=[bKH[d_], bVB[i]], writes=[bG[bank]])
                    for pr in range(2):
                        S.op("dve", lambda e, d_=d_, pr=pr, c_=c_, bank=bank, cur=cur, nxt=nxt: e.scalar_tensor_tensor(
                            out=S32[d_][nxt][:, pr, :], in0=S32[d_][cur][:, pr, :], scalar=EB[d_][:, pr, c_:c_ + 1],
                            in1=G[:, bank, pr * 128:(pr + 1) * 128], op0=ALU.mult, op1=ALU.add),
                            reads=[bS32[d_][cur][pr][0], bEB[d_], bG[bank]], writes=[bS32[d_][nxt][pr][0]])
                    if last and not is_sample:
                        S.dma("sp", lambda e, d_=d_, seq=seq, nxt=nxt: e.dma_start(
                            out=news_d[d_][seq].rearrange("(pr hh) k v -> (hh k) pr v", hh=2), in_=S32[d_][nxt]), reads=nxt_b)
            def gla_AT(i):
                for d_ in range(2):
                    for h in (0, 2, 1, 3):
                        pr, hh = h // 2, h % 2
                        S.op("pe", lambda e, d_=d_, pr=pr, hh=hh: e.matmul(
                            G[:, 2 + hh, (d_ * 2 + pr) * 128:(d_ * 2 + pr + 1) * 128],
                            KTt[d_][hh * 64:(hh + 1) * 64, pr, i * 128:(i + 1) * 128],
                            QTt[d_][hh * 64:(hh + 1) * 64, pr, i * 128:(i + 1) * 128], start=True, stop=True),
                            reads=[bKT[d_], bQT[d_]], writes=[bG[2 + hh]])
                r_ = i % 2
                for d_ in range(2):
                    for hh in range(2):
                        S.op("dve", lambda e, d_=d_, hh=hh: e.tensor_tensor(
                            out=ATT[r_][d_].rearrange("p (pr hh) t -> p pr hh t", hh=2)[:, :, hh, :],
                            in0=G[:, 2 + hh, d_ * 256:(d_ + 1) * 256].rearrange("p (pr t) -> p pr t", pr=2),
                            in1=MASKS[:, d_, :].unsqueeze(1).to_broadcast([128, 2, 128]), op=ALU.mult),
                            reads=[bG[2 + hh], bC], writes=[bATT[r_][d_]])

            def gla_O(i):
                r_ = i % 2
                ob = 4 + r_
                for h in range(4):
                    pr, hh = h // 2, h % 2
                    for ch in range(2):
                        c_ = 2 * i + ch
                        outp = G[ch * 64:(ch + 1) * 64, ob, h * 128:(h + 1) * 128]
                        tcols = slice(i * 128 + ch * 64, i * 128 + (ch + 1) * 64)
                        for d_ in range(2):
                            S.op("pe", lambda e, d_=d_, h=h, ch=ch, outp=outp: e.matmul(
                                outp, ATT[r_][d_][:, h, ch * 64:(ch + 1) * 64], VB[:, i, h * 128:(h + 1) * 128],
                                start=(d_ == 0), stop=False), reads=[bATT[r_][d_], bVB[i]], writes=[bG[ob]])
                        for d_ in range(2):
                            S.op("pe", lambda e, d_=d_, pr=pr, hh=hh, c_=c_, outp=outp, tcols=tcols: e.matmul(
                                outp, QTt[d_][hh * 64:(hh + 1) * 64, pr, tcols], S16[d_][hh * 64:(hh + 1) * 64, pr, c_, :],
                                start=False, stop=(d_ == 1)), reads=[bQT[d_], bS16[d_]], writes=[bG[ob]])

            def gla_E1(i):
                r_ = i % 2
                ob = 4 + r_
                st_ap, st_b = stat_cols(8)
                S.op("act", lambda e: e.activation(out=SQ2, in_=G[:, ob, :], func=AF.Square), reads=[bG[ob]], writes=[pg["SQ"]])
                S.op("dve", lambda e: e.reduce_sum(out=st_ap[:, 0:4], in_=SQ2.rearrange("p (h d) -> p h d", h=4), axis=AX.X),
                     reads=[pg["SQ"]], writes=[st_b])
                rstd_from_ss(st_ap[:, 0:4], st_ap[:, 4:8], 128.0, [st_b])
                S.op("dve", lambda e: e.tensor_tensor(
                    out=ON2.rearrange("p (h d) -> p h d", h=4), in0=G[:, ob, :].rearrange("p (h d) -> p h d", h=4),
                    in1=st_ap[:, 4:8].unsqueeze(2).to_broadcast([128, 4, 128]), op=ALU.mult),
                    reads=[bG[ob], st_b], writes=[pg["ON"]])
                S.op("dve", lambda e: e.tensor_tensor(out=OB2[r_], in0=ON2, in1=RBG[:, i, :], op=ALU.mult),
                     reads=[pg["ON"], pg["RBG"]], writes=[bOB[r_]])

            def gla_TR(i):
                r_ = i % 2
                ptv2 = PTR[:, r_, 0:512].rearrange("p (h c) -> p h c", h=4)
                for h in range(4):
                    S.op("pe", lambda e, h=h: e.transpose(ptv2[:, h, :], OB2[r_][:, h * 128:(h + 1) * 128], IDB[:]),
                         reads=[bOB[r_], bC], writes=[bPTR[r_]])
                S.op("act", lambda e: e.copy(OT[:, 4:8, i * 128:(i + 1) * 128], ptv2), reads=[bPTR[r_]], writes=[bOT[i]])

            stage(6 + 10 * sg)
            dead = bGT + bKH + bETP + bETM + bEH + bXG + [pg["GLT"]] + \
                [bS32[d_][r_][pr][0] for d_ in range(2) for r_ in range(2) for pr in range(2)]
            x1_offs = [0, 4096, 8192, 12288, 32768, 36864, 73728, 77824]
            X1t = [rview(o_, [128, D], F32) for o_ in x1_offs]
            bX1 = [Buf("X1_%d" % i).inherit(dead) for i in range(NT)]

            def post_residual(ps_banks, ps_ap, base_ap, base_bufs, gate_i, out_ap, out_bufs, junk_slot):
                st_ap, st_b = stat_cols(2)
                S.op("act", lambda e: e.activation(out=XN[:, junk_slot, :], in_=ps_ap, func=AF.Square, accum_out=st_ap[:, 0:1]),
                     reads=ps_banks, writes=[bXN[junk_slot], st_b])
                rstd_from_ss(st_ap[:, 0:1], st_ap[:, 1:2], float(D), [st_b])
                S.op("dve", lambda e: e.scalar_tensor_tensor(out=out_ap, in0=ps_ap, scalar=st_ap[:, 1:2], in1=GB[:, ty, gate_i, :],
                                                             op0=ALU.mult, op1=ALU.mult), reads=ps_banks + [st_b, bGB], writes=out_bufs)
                S.op("dve", lambda e: e.tensor_tensor(out=out_ap, in0=out_ap, in1=base_ap, op=ALU.add),
                     reads=out_bufs + base_bufs, writes=out_bufs)

            slot_a, wva = ws_load([(0, 512, wcols(w_out_d, 0, 512))])
            slot_b, wvb = ws_load([(0, 512, wcols(w_out_d, 512, 512))])
            st2 = {}

            def p4_M(i):
                for half, (sl, wv_) in enumerate(((slot_a, wva), (slot_b, wvb))):
                    for kc in range(8):
                        S.op("pe", lambda e, kc=kc, half=half, wv_=wv_: e.matmul(
                            G[:, half, :], OT[:, kc, i * 128:(i + 1) * 128], wv_[:, kc, :], start=(kc == 0), stop=(kc == 7)),
                            reads=[bWS[sl], bOT[i]], writes=[bG[half]])
                xs = i % 2
                S.dma("sp", lambda e: e.dma_start(out=XT[:, xs, :], in_=x_d[row0 + i * 128: row0 + (i + 1) * 128, :]),
                      writes=[bXT[xs]])

            def p4_A2(i):
                st2[i] = norm_stats(X1t[i], [bX1[i]], i % 2)
                S.op("dve", lambda e: e.tensor_scalar_mul(XN[:, i % 2, :], X1t[i], st2[i][0][:, 1:2]),
                     reads=[bX1[i], st2[i][1]], writes=[bXN[i % 2]])

            def p4_TR(i):
                pslot = i % 2
                ptv = PTR[:, pslot, :].rearrange("p (k c) -> p k c", k=8)
                for kc in range(8):
                    S.op("pe", lambda e, kc=kc: e.transpose(ptv[:, kc, :], XN[:, i % 2, kc * 128:(kc + 1) * 128], IDB[:]),
                         reads=[bXN[i % 2], bC], writes=[bPTR[pslot]])

            def p4_D1(i):
                xs = i % 2
                post_residual([bG[0], bG[1]], G[:, 0:2, :].rearrange("p a b -> p (a b)"), XT[:, xs, :], [bXT[xs]], 0,
                              X1t[i], [bX1[i]], 2)

            OFF = DBG.get("off", 4)
            for t in range(NT + OFF + 4):
                s_ = t - OFF
                if 0 <= s_ - 3 < NT:
                    norm_evac(s_ - 3, ty, 1, eng="mix")
                if 0 <= s_ < NT:
                    p4_M(s_)
                if t < NT:
                    gla_AT(t)
                if 1 <= t <= NT:
                    gla_O(t - 1)
                if 0 <= s_ - 1 < NT:
                    p4_A2(s_ - 1)
                if 2 <= t <= NT + 1:
                    gla_E1(t - 2)
                if 0 <= s_ < NT:
                    p4_D1(s_)
                if 3 <= t <= NT + 2:
                    gla_TR(t - 3)
                if 0 <= s_ - 2 < NT:
                    p4_TR(s_ - 2)

            pm = new_phase(["FT%d" % c for c in range(8)] + ["F1_%d" % j for j in range(8)] + ["RL0", "RL1", "RL2"])
            R_live = R_live + bX1
            F1 = rview(16384, [128, 8, T], BF16)
            FT = rview(40960, [128, 8, T], F32)
            RL = rview(81920, [128, 3, 512], F32)
            bFT = [pm["FT%d" % c] for c in range(8)]
            bF1 = [pm["F1_%d" % j] for j in range(8)]
            bRL = [pm["RL0"], pm["RL1"], pm["RL2"]]

            stage(7 + 10 * sg)
            bFTh = [[Buf("FT%d_%d" % (c, hf)).inherit([bFT[c]]) for hf in range(2)] for c in range(8)]
            R_live = R_live + [b_ for pair in bFTh for b_ in pair]

            def phase6_tile(i):
                b0 = 4 + (i % 2) * 2
                hf = i // 4
                for c_ in range(8):
                    bank = b0 + c_ // 4
                    S.op("pe", lambda e, c_=c_, bank=bank: e.transpose(
                        G[:, bank, (c_ % 4) * 128:(c_ % 4 + 1) * 128], FT[:, c_, i * 128:(i + 1) * 128], IDF[:]),
                        reads=[bFTh[c_][hf], bC], writes=[bG[bank]])
                xs = i % 2
                post_residual([bG[b0], bG[b0 + 1]], G[:, b0:b0 + 2, :].rearrange("p a b -> p (a b)"), X1t[i], [bX1[i]], 1,
                              XT[:, xs, :], [bXT[xs]], 2)
                S.dma("sp", lambda e: e.dma_start(out=y_d[row0 + i * 128: row0 + (i + 1) * 128, :], in_=XT[:, xs, :]),
                      reads=[bXT[xs]])

            rl_i = 0
            for q in range(4):
                for blk in range(2):
                    slot, wv = ws_load([(0, 512, wcols(w1_d, q * 1024 + blk * 512, 512))])
                    for j in range(4):
                        jj = blk * 4 + j
                        for half in range(2):
                            bank = proj_fm(wv, slot, j * 128, 128, half)
                            rs = rl_i % 3
                            rl_i += 1
                            S.op("act", lambda e, bank=bank, rs=rs: e.activation(out=RL[:, rs, :], in_=G[:, bank, :], func=AF.Relu),
                                 reads=[bG[bank]], writes=[bRL[rs]])
                            S.op("dve", lambda e, rs=rs, jj=jj, half=half: e.tensor_tensor(
                                out=F1[:, jj, half * 512:(half + 1) * 512], in0=RL[:, rs, :], in1=RL[:, rs, :], op=ALU.mult),
                                reads=[bRL[rs]], writes=[bF1[jj]])
                w2b = [ws_load([(0, 512, wrows(w2_d, q * 1024, blk * 512, 512))]) for blk in range(2)]
                order = [(blk, c4, half) for blk in range(2) for c4 in range(4) for half in range(2)] if q < 3 else \
                        [(blk, c4, half) for half in range(2) for blk in range(2) for c4 in range(4)]
                for gi_, (blk, c4, half) in enumerate(order):
                    slot, wv = w2b[blk]
                    c_ = blk * 4 + c4
                    bank = (pp_i[0]) % 4
                    pp_i[0] += 1
                    for hc in range(8):
                        S.op("pe", lambda e, hc=hc, c4=c4, half=half, bank=bank, wv=wv: e.matmul(
                            G[:, bank, :], wv[:, hc, c4 * 128:(c4 + 1) * 128], F1[:, hc, half * 512:(half + 1) * 512],
                            start=(hc == 0), stop=(hc == 7)), reads=[bWS[slot], bF1[hc]], writes=[bG[bank]])
                    dst = FT[:, c_, half * 512:(half + 1) * 512]
                    if q == 0:
                        S.op("act", lambda e, bank=bank, dst=dst: e.copy(dst, G[:, bank, :]), reads=[bG[bank]], writes=[bFTh[c_][half]])
                    else:
                        S.op("dve", lambda e, bank=bank, dst=dst: e.tensor_tensor(out=dst, in0=dst, in1=G[:, bank, :], op=ALU.add),
                             reads=[bG[bank], bFTh[c_][half]], writes=[bFTh[c_][half]])
                    if q == 3 and half == 1 and gi_ % 2 == 1:
                        phase6_tile((gi_ - 8) // 2)
            stage(8 + 10 * sg)
            for i in range(4, NT):
                phase6_tile(i)

        stage(1)
        run_sg(0)
        stage(9)
        run_sg(1)
        S.stopped = False
        S.finish("sp")
        S.emit(block)
    return nc


_NC_CACHE = {}


def kernel(x_prompt, x_sample, c, cache_k, cache_v, state_fwd, state_bwd, c_ctx,
           w_ada, b_ada, norm_attn_pre, norm_attn_post, norm_mlp_pre, norm_mlp_post,
           w_in, w_gate_fwd, b_gate_fwd, w_gate_bwd, b_gate_bwd,
           lam_q1, lam_k1, lam_q2, lam_k2, diff_norm, gla_norm, w_out, w_mlp1, w_mlp2):
    f = lambda a: np.ascontiguousarray(np.asarray(a, dtype=np.float32))
    x_prompt, x_sample = f(x_prompt), f(x_sample)
    consts = _host_consts()
    shared = {
        "w_ada": f(w_ada)[0], "b_ada": f(b_ada)[0],
        "norm_attn_pre": f(norm_attn_pre)[0], "norm_attn_post": f(norm_attn_post)[0],
        "norm_mlp_pre": f(norm_mlp_pre)[0], "norm_mlp_post": f(norm_mlp_post)[0],
        "w_in": f(w_in)[0], "w_gate_fwd": f(w_gate_fwd)[0], "w_gate_bwd": f(w_gate_bwd)[0],
        "b_gate_fwd": f(b_gate_fwd)[0], "b_gate_bwd": f(b_gate_bwd)[0],
        "lam_q1": f(lam_q1)[0], "lam_k1": f(lam_k1)[0], "lam_q2": f(lam_q2)[0], "lam_k2": f(lam_k2)[0],
        "diff_norm": f(diff_norm)[0], "gla_norm": f(gla_norm)[0],
        "w_out": f(w_out)[0], "w_mlp1": f(w_mlp1)[0], "w_mlp2": f(w_mlp2)[0],
    }
    shared.update(consts)
    in_maps = []
    for i in range(N_CORES):
        m = dict(shared)
        m["x"] = np.concatenate([x_sample[i], x_prompt[4 * i:4 * i + 4].reshape(1024, D)], axis=0)
        m["cvec"] = np.stack([f(c)[i], f(c_ctx)], axis=0)
        m["cache_k"] = f(cache_k)[i, 0]
        m["cache_v"] = f(cache_v)[i, 0]
        m["state_f"] = f(state_fwd)[i, 0]
        m["state_b"] = f(state_bwd)[i, 0]
        in_maps.append(m)
    if "nc" not in _NC_CACHE:
        _NC_CACHE["nc"] = build_nc()
    nc = _NC_CACHE["nc"]
    res = run_bass_kernel_spmd(nc, in_maps, core_ids=list(range(N_CORES)))
    outs = res.results
    y_sample = np.stack([outs[i]["y"][0:T] for i in range(N_CORES)], axis=0)
    y_prompt = np.concatenate([outs[i]["y"][T:2 * T].reshape(4, 256, D) for i in range(N_CORES)], axis=0)
    new_k = np.concatenate([outs[i]["new_k"] for i in range(N_CORES)], axis=0)[:, None]
    new_v = np.concatenate([outs[i]["new_v"] for i in range(N_CORES)], axis=0)[:, None]
    new_sf = np.concatenate([outs[i]["new_sf"] for i in range(N_CORES)], axis=0)[:, None]
    new_sb = np.concatenate([outs[i]["new_sb"] for i in range(N_CORES)], axis=0)[:, None]
    return (y_prompt.astype(np.float32), y_sample.astype(np.float32), new_k.astype(np.float32),
            new_v.astype(np.float32), new_sf.astype(np.float32), new_sb.astype(np.float32))
```

```python
import math
from contextlib import ExitStack

import numpy as np
import concourse.bass as bass
import concourse.mybir as mybir
from concourse.bass_utils import run_bass_kernel_spmd

F32 = mybir.dt.float32
BF16 = mybir.dt.bfloat16
AF = mybir.ActivationFunctionType
ALU = mybir.AluOpType
AX = mybir.AxisListType

D = 1024
T = 1024
NT = 8
EPS = 1e-6
LAM_INIT = 0.8 - 0.6 * math.exp(0.0)
N_CORES = 8
STOP = [0]
DBG = {}


class _Stop(Exception):
    pass


class Buf:
    __slots__ = ("name", "w", "r", "excl", "same_ok")

    def __init__(self, name="", excl=False, same_ok=False):
        self.name = name
        self.w = None
        self.r = {}
        self.same_ok = same_ok
        self.excl = excl

    def inherit(self, olds):
        for o in olds:
            if o.w is not None:
                self.r[o.w[0]] = max(self.r.get(o.w[0], 0), o.w[1])
            for k, v in o.r.items():
                self.r[k] = max(self.r.get(k, 0), v)
        return self


class Sched:
    COMPUTE = ("pe", "act", "dve", "pool")

    def __init__(self, nc, stack, n_dma_sems=10):
        self.nc = nc
        self.items = {e: [] for e in ("pe", "act", "dve", "pool", "sp")}
        self.sems = {}
        for e in self.COMPUTE:
            self.sems[e] = stack.enter_context(nc.semaphore("s_" + e))
        self.cnt = {e: 0 for e in self.COMPUTE}
        self.dq = {}
        for q in ("sp", "pool"):
            lst = []
            for i in range(n_dma_sems):
                key = "d_%s_%d" % (q, i)
                self.sems[key] = stack.enter_context(nc.semaphore(key))
                lst.append(key)
            self.dq[q] = {"keys": lst, "n": 0}
        self.known = {e: {} for e in self.items}
        self.stopped = False

    def _deps(self, eng, reads, writes):
        deps = {}

        def add(ev, b, is_write):
            if ev is None:
                return
            k, v = ev
            if k == eng and (eng == "pe" or (is_write and b.same_ok)):
                return
            if deps.get(k, 0) < v:
                deps[k] = v
        for b in reads:
            add(b.w, b, False)
            if b.excl:
                for k, v in b.r.items():
                    if k != eng:
                        add((k, v), b, False)
        for b in writes:
            add(b.w, b, True)
            for k, v in b.r.items():
                add((k, v), b, True)
        out = []
        kn = self.known[eng]
        for k, v in deps.items():
            if kn.get(k, 0) < v:
                kn[k] = v
                out.append((k, v))
        return out

    def _mark(self, ev, reads, writes):
        k, v = ev
        for b in reads:
            if b.r.get(k, 0) < v:
                b.r[k] = v
        for b in writes:
            b.w = ev
            b.r = {}

    def op(self, eng, fn, reads=(), writes=()):
        if self.stopped:
            return None
        waits = self._deps(eng, reads, writes)
        self.cnt[eng] += 1
        ev = (eng, self.cnt[eng])
        self.items[eng].append((waits, fn, (eng, 1)))
        self._mark(ev, reads, writes)
        return ev

    def dma(self, q, fn, reads=(), writes=()):
        if self.stopped:
            return None
        d = self.dq[q]
        i = d["n"]
        d["n"] += 1
        nk = len(d["keys"])
        key = d["keys"][i % nk]
        val = 16 * (i // nk + 1)
        waits = self._deps(q, reads, writes)
        if i >= nk and self.known[q].get(key, 0) < val - 16:
            self.known[q][key] = val - 16
            waits.append((key, val - 16))
        ev = (key, val)
        self.items[q].append((waits, fn, (key, 16)))
        self._mark(ev, reads, writes)
        return ev

    def finish(self, eng="sp"):
        waits = []
        for e in self.COMPUTE:
            if self.cnt[e] > 0:
                waits.append((e, self.cnt[e]))
        for q, d in self.dq.items():
            nk = len(d["keys"])
            for j, key in enumerate(d["keys"]):
                n = (d["n"] - j + nk - 1) // nk if d["n"] > j else 0
                if n > 0:
                    waits.append((key, 16 * n))
        self.items[eng].append((waits, None, None))

    def emit(self, block):
        sems = self.sems
        needed = {e: set() for e in self.COMPUTE}
        for lst in self.items.values():
            for waits, fn, inc in lst:
                for k, v in waits:
                    if k in needed:
                        needed[k].add(v)
        rank = {}
        for e in self.COMPUTE:
            rank[e] = {v: i + 1 for i, v in enumerate(sorted(needed[e]))}

        def run(engobj, lst):
            idx = 0
            for waits, fn, inc in lst:
                ws = [(k, rank[k][v] if k in rank else v) for k, v in waits]
                if fn is None:
                    for k, v in ws:
                        engobj.wait_ge(sems[k], v)
                    continue
                for k, v in ws[:-1]:
                    engobj.wait_ge(sems[k], v)
                ins = fn(engobj)
                if ws:
                    ins._wait_ge(sems[ws[-1][0]], ws[-1][1])
                if inc[0] in rank:
                    idx += 1
                    if idx in rank[inc[0]]:
                        ins.then_inc(sems[inc[0]], 1)
                else:
                    ins.then_inc(sems[inc[0]], inc[1])

        @block.tensor
        def _(e):
            run(e, self.items["pe"])

        @block.scalar
        def _(e):
            run(e, self.items["act"])

        @block.vector
        def _(e):
            run(e, self.items["dve"])

        @block.gpsimd
        def _(e):
            run(e, self.items["pool"])

        @block.sync
        def _(e):
            run(e, self.items["sp"])
        self.stats = {e: (self.cnt[e], len(rank[e])) for e in self.COMPUTE}


def _host_consts():
    c = {}
    c["ident_f"] = np.eye(128, dtype=np.float32)
    p = np.arange(128)
    same = (p[:, None] // 64) == (p[None, :] // 64)
    le = p[:, None] <= p[None, :]
    ge = p[:, None] >= p[None, :]
    lt = p[:, None] < p[None, :]
    gt = p[:, None] > p[None, :]
    sc = np.float32(-1.0 / 16.0)
    masks = np.zeros((6, 128, 128), np.float32)
    masks[0] = (same & le)
    masks[1] = (same & ge)
    masks[2] = (same & le) * sc
    masks[3] = (same & ge) * sc
    masks[4] = (same & gt) * sc
    masks[5] = (same & lt) * sc
    c["masks"] = np.ascontiguousarray(masks.transpose(1, 0, 2))
    d = p % 64
    is_col = (d >= 32)
    dd = d % 32
    fi = dd % 16
    second = dd >= 16
    inv = (np.float32(10000.0) ** (-(np.arange(16, dtype=np.float32)) / np.float32(16.0))).astype(np.float32)
    t = np.arange(T)
    rowpos = (t // 64).astype(np.float32)
    colpos = (t % 64).astype(np.float32)
    pos = np.where(is_col[:, None], colpos[None, :], rowpos[None, :]).astype(np.float32)
    ang = (pos * inv[fi][:, None]).astype(np.float32)
    cos = np.cos(ang).astype(np.float32)
    sin = np.sin(ang).astype(np.float32)
    sg = np.where(second[:, None], sin, -sin).astype(np.float32)
    c["rope"] = np.ascontiguousarray(np.stack([cos, sg], axis=1))
    swap = np.where(second, p - 16, p + 16)
    perm = np.zeros((128, 128), np.float32)
    perm[swap, p] = 1.0
    c["perm"] = perm
    sel = np.zeros((2, 2, 128), np.float32)
    sel[0, 0, :] = 1.0
    sel[1, 1, :] = 1.0
    c["sel"] = sel
    return c


def build_nc(debug=False):
    nc = bass.Bass("TRN2", target_bir_lowering=False)

    def din(name, shape):
        return nc.dram_tensor(name, list(shape), F32, kind="ExternalInput").ap()

    def dout(name, shape):
        return nc.dram_tensor(name, list(shape), F32, kind="ExternalOutput").ap()

    x_d = din("x", [2 * T, D])
    cvec_d = din("cvec", [2, D])
    cache_k_d = din("cache_k", [4, 256, 128])
    cache_v_d = din("cache_v", [4, 256, 128])
    state_d = [din("state_f", [4, 64, 128]), din("state_b", [4, 64, 128])]
    w_ada_d = din("w_ada", [D, 6 * D])
    b_ada_d = din("b_ada", [6 * D])
    npre1_d = din("norm_attn_pre", [D])
    npost1_d = din("norm_attn_post", [D])
    npre2_d = din("norm_mlp_pre", [D])
    npost2_d = din("norm_mlp_post", [D])
    w_in_d = din("w_in", [D, 3104])
    wg_d = [din("w_gate_fwd", [16, 256]), din("w_gate_bwd", [16, 256])]
    bg_d = [din("b_gate_fwd", [256]), din("b_gate_bwd", [256])]
    lam_d = [din("lam_q1", [64]), din("lam_k1", [64]), din("lam_q2", [64]), din("lam_k2", [64])]
    dnorm_d = din("diff_norm", [128])
    gnorm_d = din("gla_norm", [128])
    w_out_d = din("w_out", [D, D])
    w1_d = din("w_mlp1", [D, 4 * D])
    w2_d = din("w_mlp2", [4 * D, D])
    identf_d = din("ident_f", [128, 128])
    masks_d = din("masks", [128, 6, 128])
    rope_d = din("rope", [128, 2, T])
    perm_d = din("perm", [128, 128])
    sel_d = din("sel", [2, 2, 128])

    y_d = dout("y", [2 * T, D])
    newk_d = dout("new_k", [4, 4, 256, 128])
    newv_d = dout("new_v", [4, 4, 256, 128])
    news_d = [dout("new_sf", [4, 4, 64, 128]), dout("new_sb", [4, 4, 64, 128])]

    st = ExitStack()
    with st:
        S = Sched(nc, st)

        def stage(k):
            if STOP[0] == k:
                S.stopped = True

        def sb(name, shape, dt):
            return st.enter_context(nc.sbuf_tensor(name, list(shape), dt))

        IDB = sb("IDB", [128, 128], BF16)
        IDF = sb("IDF", [128, 128], F32)
        MASKS = sb("MASKS", [128, 6, 128], F32)
        PERM = sb("PERM", [128, 128], BF16)
        ROPE = sb("ROPE", [128, 2, T], F32)
        SEL = sb("SEL", [2, 2, 128], F32)
        SC = sb("SC", [128, 8, 2], BF16)
        MODC = sb("MODC", [128, 4, 8, 2], F32)
        AB = sb("AB", [128, 4, 8, 2], F32)
        GB = sb("GB", [128, 2, 2, D], F32)
        DG = sb("DG", [128, 4, 128], F32)
        GG = sb("GG", [128, 4, 128], F32)
        BG = sb("BG", [128, 2, 256], F32)
        WG = sb("WG", [48, 256], BF16)
        LAMT = sb("LAMT", [128, 4, 64], F32)
        LAMS = sb("LAMS", [128, 8], F32)
        SMALL = sb("SMALL", [128, 96], F32)
        XT = sb("XT", [128, 2, D], F32)
        XN = sb("XN", [128, 3, D], BF16)
        HT = sb("HT", [128, 8, T], BF16)
        OT = sb("OT", [128, 8, T], BF16)
        WS = sb("WS", [128, 3, 4096], BF16)
        RBYTES = 96 * 1024
        R = sb("R", [128, RBYTES // 2], BF16)
        G = st.enter_context(nc.psum_tensor("G", [128, 8, 512], F32))
        PTR = G[:, 6:8, :].bitcast(BF16)
        block = st.enter_context(nc.Block())

        def rview(off, shape, dt):
            esz = 2 if dt == BF16 else 4
            n = 1
            for s_ in shape[1:]:
                n *= s_
            assert off % 4 == 0 and off + n * esz <= RBYTES, (off, shape)
            ap = R[0:shape[0], off // 2:(off + n * esz) // 2]
            if dt != BF16:
                ap = ap.bitcast(dt)
            if len(shape) == 3:
                ap = ap.rearrange("p (a b) -> p a b", a=shape[1])
            elif len(shape) == 4:
                ap = ap.rearrange("p (a b c) -> p a b c", a=shape[1], b=shape[2])
            return ap

        bG = [Buf("G%d" % i, excl=True) for i in range(8)]
        bPTR = bG[6:8]
        bWS = [Buf("WS%d" % i) for i in range(3)]
        bXT = [Buf("XT%d" % i) for i in range(2)]
        bXN = [Buf("XN0"), Buf("XN1"), Buf("XNjunk", same_ok=True)]
        bHT = [Buf("HT%d" % i, same_ok=True) for i in range(NT)]
        bOT = [Buf("OT%d" % i, same_ok=True) for i in range(NT)]
        bC = Buf("consts")
        bSM = {}

        def smb(name):
            if name not in bSM:
                bSM[name] = Buf(name)
            return bSM[name]

        R_live = []

        def new_phase(names):
            nonlocal R_live
            out = {}
            for n_ in names:
                out[n_] = Buf(n_).inherit(R_live)
            R_live = list(out.values())
            return out

        small_next = [0]

        def small(n):
            a = small_next[0]
            small_next[0] += n
            assert small_next[0] <= 64
            return SMALL[:, a:a + n]

        ws_n = [0]

        def ws_load(src_list):
            slot = ws_n[0] % 3
            ws_n[0] += 1
            view = WS[:, slot, :].rearrange("p (k c) -> p k c", k=8)
            for c0, ncol, src in src_list:
                S.dma("pool", lambda e, c0=c0, ncol=ncol, src=src, view=view:
                      e.dma_start(out=view[:, :, c0:c0 + ncol], in_=src), writes=[bWS[slot]])
            return slot, view

        def wcols(w_ap, c0, ncol):
            return w_ap.rearrange("(kc p) c -> p kc c", p=128)[:, :, c0:c0 + ncol]

        def wrows(w_ap, r0, c0, ncol):
            return w_ap[r0:r0 + 1024, :].rearrange("(kc p) c -> p kc c", p=128)[:, :, c0:c0 + ncol]

        const_bufs = []

        def cw():
            b_ = Buf("c%d" % len(const_bufs))
            const_bufs.append(b_)
            return b_

        def cr():
            return list(const_bufs)

        S.dma("sp", lambda e: e.dma_start(out=IDF[:], in_=identf_d), writes=[cw()])
        S.dma("pool", lambda e: e.dma_start(out=IDB[:], in_=identf_d), writes=[cw()])
        S.dma("sp", lambda e: e.dma_start(out=MASKS[:], in_=masks_d), writes=[cw()])
        S.dma("pool", lambda e: e.dma_start(out=PERM[:], in_=perm_d), writes=[cw()])
        S.dma("sp", lambda e: e.dma_start(out=ROPE[:], in_=rope_d), writes=[cw()])
        S.dma("sp", lambda e: e.dma_start(out=SEL[:], in_=sel_d), writes=[cw()])
        ROWS = sb("ROWS", [64, 128], F32)
        COLS = sb("COLS", [128, 64], F32)
        S.dma("sp", lambda e: e.dma_start(out=ROWS[0:16, :], in_=cvec_d.rearrange("t (k p) -> (t k) p", p=128)), writes=[cw()])
        for mi, m in enumerate((0, 1, 3, 4)):
            S.dma("sp", lambda e, mi=mi, m=m: e.dma_start(
                out=ROWS[16 + 8 * mi:24 + 8 * mi, :], in_=b_ada_d[m * D:(m + 1) * D].rearrange("(j p) -> j p", p=128)), writes=[cw()])
        S.dma("sp", lambda e: e.dma_start(out=ROWS[48:56, :], in_=npre1_d.rearrange("(j p) -> j p", p=128)), writes=[cw()])
        S.dma("sp", lambda e: e.dma_start(out=ROWS[56:64, :], in_=npre2_d.rearrange("(j p) -> j p", p=128)), writes=[cw()])
        S.op("pe", lambda e: e.transpose(G[:, 2, 0:64], ROWS[:], IDF[0:64, 0:64]), reads=cr(), writes=[bG[2]])
        S.op("dve", lambda e: e.tensor_copy(COLS[:], G[:, 2, 0:64]), reads=[bG[2]], writes=[cw()])
        for h in range(4):
            S.dma("sp", lambda e, h=h: e.dma_start(out=DG[:, h, :], in_=dnorm_d.partition_broadcast(128)), writes=[cw()])
            S.dma("sp", lambda e, h=h: e.dma_start(out=GG[:, h, :], in_=gnorm_d.partition_broadcast(128)), writes=[cw()])
            S.dma("sp", lambda e, h=h: e.dma_start(out=LAMT[:, h, :], in_=lam_d[h].partition_broadcast(128)), writes=[cw()])
        for d_ in range(2):
            S.dma("sp", lambda e, d_=d_: e.dma_start(out=BG[:, d_, :], in_=bg_d[d_].partition_broadcast(128)), writes=[cw()])
            S.dma("pool", lambda e, d_=d_: e.dma_start(out=WG[32 * d_:32 * d_ + 16, :], in_=wg_d[d_]), writes=[cw()])
        stage(101)
        S.op("dve", lambda e: e.tensor_scalar_mul(DG[:], DG[:], float(1.0 - LAM_INIT)), reads=cr(), writes=[cw()])
        S.op("dve", lambda e: e.tensor_tensor(out=LAMT[:, 0, :], in0=LAMT[:, 0, :], in1=LAMT[:, 1, :], op=ALU.mult), reads=cr(), writes=[cw()])
        S.op("dve", lambda e: e.tensor_tensor(out=LAMT[:, 2, :], in0=LAMT[:, 2, :], in1=LAMT[:, 3, :], op=ALU.mult), reads=cr(), writes=[cw()])
        S.op("dve", lambda e: e.reduce_sum(out=LAMS[:, 0:1], in_=LAMT[:, 0, :], axis=AX.X), reads=cr(), writes=[cw()])
        S.op("dve", lambda e: e.reduce_sum(out=LAMS[:, 1:2], in_=LAMT[:, 2, :], axis=AX.X), reads=cr(), writes=[cw()])
        S.op("act", lambda e: e.activation(out=LAMS[:, 2:4], in_=LAMS[:, 0:2], func=AF.Exp), reads=cr(), writes=[cw()])
        S.op("dve", lambda e: e.memset(LAMS[:, 4:5], 1.0), reads=cr(), writes=[cw()])
        S.op("dve", lambda e: e.scalar_tensor_tensor(out=LAMS[:, 5:6], in0=LAMS[:, 3:4], scalar=float(-LAM_INIT),
                                                     in1=LAMS[:, 2:3], op0=ALU.add, op1=ALU.subtract), reads=cr(), writes=[cw()])

        stage(102)
        JOIN = sb("JOIN", [128, 2], F32)
        S.op("dve", lambda e: e.memset(JOIN[:], 0.0), reads=cr(), writes=[bC])
        def rstd_from_ss(ss_ap, out_ap, n, bufs):
            S.op("act", lambda e: e.activation(out=out_ap, in_=ss_ap, func=AF.Ln, scale=1.0 / n, bias=EPSC[:, 0:1]),
                 reads=bufs + [bC], writes=bufs)
            S.op("act", lambda e: e.activation(out=out_ap, in_=out_ap, func=AF.Exp, scale=-0.5), reads=bufs, writes=bufs)

        EPSC = sb("EPSC", [128, 2], F32)
        S.op("dve", lambda e: e.memset(EPSC[:, 0:1], EPS), writes=[bC])
        S.op("dve", lambda e: e.memset(EPSC[:, 1:2], 1.0), writes=[bC])

        stat_i = [0]

        def stat_cols(n):
            k = stat_i[0] % 8
            stat_i[0] += 1
            return SMALL[:, k * 8:k * 8 + n], smb("stat%d" % k)

        xp0 = [(rview(32768 + i * 4096, [128, D], F32), Buf("XP%d" % i)) for i in range(NT)]
        XNA = rview(65536, [128, 8, D], BF16)
        bXNA = [Buf("XNA%d" % i) for i in range(NT)]
        R_live = R_live + [b_ for _, b_ in xp0] + bXNA
        for i in range(NT):
            S.dma("sp", lambda e, i=i: e.dma_start(out=xp0[i][0], in_=x_d[i * 128:(i + 1) * 128, :]), writes=[xp0[i][1]])
        for i in range(NT):
            st_ap, st_b = stat_cols(2)
            S.op("act", lambda e, i=i, st_ap=st_ap: e.activation(out=XNA[:, i, :], in_=xp0[i][0], func=AF.Square, accum_out=st_ap[:, 0:1]),
                 reads=[xp0[i][1]], writes=[bXNA[i], st_b])
            rstd_from_ss(st_ap[:, 0:1], st_ap[:, 1:2], float(D), [st_b])
            S.op("dve", lambda e, i=i, st_ap=st_ap: e.tensor_scalar_mul(XNA[:, i, :], xp0[i][0], st_ap[:, 1:2]),
                 reads=[xp0[i][1], st_b], writes=[bXNA[i]])

        CT = COLS[:, 0:16].rearrange("p (t k) -> p k t", t=2)
        BADAC = COLS[:, 16:48].rearrange("p (m j) -> p m j", m=4)
        GPRE = COLS[:, 48:64].rearrange("p (n j) -> p n j", n=2)
        S.op("act", lambda e: e.activation(out=SC[:], in_=CT, func=AF.Silu), reads=[bC], writes=[bC])
        GROWX = XT[0:2, :, :]
        PMC = G[:, 0, 0:64].rearrange("p (a b c) -> p a b c", a=4, b=8)
        col_mods = {0: 0, 1: 1, 3: 2, 4: 3}
        bAB = [Buf("AB0"), Buf("AB1")]

        def adaln_block(m, half, cb=0, rb=1):
            slot, wv = ws_load([(0, 512, wcols(w_ada_d, m * D + half * 512, 512))])
            pmc = G[:, cb, 0:64].rearrange("p (a b c) -> p a b c", a=4, b=8)
            if m in col_mods:
                mi = col_mods[m]
                for j in range(4):
                    jj = half * 4 + j
                    for kc in range(8):
                        S.op("pe", lambda e, jj=jj, kc=kc, j=j: e.matmul(
                            pmc[:, mi, jj, :], wv[:, kc, j * 128:(j + 1) * 128], SC[:, kc, :],
                            start=(kc == 0), stop=(kc == 7)), reads=[bWS[slot], bC], writes=[bG[cb]])
                S.op("dve", lambda e: e.tensor_copy(MODC[:, mi, half * 4:half * 4 + 4, :], pmc[:, mi, half * 4:half * 4 + 4, :]),
                     reads=[bG[cb]], writes=[bAB[mi // 2]])
            else:
                gi = 0 if m == 2 else 1
                for kc in range(8):
                    S.op("pe", lambda e, kc=kc: e.matmul(
                        G[0:2, rb, :], SC[:, kc, :], wv[:, kc, :], start=(kc == 0), stop=(kc == 7)),
                        reads=[bWS[slot], bC], writes=[bG[rb]])
                S.op("dve", lambda e: e.tensor_copy(GROWX[:, gi, half * 512:(half + 1) * 512], G[0:2, rb, :]),
                     reads=[bG[rb]], writes=[bXT[0], bXT[1]])

        ada_tasks = [(m, half) for m in (3, 4, 2, 5) for half in range(2)]

        def ada_step(cb=0, rb=1):
            if ada_tasks:
                adaln_block(*ada_tasks.pop(0), cb=cb, rb=rb)


        def adaln_cols(n_):
            S.op("dve", lambda e: e.tensor_tensor(out=MODC[:, 2 * n_:2 * n_ + 2], in0=MODC[:, 2 * n_:2 * n_ + 2],
                                                  in1=BADAC[:, 2 * n_:2 * n_ + 2, :].unsqueeze(3).to_broadcast([128, 2, 8, 2]), op=ALU.add),
                 reads=[bAB[n_], bC], writes=[bAB[n_]])
            S.op("dve", lambda e: e.scalar_tensor_tensor(
                out=AB[:, 2 * n_, :, :], in0=MODC[:, 2 * n_ + 1, :, :], scalar=1.0,
                in1=GPRE[:, n_, :].unsqueeze(2).to_broadcast([128, 8, 2]), op0=ALU.add, op1=ALU.mult),
                reads=[bC, bAB[n_]], writes=[bAB[n_]])
            S.op("dve", lambda e: e.tensor_copy(AB[:, 2 * n_ + 1, :, :], MODC[:, 2 * n_, :, :]), reads=[bAB[n_]], writes=[bAB[n_]])

        for half in range(2):
            adaln_block(0, half)
        for half in range(2):
            adaln_block(1, half)
        adaln_cols(0)

        def adaln_finish_tasks(TB, bTB):
            combos = [(ty, gi, half) for ty in range(2) for gi in range(2) for half in range(2)]

            def bcast(lst):
                for n_, (ty, gi, half) in enumerate(lst):
                    bank = n_ % 2
                    S.op("pe", lambda e, ty=ty, gi=gi, half=half, bank=bank: e.matmul(
                        G[:, bank, :], SEL[:, ty, :], GROWX[:, gi, half * 512:(half + 1) * 512], start=True, stop=True),
                        reads=[bC, bXT[0], bXT[1]], writes=[bG[bank]])
                    S.op("act", lambda e, ty=ty, gi=gi, half=half, bank=bank: e.copy(
                        GB[:, ty, gi, half * 512:(half + 1) * 512], G[:, bank, :]), reads=[bG[bank]], writes=[bGB])

            def loads(gi):
                m, np_d = ((2, npost1_d), (5, npost2_d))[gi]
                S.dma("sp", lambda e: e.dma_start(out=TB[0], in_=b_ada_d[m * D:(m + 1) * D].partition_broadcast(128)), writes=[bTB[0]])
                S.dma("sp", lambda e: e.dma_start(out=TB[1], in_=np_d.partition_broadcast(128)), writes=[bTB[1]])

            def apply(gi):
                for ty in range(2):
                    S.op("dve", lambda e, ty=ty: e.tensor_tensor(out=GB[:, ty, gi, :], in0=GB[:, ty, gi, :], in1=TB[0], op=ALU.add),
                         reads=[bGB, bTB[0]], writes=[bGB])
                    S.op("dve", lambda e, ty=ty: e.tensor_tensor(out=GB[:, ty, gi, :], in0=GB[:, ty, gi, :], in1=TB[1], op=ALU.mult),
                         reads=[bGB, bTB[1]], writes=[bGB])

            def t0():
                while ada_tasks:
                    ada_step()
                adaln_cols(1)
                bcast(combos[0:4])
                loads(0)

            def t1():
                bcast(combos[4:8])

            def t2():
                apply(0)
                loads(1)

            def t3():
                apply(1)
            return [t0, t1, t2, t3]

        bGB = Buf("GB")

        def norm_stats(src_ap, src_bufs, xn_slot):
            st_ap, st_b = stat_cols(2)
            xn = XN[:, xn_slot, :]
            S.op("act", lambda e: e.activation(out=xn, in_=src_ap, func=AF.Square, accum_out=st_ap[:, 0:1]),
                 reads=src_bufs, writes=[bXN[xn_slot], st_b])
            rstd_from_ss(st_ap[:, 0:1], st_ap[:, 1:2], float(D), [st_b])
            return st_ap, st_b

        def norm_xn_T(src_ap, src_bufs, st, tile_i, xn_slot):
            st_ap, st_b = st
            xn = XN[:, xn_slot, :]
            S.op("dve", lambda e: e.tensor_scalar_mul(xn, src_ap, st_ap[:, 1:2]), reads=src_bufs + [st_b], writes=[bXN[xn_slot]])
            pslot = tile_i % 2
            ptv = PTR[:, pslot, :].rearrange("p (k c) -> p k c", k=8)
            for kc in range(8):
                S.op("pe", lambda e, kc=kc: e.transpose(ptv[:, kc, :], xn[:, kc * 128:(kc + 1) * 128], IDB[:]),
                     reads=[bXN[xn_slot], bC], writes=[bPTR[pslot]])

        def norm_evac(tile_i, ty, which, eng="act"):
            pslot = tile_i % 2
            ptv = PTR[:, pslot, :].rearrange("p (k c) -> p k c", k=8)
            for kc in range(8):
                if eng == "act" or (eng == "mix" and kc < 4):
                    S.op("act", lambda e, kc=kc: e.activation(
                        out=HT[:, kc, tile_i * 128:(tile_i + 1) * 128], in_=ptv[:, kc, :], func=AF.Identity,
                        bias=AB[:, 2 * which + 1, kc, ty:ty + 1], scale=AB[:, 2 * which, kc, ty:ty + 1]),
                        reads=[bPTR[pslot], bAB[which]], writes=[bHT[tile_i]])
                else:
                    S.op("dve", lambda e, kc=kc: e.tensor_scalar(
                        HT[:, kc, tile_i * 128:(tile_i + 1) * 128], ptv[:, kc, :],
                        AB[:, 2 * which, kc, ty:ty + 1], AB[:, 2 * which + 1, kc, ty:ty + 1], ALU.mult, ALU.add),
                        reads=[bPTR[pslot], bAB[which]], writes=[bHT[tile_i]])

        pp_i = [0]

        def proj_fm(wv, wslot, col0, M, half, banks=(0, 1, 2, 3)):
            bank = banks[pp_i[0] % len(banks)]
            pp_i[0] += 1
            for kc in range(8):
                S.op("pe", lambda e, kc=kc: e.matmul(G[0:M, bank, :], wv[:, kc, col0:col0 + M],
                                                     HT[:, kc, half * 512:(half + 1) * 512], start=(kc == 0), stop=(kc == 7)),
                     reads=[bWS[wslot]] + bHT[half * 4:half * 4 + 4], writes=[bG[bank]])
            return bank

        def proj_tm(wv, wslot, col0, N, tile_i, banks=(0, 1, 2, 3)):
            bank = banks[pp_i[0] % len(banks)]
            pp_i[0] += 1
            for kc in range(8):
                S.op("pe", lambda e, kc=kc: e.matmul(G[:, bank, 0:N], HT[:, kc, tile_i * 128:(tile_i + 1) * 128],
                                                     wv[:, kc, col0:col0 + N], start=(kc == 0), stop=(kc == 7)),
                     reads=[bWS[wslot], bHT[tile_i]], writes=[bG[bank]])
            return bank

        def run_sg(sg):
            ty = sg
            row0 = sg * T
            is_sample = (sg == 0)

            nonlocal R_live
            xp = []
            if sg == 0:
                for i in range(NT + 1):
                    if i < NT:
                        pslot = i % 2
                        ptv = PTR[:, pslot, :].rearrange("p (k c) -> p k c", k=8)
                        for kc in range(8):
                            S.op("pe", lambda e, kc=kc, i=i, ptv=ptv: e.transpose(ptv[:, kc, :], XNA[:, i, kc * 128:(kc + 1) * 128], IDB[:]),
                                 reads=[bXNA[i], bC], writes=[bPTR[pslot]])
                    if i >= 1:
                        norm_evac(i - 1, ty, 0, eng="dve")
            else:
                otf = OT[:].rearrange("p k t -> p (k t)").bitcast(F32).rearrange("p (a b) -> p a b", a=4)
                xp_ot = [Buf("XPOT%d" % i).inherit(bOT) for i in range(4)]
                for i in range(4):
                    xp.append((otf[:, i, :], xp_ot[i]))
                for i in range(2):
                    xp.append((rview(88064 + i * 4096, [128, D], F32), Buf("XPR%d" % i).inherit(R_live)))
                R_live = R_live + [xp[4][1], xp[5][1]]
                for i in range(2):
                    xp.append((XT[:, i, :], bXT[i]))
                for i in range(NT):
                    S.dma("sp", lambda e, i=i: e.dma_start(out=xp[i][0], in_=x_d[row0 + i * 128: row0 + (i + 1) * 128, :]),
                          writes=[xp[i][1]])
                for i in range(NT + 1):
                    if i < NT:
                        st_ = norm_stats(xp[i][0], [xp[i][1]], i % 2)
                        norm_xn_T(xp[i][0], [xp[i][1]], st_, i, i % 2)
                    if i >= 1:
                        norm_evac(i - 1, ty, 0, eng="dve")
            if sg == 1:
                for b_ in bOT:
                    b_.inherit(xp_ot)
            stage(2 + 10 * sg)
            pa = new_phase(["QAT0", "QAT1", "QAT2", "QAT3", "KAT0", "KAT1", "KAT2", "KAT3", "VA", "PT0", "PT1", "PT2",
                            "OA00", "OA01", "OA10", "OA11", "ACCS", "OB16b", "CK32", "T1", "T2", "XB16", "NK0", "NK1", "SQ", "ON", "OB16"])
            QAT = rview(0, [128, 4, T], BF16)
            KAT = [rview(8192, [128, 4, 1280], BF16), rview(59392, [128, 4, 1280], BF16)]
            VA = rview(18432, [128, 10, 4, 129], BF16)
            PT = rview(28800, [128, 3, 512], BF16)
            OA4 = rview(32768, [128, 2, 2, 512], F32)
            CK32 = rview(40960, [128, 2, 512], F32)
            T1 = rview(45056, [128, 512], F32)
            T2 = rview(47104, [128, 512], F32)
            XB16 = rview(49152, [128, 512], BF16)
            NK = rview(50176, [128, 2, 512], F32)
            SQ = rview(54272, [128, 512], F32)
            ON = rview(56320, [128, 512], F32)
            OB16 = rview(58368, [128, 512], BF16)
            bQAT = [pa["QAT%d" % h] for h in range(4)]
            bKAT = [pa["KAT%d" % h] for h in range(4)]
            bPT = [pa["PT%d" % h] for h in range(3)]
            bOAq = [[pa["OA%d%d" % (a, b)] for b in range(2)] for a in range(2)]
            bNK = [pa["NK0"], pa["NK1"]]
            KO = 256 if is_sample else 0
            VO = 2 if is_sample else 0

            S.op("dve", lambda e: e.memset(VA[:, :, :, 128:129], 1.0), writes=[pa["VA"]])
            S.op("dve", lambda e: e.memset(KAT[0][64:128, :, :], 0.0), writes=bKAT)
            S.op("dve", lambda e: e.memset(KAT[1][0:64, :, :], 0.0), writes=bKAT)
            if is_sample:
                for kc in range(2):
                    S.dma("sp", lambda e, kc=kc: e.dma_start(
                        out=CK32[:, kc, :].rearrange("p (h d) -> p h d", h=4),
                        in_=cache_k_d[:, kc * 128:(kc + 1) * 128, :].rearrange("h p d -> p h d")), writes=[pa["CK32"]])
                    S.dma("pool", lambda e, kc=kc: e.dma_start(
                        out=VA[:, kc, :, 0:128],
                        in_=cache_v_d[:, kc * 128:(kc + 1) * 128, :].rearrange("h p d -> p h d")), writes=[pa["VA"]])
                stage(21)
                for kc in range(2):
                    bank = 4 + kc
                    for h in range(4):
                        S.op("pe", lambda e, kc=kc, h=h, bank=bank: e.transpose(
                            G[:, bank, h * 128:(h + 1) * 128], CK32[:, kc, h * 128:(h + 1) * 128], IDF[:]),
                            reads=[pa["CK32"], bC], writes=[bG[bank]])
                    for s_ in range(2):
                        S.op("act", lambda e, kc=kc, bank=bank, s_=s_: e.copy(
                            KAT[s_][s_ * 64:(s_ + 1) * 64, :, kc * 128:(kc + 1) * 128],
                            G[s_ * 64:(s_ + 1) * 64, bank, :].rearrange("p (h k) -> p h k", h=4)),
                            reads=[bG[bank]], writes=bKAT)

            def fm_evac(bank, dst_ap, dst_bufs, half):
                if not is_sample or DBG.get('norope'):
                    for (p0, p1, dap) in dst_ap:
                        S.op("act", lambda e, p0=p0, p1=p1, dap=dap: e.copy(dap, G[p0:p1, bank, :]), reads=[bG[bank]], writes=dst_bufs)
                    return
                S.op("act", lambda e: e.copy(XB16, G[:, bank, :]), reads=[bG[bank]], writes=[pa["XB16"]])
                if not DBG.get('nomm'):
                    S.op("pe", lambda e: e.matmul(G[:, 4, :], PERM[:], XB16, start=True, stop=True),
                         reads=[pa["XB16"], bC], writes=[bG[4]])
                if DBG.get('nodve'):
                    return
                S.op("dve", lambda e: e.tensor_tensor(out=T1, in0=G[:, bank, :], in1=ROPE[:, 0, half * 512:(half + 1) * 512], op=ALU.mult),
                     reads=[bG[bank], bC], writes=[pa["T1"]])
                if DBG.get('nodve2'):
                    return
                b4 = bank if DBG.get('nomm') else 4
                S.op("dve", lambda e: e.tensor_tensor(out=T2, in0=G[:, b4, :], in1=ROPE[:, 1, half * 512:(half + 1) * 512], op=ALU.mult),
                     reads=[bG[b4], bC], writes=[pa["T2"]])
                if DBG.get('nodve3'):
                    return
                for (p0, p1, dap) in dst_ap:
                    S.op("dve", lambda e, p0=p0, p1=p1, dap=dap: e.tensor_tensor(out=dap, in0=T1[p0:p1, :], in1=T2[p0:p1, :], op=ALU.add),
                         reads=[pa["T1"], pa["T2"]], writes=dst_bufs)

            stage(22)
            slot, wv = ws_load([(0, 512, wcols(w_in_d, 0, 512))])
            prev = None
            for h in range(4):
                for half in range(2):
                    bank = proj_fm(wv, slot, h * 128, 128, half)
                    if prev is not None:
                        fm_evac(*prev)
                    prev = (bank, [(0, 128, QAT[:, h, half * 512:(half + 1) * 512])], [bQAT[h]], half)
            fm_evac(*prev)
            stage(23)
            slot, wv = ws_load([(0, 512, wcols(w_in_d, 512, 512))])
            prev = None
            for h in range(4):
                for half in range(2):
                    bank = proj_fm(wv, slot, h * 128, 128, half)
                    if prev is not None:
                        fm_evac(*prev)
                    prev = (bank, [(0, 64, KAT[0][0:64, h, KO + half * 512:KO + (half + 1) * 512]),
                                   (64, 128, KAT[1][64:128, h, KO + half * 512:KO + (half + 1) * 512])], [bKAT[h]], half)
            fm_evac(*prev)
            if not is_sample:
                for i in range(NT):
                    bank = proj_tm(wv, slot, 0, 512, i)
                    ns = i % 2
                    S.op("act", lambda e, bank=bank, ns=ns: e.copy(NK[:, ns, :], G[:, bank, :]), reads=[bG[bank]], writes=[bNK[ns]])
                    S.dma("sp", lambda e, i=i, ns=ns: e.dma_start(
                        out=newk_d[i // 2, :, (i % 2) * 128:(i % 2 + 1) * 128, :].rearrange("h t d -> t h d"),
                        in_=NK[:, ns, :].rearrange("p (h d) -> p h d", h=4)), reads=[bNK[ns]])
            stage(24)
            slot, wv = ws_load([(0, 512, wcols(w_in_d, 1024, 512))])
            for i in range(NT):
                bank = proj_tm(wv, slot, 0, 512, i)
                S.op("act", lambda e, bank=bank, i=i: e.copy(VA[:, VO + i, :, 0:128], G[:, bank, :].rearrange("p (h d) -> p h d", h=4)),
                     reads=[bG[bank]], writes=[pa["VA"]])
                if not is_sample:
                    ns = i % 2
                    S.op("dve", lambda e, bank=bank, ns=ns: e.tensor_copy(NK[:, ns, :], G[:, bank, :]), reads=[bG[bank]], writes=[bNK[ns]])
                    S.dma("sp", lambda e, i=i, ns=ns: e.dma_start(
                        out=newv_d[i // 2, :, (i % 2) * 128:(i % 2 + 1) * 128, :].rearrange("h t d -> t h d"),
                        in_=NK[:, ns, :].rearrange("p (h d) -> p h d", h=4)), reads=[bNK[ns]])

            stage(3 + 10 * sg)
            if is_sample:
                qblocks = [(qb_ * 256, [(kc * 128, kc) for kc in range(10)]) for qb_ in range(4)]
            else:
                qblocks = [(s_ * 256, [(s_ * 256 + kc * 128, 2 * s_ + kc) for kc in range(2)]) for s_ in range(4)]
            ACCS = rview(69632, [128, 4, 129], F32)
            its = []
            for qbi, (q0, keys) in enumerate(qblocks):
                for h in range(4):
                    for ki, (kcol, vch) in enumerate(keys):
                        its.append((qbi, q0, h, ki, len(keys), kcol, vch))

            head_no = {}
            for (qbi_, q0_, h_, ki_, nk_, kcol_, vch_) in its:
                if (qbi_, h_) not in head_no:
                    head_no[(qbi_, h_)] = len(head_no)

            def att_scores(n):
                qbi, q0, h, ki, nk, kcol, vch = its[n]
                sbank = n % 2
                psc = G[:, sbank, :].rearrange("p (s q) -> p s q", s=2)
                for s_ in range(2):
                    S.op("pe", lambda e, s_=s_: e.matmul(
                        psc[:, s_, :], KAT[s_][:, h, kcol:kcol + 128], QAT[:, h, q0:q0 + 256], start=True, stop=True),
                        reads=[bKAT[h], bQAT[h]], writes=[bG[sbank]])

            def att_exp_pv(n):
                qbi, q0, h, ki, nk, kcol, vch = its[n]
                sbank = n % 2
                pts = n % 3
                oas = qbi % 2
                S.op("act", lambda e: e.activation(out=PT[:, pts, :], in_=G[:, sbank, :], func=AF.Exp, scale=0.125),
                     reads=[bG[sbank]], writes=[bPT[pts]])
                ptv = PT[:, pts, :].rearrange("p (s q) -> p s q", s=2)
                hn = head_no[(qbi, h)]
                ab = 2 + 2 * (hn % 2)
                for qt in range(2):
                    for s_ in range(2):
                        bank = ab + qt
                        S.op("pe", lambda e, qt=qt, s_=s_, bank=bank: e.matmul(
                            G[:, bank, s_ * 129:(s_ + 1) * 129], ptv[:, s_, qt * 128:(qt + 1) * 128], VA[:, vch, h, :],
                            start=(ki == 0 and s_ == 0), stop=(ki == nk - 1), skip_group_check=True),
                            reads=[bPT[pts], pa["VA"]], writes=[bG[bank]])
                if ki != nk - 1:
                    return
                accv = G[:, ab:ab + 2, 0:258].rearrange("p q (s c) -> p q s c", s=2)
                acc_b = [bG[ab], bG[ab + 1]]
                st_ap, st_b = stat_cols(8)
                S.op("dve", lambda e: e.reciprocal(st_ap[:, 0:4].rearrange("p (q s) -> p q s", q=2), accv[:, :, :, 128]),
                     reads=acc_b, writes=[st_b])
                S.op("dve", lambda e: e.tensor_tensor(
                    out=st_ap[:, 4:8].rearrange("p (q s) -> p q s", q=2), in0=st_ap[:, 0:4].rearrange("p (q s) -> p q s", q=2),
                    in1=LAMS[:, 4:6].unsqueeze(1).to_broadcast([128, 2, 2]), op=ALU.mult), reads=[st_b, bC], writes=[st_b])
                for qt in range(2):
                    S.op("dve", lambda e, qt=qt: e.tensor_scalar_mul(T1[:, qt * 128:(qt + 1) * 128], accv[:, qt, 1, 0:128],
                                                                   st_ap[:, 4 + 2 * qt + 1:4 + 2 * qt + 2]),
                         reads=[bG[ab + qt], st_b], writes=[pa["T1"]])
                    S.op("dve", lambda e, qt=qt: e.scalar_tensor_tensor(
                        out=OA4[:, oas, qt, h * 128:(h + 1) * 128], in0=accv[:, qt, 0, 0:128],
                        scalar=st_ap[:, 4 + 2 * qt:4 + 2 * qt + 1], in1=T1[:, qt * 128:(qt + 1) * 128], op0=ALU.mult, op1=ALU.add),
                        reads=[bG[ab + qt], st_b, pa["T1"]], writes=[bOAq[oas][qt]])
                if h != 3:
                    return
                for qt in range(2):
                    tile_i = (q0 // 128) + qt
                    oav = OA4[:, oas, qt, :]
                    st_ap2, st_b2 = SMALL[:, 64 + 8 * (ep_i[0] % 2):72 + 8 * (ep_i[0] % 2)], smb("ep%d" % (ep_i[0] % 2))
                    ep_i[0] += 1
                    ob = qt
                    pslot = tile_i % 2
                    ptv2 = PTR[:, pslot, 0:512].rearrange("p (h c) -> p h c", h=4)
                    bo = bOAq[oas][qt]

                    def t_sq(oav=oav, bo=bo):
                        S.op("act", lambda e: e.activation(out=SQ, in_=oav, func=AF.Square), reads=[bo], writes=[pa["SQ"]])

                    def t_red(st_ap2=st_ap2, st_b2=st_b2):
                        S.op("dve", lambda e: e.reduce_sum(out=st_ap2[:, 0:4], in_=SQ.rearrange("p (h d) -> p h d", h=4), axis=AX.X),
                             reads=[pa["SQ"]], writes=[st_b2])

                    def t_rstd(st_ap2=st_ap2, st_b2=st_b2):
                        rstd_from_ss(st_ap2[:, 0:4], st_ap2[:, 4:8], 128.0, [st_b2])

                    def t_mul(oav=oav, bo=bo, st_ap2=st_ap2, st_b2=st_b2, ob=ob):
                        S.op("dve", lambda e: e.tensor_tensor(
                            out=ON.rearrange("p (h d) -> p h d", h=4), in0=oav.rearrange("p (h d) -> p h d", h=4),
                            in1=st_ap2[:, 4:8].unsqueeze(2).to_broadcast([128, 4, 128]), op=ALU.mult),
                            reads=[bo, st_b2], writes=[pa["ON"]])
                        S.op("dve", lambda e: e.tensor_tensor(out=OB16r[ob], in0=ON, in1=DG[:].rearrange("p h d -> p (h d)"), op=ALU.mult),
                             reads=[pa["ON"], bC], writes=[bOB16r[ob]])

                    def t_tr(ob=ob, ptv2=ptv2, pslot=pslot):
                        for hh_ in range(4):
                            S.op("pe", lambda e, hh_=hh_: e.transpose(ptv2[:, hh_, :], OB16r[ob][:, hh_ * 128:(hh_ + 1) * 128], IDB[:]),
                                 reads=[bOB16r[ob], bC], writes=[bPTR[pslot]])

                    def t_cp(ptv2=ptv2, pslot=pslot, tile_i=tile_i):
                        S.op("act", lambda e: e.copy(OT[:, 0:4, tile_i * 128:(tile_i + 1) * 128], ptv2),
                             reads=[bPTR[pslot]], writes=[bOT[tile_i]])

                    pending.extend([t_sq, t_red, t_rstd, t_mul, t_tr, t_cp])

            ep_i = [0]
            pending = []
            OB16r = [OB16, rview(71936, [128, 512], BF16)]
            bOB16r = [pa["OB16"], pa["OB16b"]]
            for n in range(len(its) + 1):
                if n < len(its):
                    att_scores(n)
                if n >= 1:
                    att_exp_pv(n - 1)
                for _ in range(1 if is_sample else 2):
                    if pending:
                        pending.pop(0)()
                if sg == 0 and n % 18 == 9:
                    ada_step(cb=6, rb=7)
            while pending:
                pending.pop(0)()

            stage(4 + 10 * sg)
            names = (["GT%d" % i for i in range(NT)] + ["QT0", "QT1", "KT0", "KT1", "KH0", "KH1", "RBG", "S160", "S161", "SQ", "ON", "GLT",
                     "EB0", "EB1"] + ["VB%d" % i for i in range(NT)]
                     + ["ETP0", "ETP1", "ETM0", "ETM1", "EH0", "EH1", "XG0", "XG1", "RBT0", "RBT1", "OB0", "OB1"]
                     + ["ATT%d%d" % (r_, d_) for r_ in range(2) for d_ in range(2)]
                     + ["S32_%d%d%d%d" % (d_, r_, pr, hh) for d_ in range(2) for r_ in range(2) for pr in range(2) for hh in range(2)])
            pg = new_phase(names)
            GT = rview(0, [128, 8, 512], F32)
            QTt = [rview(16384 + d_ * 4096, [128, 2, T], BF16) for d_ in range(2)]
            KTt = [rview(24576 + d_ * 4096, [128, 2, T], BF16) for d_ in range(2)]
            KH = [rview(32768 + d_ * 4096, [128, 8, 256], BF16) for d_ in range(2)]
            VB = rview(40960, [128, 8, 512], BF16)
            RBG = rview(49152, [128, 8, 512], BF16)
            S16 = [rview(57344 + d_ * 8192, [128, 2, 16, 128], BF16) for d_ in range(2)]
            RBT = [rview(57344 + r_ * 2048, [128, 512], F32) for r_ in range(2)]
            ETP = [rview(73728 + r_ * 1024, [128, 2, 128], F32) for r_ in range(2)]
            ETM = [rview(75776 + r_ * 1024, [128, 2, 128], F32) for r_ in range(2)]
            EH = [rview(77824 + r_ * 1024, [128, 256], F32) for r_ in range(2)]
            XG = [rview(79872 + r_ * 2048, [128, 512], F32) for r_ in range(2)]
            S32 = [[rview(79872 + (d_ * 2 + r_) * 1024, [128, 2, 128], F32) for r_ in range(2)] for d_ in range(2)]
            ATT = [[rview(83968 + (r_ * 2 + d_) * 1024, [128, 4, 128], BF16) for d_ in range(2)] for r_ in range(2)]
            SQ2 = rview(88064, [128, 512], F32)
            ON2 = rview(90112, [128, 512], F32)
            OB2 = [rview(92160 + r_ * 1024, [128, 512], BF16) for r_ in range(2)]
            GLT = rview(94208, [64, T], BF16)
            EB = [rview(96256 + d_ * 128, [128, 2, 16], F32) for d_ in range(2)]
            bGT = [pg["GT%d" % i] for i in range(NT)]
            bVB = [pg["VB%d" % i] for i in range(NT)]
            bQT = [pg["QT0"], pg["QT1"]]
            bKT = [pg["KT0"], pg["KT1"]]
            bKH = [pg["KH0"], pg["KH1"]]
            bS16 = [pg["S160"], pg["S161"]]
            bEB = [pg["EB0"], pg["EB1"]]
            bETP = [pg["ETP0"], pg["ETP1"]]
            bETM = [pg["ETM0"], pg["ETM1"]]
            bEH = [pg["EH0"], pg["EH1"]]
            bXG = [pg["XG0"], pg["XG1"]]
            bRBT = [pg["RBT0"], pg["RBT1"]]
            bOB = [pg["OB0"], pg["OB1"]]
            bATT = [[pg["ATT%d%d" % (r_, d_)] for d_ in range(2)] for r_ in range(2)]
            bS32 = [[[[pg["S32_%d%d%d%d" % (d_, r_, pr, hh)] for hh in range(2)] for pr in range(2)] for r_ in range(2)] for d_ in range(2)]
            for b_ in bQT + bKT + bKH + bS16 + bEB + [pg["RBG"]]:
                b_.same_ok = True

            slot, wv = ws_load([(0, 16, wcols(w_in_d, 3072, 16)), (32, 16, wcols(w_in_d, 3088, 16))])
            for half in range(2):
                bank = proj_fm(wv, slot, 0, 64, half)
                S.op("act", lambda e, bank=bank, half=half: e.copy(GLT[:, half * 512:(half + 1) * 512], G[0:64, bank, :]),
                     reads=[bG[bank]], writes=[pg["GLT"]])
            for i in range(NT):
                r_ = i % 2
                gb0 = 4 if r_ == 0 else 2
                for d_ in range(2):
                    S.op("pe", lambda e, i=i, d_=d_, gb0=gb0: e.matmul(
                        G[:, gb0 + d_, 0:256], GLT[32 * d_:32 * d_ + 16, i * 128:(i + 1) * 128],
                        WG[32 * d_:32 * d_ + 16, :], start=True, stop=True), reads=[pg["GLT"], bC], writes=[bG[gb0 + d_]])
                S.op("dve", lambda e, r_=r_, gb0=gb0: e.tensor_tensor(out=XG[r_].rearrange("p (a b) -> p a b", a=2),
                                                                      in0=G[:, gb0:gb0 + 2, 0:256], in1=BG[:], op=ALU.add),
                     reads=[bG[gb0], bG[gb0 + 1], bC], writes=[bXG[r_]])
                S.op("act", lambda e, r_=r_: e.activation(out=XG[r_], in_=XG[r_], func=AF.Exp, scale=-1.0), reads=[bXG[r_]], writes=[bXG[r_]])
                S.op("act", lambda e, i=i, r_=r_: e.activation(out=GT[:, i, :], in_=XG[r_], func=AF.Ln, bias=EPSC[:, 1:2]),
                     reads=[bXG[r_], bC], writes=[bGT[i]])
            slot, wv = ws_load([(0, 512, wcols(w_in_d, 2048, 512))])
            for i in range(NT):
                bank = proj_tm(wv, slot, 0, 512, i)
                S.op("act", lambda e, bank=bank, i=i: e.copy(VB[:, i, :], G[:, bank, :]), reads=[bG[bank]], writes=[bVB[i]])
            slot, wv = ws_load([(0, 512, wcols(w_in_d, 2560, 512))])
            for i in range(NT):
                bank = proj_tm(wv, slot, 0, 512, i)
                r_ = i % 2
                S.op("act", lambda e, bank=bank, r_=r_: e.activation(out=RBT[r_], in_=G[:, bank, :], func=AF.Silu), reads=[bG[bank]], writes=[bRBT[r_]])
                S.op("dve", lambda e, i=i, r_=r_: e.tensor_tensor(out=RBG[:, i, :], in0=RBT[r_], in1=GG[:].rearrange("p h d -> p (h d)"), op=ALU.mult),
                     reads=[bRBT[r_], bC], writes=[pg["RBG"]])
            slot, wv = ws_load([(0, 512, wcols(w_in_d, 1536, 512))])
            pi = 0
            for half in range(2):
                for c_ in range(4):
                    for kc in range(8):
                        S.op("pe", lambda e, c_=c_, kc=kc, half=half, wv=wv: e.matmul(
                            G[:, c_, :], wv[:, kc, c_ * 128:(c_ + 1) * 128], HT[:, kc, half * 512:(half + 1) * 512],
                            start=(kc == 0), stop=(kc == 7)), reads=[bWS[slot]] + bHT[half * 4:half * 4 + 4], writes=[bG[c_]])
                for ti in range(4):
                    i = half * 4 + ti
                    kb_bank = 4 + (i % 2)
                    for kc in range(8):
                        S.op("pe", lambda e, kc=kc, i=i, wv=wv, kb_bank=kb_bank: e.matmul(
                            G[:, kb_bank, 0:256], HT[:, kc, i * 128:(i + 1) * 128], wv[:, kc, 256:512], start=(kc == 0), stop=(kc == 7)),
                            reads=[bWS[slot], bHT[i]], writes=[bG[kb_bank]])
                    for d_ in range(2):
                        r_ = pi % 2
                        cb = 6 + r_
                        pi += 1
                        for pr in range(2):
                            S.op("pe", lambda e, i=i, d_=d_, pr=pr, cb=cb: e.matmul(
                                G[:, cb, pr * 128:(pr + 1) * 128], GT[:, i, d_ * 256 + pr * 128:d_ * 256 + (pr + 1) * 128],
                                MASKS[:, 2 + d_, :], start=True, stop=True), reads=[bGT[i], bC], writes=[bG[cb]])
                        S.op("pe", lambda e, i=i, d_=d_, cb=cb: e.matmul(
                            G[:, cb, 256:512], MASKS[:, 4 + d_, :], GT[:, i, d_ * 256:(d_ + 1) * 256], start=True, stop=True),
                            reads=[bGT[i], bC], writes=[bG[cb]])
                        S.op("act", lambda e, r_=r_, cb=cb: e.activation(out=ETP[r_].rearrange("p a b -> p (a b)"), in_=G[:, cb, 0:256], func=AF.Exp),
                             reads=[bG[cb]], writes=[bETP[r_]])
                        S.op("act", lambda e, r_=r_, cb=cb: e.activation(out=ETM[r_].rearrange("p a b -> p (a b)"), in_=G[:, cb, 0:256], func=AF.Exp, scale=-1.0),
                             reads=[bG[cb]], writes=[bETM[r_]])
                        S.op("act", lambda e, r_=r_, cb=cb: e.activation(out=EH[r_], in_=G[:, cb, 256:512], func=AF.Exp), reads=[bG[cb]], writes=[bEH[r_]])
                        S.op("dve", lambda e, d_=d_, i=i, ti=ti, r_=r_: e.scalar_tensor_tensor(
                            out=QTt[d_][:, :, i * 128:(i + 1) * 128], in0=G[:, 0:2, ti * 128:(ti + 1) * 128], scalar=0.125,
                            in1=ETP[r_], op0=ALU.mult, op1=ALU.mult), reads=[bG[0], bG[1], bETP[r_]], writes=[bQT[d_]])
                        S.op("dve", lambda e, d_=d_, i=i, ti=ti, r_=r_: e.tensor_tensor(
                            out=KTt[d_][:, :, i * 128:(i + 1) * 128], in0=G[:, 2:4, ti * 128:(ti + 1) * 128], in1=ETM[r_], op=ALU.mult),
                            reads=[bG[2], bG[3], bETM[r_]], writes=[bKT[d_]])
                        S.op("dve", lambda e, d_=d_, i=i, r_=r_, kb_bank=kb_bank: e.tensor_tensor(
                            out=KH[d_][:, i, :], in0=G[:, kb_bank, 0:256], in1=EH[r_], op=ALU.mult),
                            reads=[bG[kb_bank], bEH[r_]], writes=[bKH[d_]])
                        col = 63 if d_ == 0 else 0
                        S.op("dve", lambda e, d_=d_, i=i, col=col, r_=r_: e.tensor_copy(
                            EB[d_][:, :, 2 * i:2 * i + 2], ETP[r_].rearrange("p a (c t) -> p a c t", c=2)[:, :, :, col]),
                            reads=[bETP[r_]], writes=[bEB[d_]])
            stage(5 + 10 * sg)
            for d_ in range(2):
                bS16[d_].inherit(bRBT)
                for r_ in range(2):
                    for pr in range(2):
                        for hh in range(2):
                            bS32[d_][r_][pr][hh].inherit(bXG)
            nseq = 1 if is_sample else 4
            cps = 16 // nseq
            kv_i = 0
            for k_ in range(16):
                for d_ in range(2):
                    c_ = k_ if d_ == 0 else 15 - k_
                    cur, nxt = k_ % 2, (k_ + 1) % 2
                    seq = c_ // cps
                    first = (c_ % cps == 0) if d_ == 0 else (c_ % cps == cps - 1)
                    last = (c_ % cps == cps - 1) if d_ == 0 else (c_ % cps == 0)
                    cur_b = [bS32[d_][cur][pr][0] for pr in range(2)]
                    nxt_b = [bS32[d_][nxt][pr][0] for pr in range(2)]
                    if first:
                        if is_sample:
                            S.dma("sp", lambda e, d_=d_, cur=cur: e.dma_start(
                                out=S32[d_][cur], in_=state_d[d_].rearrange("(pr hh) k v -> (hh k) pr v", hh=2)), writes=cur_b)
                        else:
                            S.op("dve", lambda e, d_=d_, cur=cur: e.memset(S32[d_][cur], 0.0), writes=cur_b)
                    S.op("act", lambda e, d_=d_, c_=c_, cur=cur: e.copy(S16[d_][:, :, c_, :], S32[d_][cur]), reads=cur_b, writes=[bS16[d_]])
                    i, ch = c_ // 2, c_ % 2
                    bank = kv_i % 4
                    kv_i += 1
                    for pr in range(2):
                        for hh in range(2):
                            h = 2 * pr + hh
                            S.op("pe", lambda e, d_=d_, i=i, ch=ch, pr=pr, hh=hh, h=h, bank=bank: e.matmul(
                                G[hh * 64:(hh + 1) * 64, bank, pr * 128:(pr + 1) * 128],
                                KH[d_][ch * 64:(ch + 1) * 64, i, h * 64:(h + 1) * 64],
                                VB[ch * 64:(ch + 1) * 64, i, h * 128:(h + 1) * 128], start=True, stop=True),
                                reads=[bKH[d_], bVB[i]], writes=[bG[bank]])
                    for pr in range(2):
                        S.op("dve", lambda e, d_=d_, pr=pr, c_=c_, bank=bank, cur=cur, nxt=nxt: e.scalar_tensor_tensor(
                            out=S32[d_][nxt][:, pr, :], in0=S32[d_][cur][:, pr, :], scalar=EB[d_][:, pr, c_:c_ + 1],
                            in1=G[:, bank, pr * 128:(pr + 1) * 128], op0=ALU.mult, op1=ALU.add),
                            reads=[bS32[d_][cur][pr][0], bEB[d_], bG[bank]], writes=[bS32[d_][nxt][pr][0]])
                    if last and not is_sample:
                        S.dma("sp", lambda e, d_=d_, seq=seq, nxt=nxt: e.dma_start(
                            out=news_d[d_][seq].rearrange("(pr hh) k v -> (hh k) pr v", hh=2), in_=S32[d_][nxt]), reads=nxt_b)
            def gla_AT(i):
                for d_ in range(2):
                    for h in (0, 2, 1, 3):
                        pr, hh = h // 2, h % 2
                        S.op("pe", lambda e, d_=d_, pr=pr, hh=hh: e.matmul(
                            G[:, 2 + hh, (d_ * 2 + pr) * 128:(d_ * 2 + pr + 1) * 128],
                            KTt[d_][hh * 64:(hh + 1) * 64, pr, i * 128:(i + 1) * 128],
                            QTt[d_][hh * 64:(hh + 1) * 64, pr, i * 128:(i + 1) * 128], start=True, stop=True),
                            reads=[bKT[d_], bQT[d_]], writes=[bG[2 + hh]])
                r_ = i % 2
                for d_ in range(2):
                    for hh in range(2):
                        S.op("dve", lambda e, d_=d_, hh=hh: e.tensor_tensor(
                            out=ATT[r_][d_].rearrange("p (pr hh) t -> p pr hh t", hh=2)[:, :, hh, :],
                            in0=G[:, 2 + hh, d_ * 256:(d_ + 1) * 256].rearrange("p (pr t) -> p pr t", pr=2),
                            in1=MASKS[:, d_, :].unsqueeze(1).to_broadcast([128, 2, 128]), op=ALU.mult),
                            reads=[bG[2 + hh], bC], writes=[bATT[r_][d_]])

            def gla_O(i):
                r_ = i % 2
                ob = 4 + r_
                for h in range(4):
                    pr, hh = h // 2, h % 2
                    for ch in range(2):
                        c_ = 2 * i + ch
                        outp = G[ch * 64:(ch + 1) * 64, ob, h * 128:(h + 1) * 128]
                        tcols = slice(i * 128 + ch * 64, i * 128 + (ch + 1) * 64)
                        for d_ in range(2):
                            S.op("pe", lambda e, d_=d_, h=h, ch=ch, outp=outp: e.matmul(
                                outp, ATT[r_][d_][:, h, ch * 64:(ch + 1) * 64], VB[:, i, h * 128:(h + 1) * 128],
                                start=(d_ == 0), stop=False), reads=[bATT[r_][d_], bVB[i]], writes=[bG[ob]])
                        for d_ in range(2):
                            S.op("pe", lambda e, d_=d_, pr=pr, hh=hh, c_=c_, outp=outp, tcols=tcols: e.matmul(
                                outp, QTt[d_][hh * 64:(hh + 1) * 64, pr, tcols], S16[d_][hh * 64:(hh + 1) * 64, pr, c_, :],
                                start=False, stop=(d_ == 1)), reads=[bQT[d_], bS16[d_]], writes=[bG[ob]])

            def gla_E1(i):
                r_ = i % 2
                ob = 4 + r_
                st_ap, st_b = stat_cols(8)
                S.op("act", lambda e: e.activation(out=SQ2, in_=G[:, ob, :], func=AF.Square), reads=[bG[ob]], writes=[pg["SQ"]])
                S.op("dve", lambda e: e.reduce_sum(out=st_ap[:, 0:4], in_=SQ2.rearrange("p (h d) -> p h d", h=4), axis=AX.X),
                     reads=[pg["SQ"]], writes=[st_b])
                rstd_from_ss(st_ap[:, 0:4], st_ap[:, 4:8], 128.0, [st_b])
                S.op("dve", lambda e: e.tensor_tensor(
                    out=ON2.rearrange("p (h d) -> p h d", h=4), in0=G[:, ob, :].rearrange("p (h d) -> p h d", h=4),
                    in1=st_ap[:, 4:8].unsqueeze(2).to_broadcast([128, 4, 128]), op=ALU.mult),
                    reads=[bG[ob], st_b], writes=[pg["ON"]])
                S.op("dve", lambda e: e.tensor_tensor(out=OB2[r_], in0=ON2, in1=RBG[:, i, :], op=ALU.mult),
                     reads=[pg["ON"], pg["RBG"]], writes=[bOB[r_]])

            def gla_TR(i):
                r_ = i % 2
                ptv2 = PTR[:, r_, 0:512].rearrange("p (h c) -> p h c", h=4)
                for h in range(4):
                    S.op("pe", lambda e, h=h: e.transpose(ptv2[:, h, :], OB2[r_][:, h * 128:(h + 1) * 128], IDB[:]),
                         reads=[bOB[r_], bC], writes=[bPTR[r_]])
                S.op("act", lambda e: e.copy(OT[:, 4:8, i * 128:(i + 1) * 128], ptv2), reads=[bPTR[r_]], writes=[bOT[i]])

            stage(6 + 10 * sg)
            dead = bGT + bKH + bETP + bETM + bEH + bXG + [pg["GLT"]] + \
                [bS32[d_][r_][pr][0] for d_ in range(2) for r_ in range(2) for pr in range(2)]
            x1_offs = [0, 4096, 8192, 12288, 32768, 36864, 73728, 77824]
            X1t = [rview(o_, [128, D], F32) for o_ in x1_offs]
            bX1 = [Buf("X1_%d" % i).inherit(dead) for i in range(NT)]
            if sg == 0:
                htf = HT[:].rearrange("p k t -> p (k t)").bitcast(F32).rearrange("p (a b) -> p a b", a=4)
                bTB = [Buf("TB0").inherit(bHT), Buf("TB1").inherit(bHT)]
                fin_tasks = adaln_finish_tasks([htf[:, 0, :], htf[:, 1, :]], bTB)
            else:
                fin_tasks = []

            def post_residual(ps_banks, ps_ap, base_ap, base_bufs, gate_i, out_ap, out_bufs, junk_slot):
                st_ap, st_b = stat_cols(2)
                S.op("act", lambda e: e.activation(out=XN[:, junk_slot, :], in_=ps_ap, func=AF.Square, accum_out=st_ap[:, 0:1]),
                     reads=ps_banks, writes=[bXN[junk_slot], st_b])
                rstd_from_ss(st_ap[:, 0:1], st_ap[:, 1:2], float(D), [st_b])
                S.op("dve", lambda e: e.scalar_tensor_tensor(out=out_ap, in0=ps_ap, scalar=st_ap[:, 1:2], in1=GB[:, ty, gate_i, :],
                                                             op0=ALU.mult, op1=ALU.mult), reads=ps_banks + [st_b, bGB], writes=out_bufs)
                S.op("dve", lambda e: e.tensor_tensor(out=out_ap, in0=out_ap, in1=base_ap, op=ALU.add),
                     reads=out_bufs + base_bufs, writes=out_bufs)

            slot_a, wva = ws_load([(0, 512, wcols(w_out_d, 0, 512))])
            slot_b, wvb = ws_load([(0, 512, wcols(w_out_d, 512, 512))])
            st2 = {}

            def p4_M(i):
                for half, (sl, wv_) in enumerate(((slot_a, wva), (slot_b, wvb))):
                    for kc in range(8):
                        S.op("pe", lambda e, kc=kc, half=half, wv_=wv_: e.matmul(
                            G[:, half, :], OT[:, kc, i * 128:(i + 1) * 128], wv_[:, kc, :], start=(kc == 0), stop=(kc == 7)),
                            reads=[bWS[sl], bOT[i]], writes=[bG[half]])
                xs = i % 2
                S.dma("sp", lambda e: e.dma_start(out=XT[:, xs, :], in_=x_d[row0 + i * 128: row0 + (i + 1) * 128, :]),
                      writes=[bXT[xs]])

            def p4_A2(i):
                st2[i] = norm_stats(X1t[i], [bX1[i]], i % 2)
                S.op("dve", lambda e: e.tensor_scalar_mul(XN[:, i % 2, :], X1t[i], st2[i][0][:, 1:2]),
                     reads=[bX1[i], st2[i][1]], writes=[bXN[i % 2]])

            def p4_TR(i):
                pslot = i % 2
                ptv = PTR[:, pslot, :].rearrange("p (k c) -> p k c", k=8)
                for kc in range(8):
                    S.op("pe", lambda e, kc=kc: e.transpose(ptv[:, kc, :], XN[:, i % 2, kc * 128:(kc + 1) * 128], IDB[:]),
                         reads=[bXN[i % 2], bC], writes=[bPTR[pslot]])

            def p4_D1(i):
                xs = i % 2
                post_residual([bG[0], bG[1]], G[:, 0:2, :].rearrange("p a b -> p (a b)"), XT[:, xs, :], [bXT[xs]], 0,
                              X1t[i], [bX1[i]], 2)

            OFF = DBG.get("off", 4)
            for t in range(NT + OFF + 4):
                s_ = t - OFF
                if 0 <= s_ - 3 < NT:
                    norm_evac(s_ - 3, ty, 1, eng="mix")
                if fin_tasks:
                    fin_tasks.pop(0)()
                    if not fin_tasks:
                        for b_ in bHT:
                            b_.inherit(bTB)
                if 0 <= s_ < NT:
                    p4_M(s_)
                if t < NT:
                    gla_AT(t)
                if 1 <= t <= NT:
                    gla_O(t - 1)
                if 0 <= s_ - 1 < NT:
                    p4_A2(s_ - 1)
                if 2 <= t <= NT + 1:
                    gla_E1(t - 2)
                if 0 <= s_ < NT:
                    p4_D1(s_)
                if 3 <= t <= NT + 2:
                    gla_TR(t - 3)
                if 0 <= s_ - 2 < NT:
                    p4_TR(s_ - 2)

            pm = new_phase(["FT%d" % c for c in range(8)] + ["F1_%d" % j for j in range(8)] + ["RL0", "RL1", "RL2"])
            R_live = R_live + bX1
            F1 = rview(16384, [128, 8, T], BF16)
            FT = rview(40960, [128, 8, T], F32)
            RL = rview(81920, [128, 3, 512], F32)
            bFT = [pm["FT%d" % c] for c in range(8)]
            bF1 = [pm["F1_%d" % j] for j in range(8)]
            bRL = [pm["RL0"], pm["RL1"], pm["RL2"]]

            stage(7 + 10 * sg)
            bFTh = [[Buf("FT%d_%d" % (c, hf)).inherit([bFT[c]]) for hf in range(2)] for c in range(8)]
            R_live = R_live + [b_ for pair in bFTh for b_ in pair]

            def phase6_tile(i):
                b0 = 4 + (i % 2) * 2
                hf = i // 4
                for c_ in range(8):
                    bank = b0 + c_ // 4
                    S.op("pe", lambda e, c_=c_, bank=bank: e.transpose(
                        G[:, bank, (c_ % 4) * 128:(c_ % 4 + 1) * 128], FT[:, c_, i * 128:(i + 1) * 128], IDF[:]),
                        reads=[bFTh[c_][hf], bC], writes=[bG[bank]])
                xs = i % 2
                post_residual([bG[b0], bG[b0 + 1]], G[:, b0:b0 + 2, :].rearrange("p a b -> p (a b)"), X1t[i], [bX1[i]], 1,
                              XT[:, xs, :], [bXT[xs]], 2)
                S.dma("sp", lambda e: e.dma_start(out=y_d[row0 + i * 128: row0 + (i + 1) * 128, :], in_=XT[:, xs, :]),
                      reads=[bXT[xs]])

            rl_i = 0
            for q in range(4):
                for blk in range(2):
                    slot, wv = ws_load([(0, 512, wcols(w1_d, q * 1024 + blk * 512, 512))])
                    for j in range(4):
                        jj = blk * 4 + j
                        for half in range(2):
                            bank = proj_fm(wv, slot, j * 128, 128, half)
                            rs = rl_i % 3
                            rl_i += 1
                            S.op("act", lambda e, bank=bank, rs=rs: e.activation(out=RL[:, rs, :], in_=G[:, bank, :], func=AF.Relu),
                                 reads=[bG[bank]], writes=[bRL[rs]])
                            S.op("dve", lambda e, rs=rs, jj=jj, half=half: e.tensor_tensor(
                                out=F1[:, jj, half * 512:(half + 1) * 512], in0=RL[:, rs, :], in1=RL[:, rs, :], op=ALU.mult),
                                reads=[bRL[rs]], writes=[bF1[jj]])
                w2b = [ws_load([(0, 512, wrows(w2_d, q * 1024, blk * 512, 512))]) for blk in range(2)]
                order = [(blk, c4, half) for blk in range(2) for c4 in range(4) for half in range(2)] if q < 3 else \
                        [(blk, c4, half) for half in range(2) for blk in range(2) for c4 in range(4)]
                for gi_, (blk, c4, half) in enumerate(order):
                    slot, wv = w2b[blk]
                    c_ = blk * 4 + c4
                    bank = (pp_i[0]) % 4
                    pp_i[0] += 1
                    for hc in range(8):
                        S.op("pe", lambda e, hc=hc, c4=c4, half=half, bank=bank, wv=wv: e.matmul(
                            G[:, bank, :], wv[:, hc, c4 * 128:(c4 + 1) * 128], F1[:, hc, half * 512:(half + 1) * 512],
                            start=(hc == 0), stop=(hc == 7)), reads=[bWS[slot], bF1[hc]], writes=[bG[bank]])
                    dst = FT[:, c_, half * 512:(half + 1) * 512]
                    if q == 0:
                        S.op("act", lambda e, bank=bank, dst=dst: e.copy(dst, G[:, bank, :]), reads=[bG[bank]], writes=[bFTh[c_][half]])
                    else:
                        S.op("dve", lambda e, bank=bank, dst=dst: e.tensor_tensor(out=dst, in0=dst, in1=G[:, bank, :], op=ALU.add),
                             reads=[bG[bank], bFTh[c_][half]], writes=[bFTh[c_][half]])
                    if q == 3 and half == 1 and gi_ % 2 == 1:
                        phase6_tile((gi_ - 8) // 2)
            stage(8 + 10 * sg)
            for i in range(4, NT):
                phase6_tile(i)

        stage(1)
        run_sg(0)
        stage(9)
        run_sg(1)
        S.stopped = False
        S.finish("sp")
        S.emit(block)
    return nc


_NC_CACHE = {}


def kernel(x_prompt, x_sample, c, cache_k, cache_v, state_fwd, state_bwd, c_ctx,
           w_ada, b_ada, norm_attn_pre, norm_attn_post, norm_mlp_pre, norm_mlp_post,
           w_in, w_gate_fwd, b_gate_fwd, w_gate_bwd, b_gate_bwd,
           lam_q1, lam_k1, lam_q2, lam_k2, diff_norm, gla_norm, w_out, w_mlp1, w_mlp2):
    f = lambda a: np.ascontiguousarray(np.asarray(a, dtype=np.float32))
    x_prompt, x_sample = f(x_prompt), f(x_sample)
    consts = _host_consts()
    shared = {
        "w_ada": f(w_ada)[0], "b_ada": f(b_ada)[0],
        "norm_attn_pre": f(norm_attn_pre)[0], "norm_attn_post": f(norm_attn_post)[0],
        "norm_mlp_pre": f(norm_mlp_pre)[0], "norm_mlp_post": f(norm_mlp_post)[0],
        "w_in": f(w_in)[0], "w_gate_fwd": f(w_gate_fwd)[0], "w_gate_bwd": f(w_gate_bwd)[0],
        "b_gate_fwd": f(b_gate_fwd)[0], "b_gate_bwd": f(b_gate_bwd)[0],
        "lam_q1": f(lam_q1)[0], "lam_k1": f(lam_k1)[0], "lam_q2": f(lam_q2)[0], "lam_k2": f(lam_k2)[0],
        "diff_norm": f(diff_norm)[0], "gla_norm": f(gla_norm)[0],
        "w_out": f(w_out)[0], "w_mlp1": f(w_mlp1)[0], "w_mlp2": f(w_mlp2)[0],
    }
    shared.update(consts)
    in_maps = []
    for i in range(N_CORES):
        m = dict(shared)
        m["x"] = np.concatenate([x_sample[i], x_prompt[4 * i:4 * i + 4].reshape(1024, D)], axis=0)
        m["cvec"] = np.stack([f(c)[i], f(c_ctx)], axis=0)
        m["cache_k"] = f(cache_k)[i, 0]
        m["cache_v"] = f(cache_v)[i, 0]
        m["state_f"] = f(state_fwd)[i, 0]
        m["state_b"] = f(state_bwd)[i, 0]
        in_maps.append(m)
    if "nc" not in _NC_CACHE:
        _NC_CACHE["nc"] = build_nc()
    nc = _NC_CACHE["nc"]
    res = run_bass_kernel_spmd(nc, in_maps, core_ids=list(range(N_CORES)))
    outs = res.results
    y_sample = np.stack([outs[i]["y"][0:T] for i in range(N_CORES)], axis=0)
    y_prompt = np.concatenate([outs[i]["y"][T:2 * T].reshape(4, 256, D) for i in range(N_CORES)], axis=0)
    new_k = np.concatenate([outs[i]["new_k"] for i in range(N_CORES)], axis=0)[:, None]
    new_v = np.concatenate([outs[i]["new_v"] for i in range(N_CORES)], axis=0)[:, None]
    new_sf = np.concatenate([outs[i]["new_sf"] for i in range(N_CORES)], axis=0)[:, None]
    new_sb = np.concatenate([outs[i]["new_sb"] for i in range(N_CORES)], axis=0)[:, None]
    return (y_prompt.astype(np.float32), y_sample.astype(np.float32), new_k.astype(np.float32),
            new_v.astype(np.float32), new_sf.astype(np.float32), new_sb.astype(np.float32))
```

```python
import math
from contextlib import ExitStack

import numpy as np
import concourse.bass as bass
import concourse.mybir as mybir
from concourse.bass_utils import run_bass_kernel_spmd

F32 = mybir.dt.float32
BF16 = mybir.dt.bfloat16
AF = mybir.ActivationFunctionType
ALU = mybir.AluOpType
AX = mybir.AxisListType

D = 1024
T = 1024
NT = 8
EPS = 1e-6
LAM_INIT = 0.8 - 0.6 * math.exp(0.0)
N_CORES = 8
STOP = [0]
DBG = {}


class _Stop(Exception):
    pass


class Buf:
    __slots__ = ("name", "w", "r", "excl", "same_ok")

    def __init__(self, name="", excl=False, same_ok=False):
        self.name = name
        self.w = None
        self.r = {}
        self.same_ok = same_ok
        self.excl = excl

    def inherit(self, olds):
        for o in olds:
            if o.w is not None:
                self.r[o.w[0]] = max(self.r.get(o.w[0], 0), o.w[1])
            for k, v in o.r.items():
                self.r[k] = max(self.r.get(k, 0), v)
        return self


class Sched:
    COMPUTE = ("pe", "act", "dve", "pool")

    def __init__(self, nc, stack, n_dma_sems=10):
        self.nc = nc
        self.items = {e: [] for e in ("pe", "act", "dve", "pool", "sp")}
        self.sems = {}
        for e in self.COMPUTE:
            self.sems[e] = stack.enter_context(nc.semaphore("s_" + e))
        self.cnt = {e: 0 for e in self.COMPUTE}
        self.dq = {}
        for q in ("sp", "pool"):
            lst = []
            for i in range(n_dma_sems):
                key = "d_%s_%d" % (q, i)
                self.sems[key] = stack.enter_context(nc.semaphore(key))
                lst.append(key)
            self.dq[q] = {"keys": lst, "n": 0}
        self.known = {e: {} for e in self.items}
        self.stopped = False

    def _deps(self, eng, reads, writes):
        deps = {}

        def add(ev, b, is_write):
            if ev is None:
                return
            k, v = ev
            if k == eng and (eng == "pe" or (is_write and b.same_ok)):
                return
            if deps.get(k, 0) < v:
                deps[k] = v
        for b in reads:
            add(b.w, b, False)
            if b.excl:
                for k, v in b.r.items():
                    if k != eng:
                        add((k, v), b, False)
        for b in writes:
            add(b.w, b, True)
            for k, v in b.r.items():
                add((k, v), b, True)
        out = []
        kn = self.known[eng]
        for k, v in deps.items():
            if kn.get(k, 0) < v:
                kn[k] = v
                out.append((k, v))
        return out

    def _mark(self, ev, reads, writes):
        k, v = ev
        for b in reads:
            if b.r.get(k, 0) < v:
                b.r[k] = v
        for b in writes:
            b.w = ev
            b.r = {}

    def op(self, eng, fn, reads=(), writes=()):
        if self.stopped:
            return None
        waits = self._deps(eng, reads, writes)
        self.cnt[eng] += 1
        ev = (eng, self.cnt[eng])
        self.items[eng].append((waits, fn, (eng, 1)))
        self._mark(ev, reads, writes)
        return ev

    def dma(self, q, fn, reads=(), writes=()):
        if self.stopped:
            return None
        d = self.dq[q]
        i = d["n"]
        d["n"] += 1
        nk = len(d["keys"])
        key = d["keys"][i % nk]
        val = 16 * (i // nk + 1)
        waits = self._deps(q, reads, writes)
        if i >= nk and self.known[q].get(key, 0) < val - 16:
            self.known[q][key] = val - 16
            waits.append((key, val - 16))
        ev = (key, val)
        self.items[q].append((waits, fn, (key, 16)))
        self._mark(ev, reads, writes)
        return ev

    def finish(self, eng="sp"):
        waits = []
        for e in self.COMPUTE:
            if self.cnt[e] > 0:
                waits.append((e, self.cnt[e]))
        for q, d in self.dq.items():
            nk = len(d["keys"])
            for j, key in enumerate(d["keys"]):
                n = (d["n"] - j + nk - 1) // nk if d["n"] > j else 0
                if n > 0:
                    waits.append((key, 16 * n))
        self.items[eng].append((waits, None, None))

    def emit(self, block):
        sems = self.sems
        needed = {e: set() for e in self.COMPUTE}
        for lst in self.items.values():
            for waits, fn, inc in lst:
                for k, v in waits:
                    if k in needed:
                        needed[k].add(v)
        rank = {}
        for e in self.COMPUTE:
            rank[e] = {v: i + 1 for i, v in enumerate(sorted(needed[e]))}

        def run(engobj, lst):
            idx = 0
            for waits, fn, inc in lst:
                ws = [(k, rank[k][v] if k in rank else v) for k, v in waits]
                if fn is None:
                    for k, v in ws:
                        engobj.wait_ge(sems[k], v)
                    continue
                for k, v in ws[:-1]:
                    engobj.wait_ge(sems[k], v)
                ins = fn(engobj)
                if ws:
                    ins._wait_ge(sems[ws[-1][0]], ws[-1][1])
                if inc[0] in rank:
                    idx += 1
                    if idx in rank[inc[0]]:
                        ins.then_inc(sems[inc[0]], 1)
                else:
                    ins.then_inc(sems[inc[0]], inc[1])

        @block.tensor
        def _(e):
            run(e, self.items["pe"])

        @block.scalar
        def _(e):
            run(e, self.items["act"])

        @block.vector
        def _(e):
            run(e, self.items["dve"])

        @block.gpsimd
        def _(e):
            run(e, self.items["pool"])

        @block.sync
        def _(e):
            run(e, self.items["sp"])
        self.stats = {e: (self.cnt[e], len(rank[e])) for e in self.COMPUTE}


def _host_consts():
    c = {}
    c["ident_f"] = np.eye(128, dtype=np.float32)
    p = np.arange(128)
    same = (p[:, None] // 64) == (p[None, :] // 64)
    le = p[:, None] <= p[None, :]
    ge = p[:, None] >= p[None, :]
    lt = p[:, None] < p[None, :]
    gt = p[:, None] > p[None, :]
    sc = np.float32(-1.0 / 16.0)
    masks = np.zeros((6, 128, 128), np.float32)
    masks[0] = (same & le)
    masks[1] = (same & ge)
    masks[2] = (same & le) * sc
    masks[3] = (same & ge) * sc
    masks[4] = (same & gt) * sc
    masks[5] = (same & lt) * sc
    c["masks"] = np.ascontiguousarray(masks.transpose(1, 0, 2))
    d = p % 64
    is_col = (d >= 32)
    dd = d % 32
    fi = dd % 16
    second = dd >= 16
    inv = (np.float32(10000.0) ** (-(np.arange(16, dtype=np.float32)) / np.float32(16.0))).astype(np.float32)
    t = np.arange(T)
    rowpos = (t // 64).astype(np.float32)
    colpos = (t % 64).astype(np.float32)
    pos = np.where(is_col[:, None], colpos[None, :], rowpos[None, :]).astype(np.float32)
    ang = (pos * inv[fi][:, None]).astype(np.float32)
    cos = np.cos(ang).astype(np.float32)
    sin = np.sin(ang).astype(np.float32)
    sg = np.where(second[:, None], sin, -sin).astype(np.float32)
    c["rope"] = np.ascontiguousarray(np.stack([cos, sg], axis=1))
    swap = np.where(second, p - 16, p + 16)
    perm = np.zeros((128, 128), np.float32)
    perm[swap, p] = 1.0
    c["perm"] = perm
    sel = np.zeros((2, 2, 128), np.float32)
    sel[0, 0, :] = 1.0
    sel[1, 1, :] = 1.0
    c["sel"] = sel
    return c


def build_nc(debug=False):
    nc = bass.Bass("TRN2", target_bir_lowering=False)

    def din(name, shape):
        return nc.dram_tensor(name, list(shape), F32, kind="ExternalInput").ap()

    def dout(name, shape):
        return nc.dram_tensor(name, list(shape), F32, kind="ExternalOutput").ap()

    x_d = din("x", [2 * T, D])
    cvec_d = din("cvec", [2, D])
    cache_k_d = din("cache_k", [4, 256, 128])
    cache_v_d = din("cache_v", [4, 256, 128])
    state_d = [din("state_f", [4, 64, 128]), din("state_b", [4, 64, 128])]
    w_ada_d = din("w_ada", [D, 6 * D])
    b_ada_d = din("b_ada", [6 * D])
    npre1_d = din("norm_attn_pre", [D])
    npost1_d = din("norm_attn_post", [D])
    npre2_d = din("norm_mlp_pre", [D])
    npost2_d = din("norm_mlp_post", [D])
    w_in_d = din("w_in", [D, 3104])
    wg_d = [din("w_gate_fwd", [16, 256]), din("w_gate_bwd", [16, 256])]
    bg_d = [din("b_gate_fwd", [256]), din("b_gate_bwd", [256])]
    lam_d = [din("lam_q1", [64]), din("lam_k1", [64]), din("lam_q2", [64]), din("lam_k2", [64])]
    dnorm_d = din("diff_norm", [128])
    gnorm_d = din("gla_norm", [128])
    w_out_d = din("w_out", [D, D])
    w1_d = din("w_mlp1", [D, 4 * D])
    w2_d = din("w_mlp2", [4 * D, D])
    identf_d = din("ident_f", [128, 128])
    masks_d = din("masks", [128, 6, 128])
    rope_d = din("rope", [128, 2, T])
    perm_d = din("perm", [128, 128])
    sel_d = din("sel", [2, 2, 128])

    y_d = dout("y", [2 * T, D])
    newk_d = dout("new_k", [4, 4, 256, 128])
    newv_d = dout("new_v", [4, 4, 256, 128])
    news_d = [dout("new_sf", [4, 4, 64, 128]), dout("new_sb", [4, 4, 64, 128])]

    st = ExitStack()
    with st:
        S = Sched(nc, st)

        def stage(k):
            if STOP[0] == k:
                S.stopped = True

        def sb(name, shape, dt):
            return st.enter_context(nc.sbuf_tensor(name, list(shape), dt))

        IDB = sb("IDB", [128, 128], BF16)
        IDF = sb("IDF", [128, 128], F32)
        MASKS = sb("MASKS", [128, 6, 128], F32)
        PERM = sb("PERM", [128, 128], BF16)
        ROPE = sb("ROPE", [128, 2, T], F32)
        SEL = sb("SEL", [2, 2, 128], F32)
        SC = sb("SC", [128, 8, 2], BF16)
        MODC = sb("MODC", [128, 4, 8, 2], F32)
        AB = sb("AB", [128, 4, 8, 2], F32)
        GB = sb("GB", [128, 2, 2, D], F32)
        DG = sb("DG", [128, 4, 128], F32)
        GG = sb("GG", [128, 4, 128], F32)
        BG = sb("BG", [128, 2, 256], F32)
        WG = sb("WG", [48, 256], BF16)
        LAMT = sb("LAMT", [128, 4, 64], F32)
        LAMS = sb("LAMS", [128, 8], F32)
        SMALL = sb("SMALL", [128, 96], F32)
        XT = sb("XT", [128, 2, D], F32)
        XN = sb("XN", [128, 3, D], BF16)
        HT = sb("HT", [128, 8, T], BF16)
        OT = sb("OT", [128, 8, T], BF16)
        WS = sb("WS", [128, 3, 4096], BF16)
        RBYTES = 96 * 1024
        R = sb("R", [128, RBYTES // 2], BF16)
        G = st.enter_context(nc.psum_tensor("G", [128, 8, 512], F32))
        PTR = G[:, 6:8, :].bitcast(BF16)
        block = st.enter_context(nc.Block())

        def rview(off, shape, dt):
            esz = 2 if dt == BF16 else 4
            n = 1
            for s_ in shape[1:]:
                n *= s_
            assert off % 4 == 0 and off + n * esz <= RBYTES, (off, shape)
            ap = R[0:shape[0], off // 2:(off + n * esz) // 2]
            if dt != BF16:
                ap = ap.bitcast(dt)
            if len(shape) == 3:
                ap = ap.rearrange("p (a b) -> p a b", a=shape[1])
            elif len(shape) == 4:
                ap = ap.rearrange("p (a b c) -> p a b c", a=shape[1], b=shape[2])
            return ap

        bG = [Buf("G%d" % i, excl=True) for i in range(8)]
        bPTR = bG[6:8]
        bWS = [Buf("WS%d" % i) for i in range(3)]
        bXT = [Buf("XT%d" % i) for i in range(2)]
        bXN = [Buf("XN0"), Buf("XN1"), Buf("XNjunk", same_ok=True)]
        bHT = [Buf("HT%d" % i, same_ok=True) for i in range(NT)]
        bOT = [Buf("OT%d" % i, same_ok=True) for i in range(NT)]
        bC = Buf("consts")
        bSM = {}

        def smb(name):
            if name not in bSM:
                bSM[name] = Buf(name)
            return bSM[name]

        R_live = []

        def new_phase(names):
            nonlocal R_live
            out = {}
            for n_ in names:
                out[n_] = Buf(n_).inherit(R_live)
            R_live = list(out.values())
            return out

        small_next = [0]

        def small(n):
            a = small_next[0]
            small_next[0] += n
            assert small_next[0] <= 64
            return SMALL[:, a:a + n]

        ws_n = [0]

        def ws_load(src_list):
            slot = ws_n[0] % 3
            ws_n[0] += 1
            view = WS[:, slot, :].rearrange("p (k c) -> p k c", k=8)
            for c0, ncol, src in src_list:
                S.dma("pool", lambda e, c0=c0, ncol=ncol, src=src, view=view:
                      e.dma_start(out=view[:, :, c0:c0 + ncol], in_=src), writes=[bWS[slot]])
            return slot, view

        def wcols(w_ap, c0, ncol):
            return w_ap.rearrange("(kc p) c -> p kc c", p=128)[:, :, c0:c0 + ncol]

        def wrows(w_ap, r0, c0, ncol):
            return w_ap[r0:r0 + 1024, :].rearrange("(kc p) c -> p kc c", p=128)[:, :, c0:c0 + ncol]

        const_bufs = []

        def cw():
            b_ = Buf("c%d" % len(const_bufs))
            const_bufs.append(b_)
            return b_

        def cr():
            return list(const_bufs)

        S.dma("sp", lambda e: e.dma_start(out=IDF[:], in_=identf_d), writes=[cw()])
        S.dma("pool", lambda e: e.dma_start(out=IDB[:], in_=identf_d), writes=[cw()])
        S.dma("sp", lambda e: e.dma_start(out=MASKS[:], in_=masks_d), writes=[cw()])
        S.dma("pool", lambda e: e.dma_start(out=PERM[:], in_=perm_d), writes=[cw()])
        S.dma("sp", lambda e: e.dma_start(out=ROPE[:], in_=rope_d), writes=[cw()])
        S.dma("sp", lambda e: e.dma_start(out=SEL[:], in_=sel_d), writes=[cw()])
        ROWS = sb("ROWS", [64, 128], F32)
        COLS = sb("COLS", [128, 64], F32)
        S.dma("sp", lambda e: e.dma_start(out=ROWS[0:16, :], in_=cvec_d.rearrange("t (k p) -> (t k) p", p=128)), writes=[cw()])
        for mi, m in enumerate((0, 1, 3, 4)):
            S.dma("sp", lambda e, mi=mi, m=m: e.dma_start(
                out=ROWS[16 + 8 * mi:24 + 8 * mi, :], in_=b_ada_d[m * D:(m + 1) * D].rearrange("(j p) -> j p", p=128)), writes=[cw()])
        S.dma("sp", lambda e: e.dma_start(out=ROWS[48:56, :], in_=npre1_d.rearrange("(j p) -> j p", p=128)), writes=[cw()])
        S.dma("sp", lambda e: e.dma_start(out=ROWS[56:64, :], in_=npre2_d.rearrange("(j p) -> j p", p=128)), writes=[cw()])
        S.op("pe", lambda e: e.transpose(G[:, 2, 0:64], ROWS[:], IDF[0:64, 0:64]), reads=cr(), writes=[bG[2]])
        S.op("dve", lambda e: e.tensor_copy(COLS[:], G[:, 2, 0:64]), reads=[bG[2]], writes=[cw()])
        for h in range(4):
            S.dma("sp", lambda e, h=h: e.dma_start(out=DG[:, h, :], in_=dnorm_d.partition_broadcast(128)), writes=[cw()])
            S.dma("sp", lambda e, h=h: e.dma_start(out=GG[:, h, :], in_=gnorm_d.partition_broadcast(128)), writes=[cw()])
            S.dma("sp", lambda e, h=h: e.dma_start(out=LAMT[:, h, :], in_=lam_d[h].partition_broadcast(128)), writes=[cw()])
        for d_ in range(2):
            S.dma("sp", lambda e, d_=d_: e.dma_start(out=BG[:, d_, :], in_=bg_d[d_].partition_broadcast(128)), writes=[cw()])
            S.dma("pool", lambda e, d_=d_: e.dma_start(out=WG[32 * d_:32 * d_ + 16, :], in_=wg_d[d_]), writes=[cw()])
        stage(101)
        S.op("dve", lambda e: e.tensor_scalar_mul(DG[:], DG[:], float(1.0 - LAM_INIT)), reads=cr(), writes=[cw()])
        S.op("dve", lambda e: e.tensor_tensor(out=LAMT[:, 0, :], in0=LAMT[:, 0, :], in1=LAMT[:, 1, :], op=ALU.mult), reads=cr(), writes=[cw()])
        S.op("dve", lambda e: e.tensor_tensor(out=LAMT[:, 2, :], in0=LAMT[:, 2, :], in1=LAMT[:, 3, :], op=ALU.mult), reads=cr(), writes=[cw()])
        S.op("dve", lambda e: e.reduce_sum(out=LAMS[:, 0:1], in_=LAMT[:, 0, :], axis=AX.X), reads=cr(), writes=[cw()])
        S.op("dve", lambda e: e.reduce_sum(out=LAMS[:, 1:2], in_=LAMT[:, 2, :], axis=AX.X), reads=cr(), writes=[cw()])
        S.op("act", lambda e: e.activation(out=LAMS[:, 2:4], in_=LAMS[:, 0:2], func=AF.Exp), reads=cr(), writes=[cw()])
        S.op("dve", lambda e: e.memset(LAMS[:, 4:5], 1.0), reads=cr(), writes=[cw()])
        S.op("dve", lambda e: e.scalar_tensor_tensor(out=LAMS[:, 5:6], in0=LAMS[:, 3:4], scalar=float(-LAM_INIT),
                                                     in1=LAMS[:, 2:3], op0=ALU.add, op1=ALU.subtract), reads=cr(), writes=[cw()])

        stage(102)
        JOIN = sb("JOIN", [128, 2], F32)
        S.op("dve", lambda e: e.memset(JOIN[:], 0.0), reads=cr(), writes=[bC])
        def rstd_from_ss(ss_ap, out_ap, n, bufs):
            S.op("act", lambda e: e.activation(out=out_ap, in_=ss_ap, func=AF.Ln, scale=1.0 / n, bias=EPSC[:, 0:1]),
                 reads=bufs + [bC], writes=bufs)
            S.op("act", lambda e: e.activation(out=out_ap, in_=out_ap, func=AF.Exp, scale=-0.5), reads=bufs, writes=bufs)

        EPSC = sb("EPSC", [128, 2], F32)
        S.op("dve", lambda e: e.memset(EPSC[:, 0:1], EPS), writes=[bC])
        S.op("dve", lambda e: e.memset(EPSC[:, 1:2], 1.0), writes=[bC])

        stat_i = [0]

        def stat_cols(n):
            k = stat_i[0] % 8
            stat_i[0] += 1
            return SMALL[:, k * 8:k * 8 + n], smb("stat%d" % k)

        xp0 = [(rview(32768 + i * 4096, [128, D], F32), Buf("XP%d" % i)) for i in range(NT)]
        XNA = rview(65536, [128, 8, D], BF16)
        bXNA = [Buf("XNA%d" % i) for i in range(NT)]
        R_live = R_live + [b_ for _, b_ in xp0] + bXNA
        for i in range(NT):
            S.dma("sp", lambda e, i=i: e.dma_start(out=xp0[i][0], in_=x_d[i * 128:(i + 1) * 128, :]), writes=[xp0[i][1]])
        for i in range(NT):
            st_ap, st_b = stat_cols(2)
            S.op("act", lambda e, i=i, st_ap=st_ap: e.activation(out=XNA[:, i, :], in_=xp0[i][0], func=AF.Square, accum_out=st_ap[:, 0:1]),
                 reads=[xp0[i][1]], writes=[bXNA[i], st_b])
            rstd_from_ss(st_ap[:, 0:1], st_ap[:, 1:2], float(D), [st_b])
            S.op("dve", lambda e, i=i, st_ap=st_ap: e.tensor_scalar_mul(XNA[:, i, :], xp0[i][0], st_ap[:, 1:2]),
                 reads=[xp0[i][1], st_b], writes=[bXNA[i]])

        CT = COLS[:, 0:16].rearrange("p (t k) -> p k t", t=2)
        BADAC = COLS[:, 16:48].rearrange("p (m j) -> p m j", m=4)
        GPRE = COLS[:, 48:64].rearrange("p (n j) -> p n j", n=2)
        S.op("act", lambda e: e.activation(out=SC[:], in_=CT, func=AF.Silu), reads=[bC], writes=[bC])
        GROWX = XT[0:2, :, :]
        PMC = G[:, 0, 0:64].rearrange("p (a b c) -> p a b c", a=4, b=8)
        col_mods = {0: 0, 1: 1, 3: 2, 4: 3}
        bAB = [Buf("AB0"), Buf("AB1")]

        def adaln_block(m, half, cb=0, rb=1):
            slot, wv = ws_load([(0, 512, wcols(w_ada_d, m * D + half * 512, 512))])
            pmc = G[:, cb, 0:64].rearrange("p (a b c) -> p a b c", a=4, b=8)
            if m in col_mods:
                mi = col_mods[m]
                for j in range(4):
                    jj = half * 4 + j
                    for kc in range(8):
                        S.op("pe", lambda e, jj=jj, kc=kc, j=j: e.matmul(
                            pmc[:, mi, jj, :], wv[:, kc, j * 128:(j + 1) * 128], SC[:, kc, :],
                            start=(kc == 0), stop=(kc == 7)), reads=[bWS[slot], bC], writes=[bG[cb]])
                S.op("dve", lambda e: e.tensor_copy(MODC[:, mi, half * 4:half * 4 + 4, :], pmc[:, mi, half * 4:half * 4 + 4, :]),
                     reads=[bG[cb]], writes=[bAB[mi // 2]])
            else:
                gi = 0 if m == 2 else 1
                for kc in range(8):
                    S.op("pe", lambda e, kc=kc: e.matmul(
                        G[0:2, rb, :], SC[:, kc, :], wv[:, kc, :], start=(kc == 0), stop=(kc == 7)),
                        reads=[bWS[slot], bC], writes=[bG[rb]])
                S.op("dve", lambda e: e.tensor_copy(GROWX[:, gi, half * 512:(half + 1) * 512], G[0:2, rb, :]),
                     reads=[bG[rb]], writes=[bXT[0], bXT[1]])

        ada_tasks = [(m, half) for m in (3, 4, 2, 5) for half in range(2)]

        def ada_step(cb=0, rb=1):
            if ada_tasks:
                adaln_block(*ada_tasks.pop(0), cb=cb, rb=rb)


        def adaln_cols(n_):
            S.op("dve", lambda e: e.tensor_tensor(out=MODC[:, 2 * n_:2 * n_ + 2], in0=MODC[:, 2 * n_:2 * n_ + 2],
                                                  in1=BADAC[:, 2 * n_:2 * n_ + 2, :].unsqueeze(3).to_broadcast([128, 2, 8, 2]), op=ALU.add),
                 reads=[bAB[n_], bC], writes=[bAB[n_]])
            S.op("dve", lambda e: e.scalar_tensor_tensor(
                out=AB[:, 2 * n_, :, :], in0=MODC[:, 2 * n_ + 1, :, :], scalar=1.0,
                in1=GPRE[:, n_, :].unsqueeze(2).to_broadcast([128, 8, 2]), op0=ALU.add, op1=ALU.mult),
                reads=[bC, bAB[n_]], writes=[bAB[n_]])
            S.op("dve", lambda e: e.tensor_copy(AB[:, 2 * n_ + 1, :, :], MODC[:, 2 * n_, :, :]), reads=[bAB[n_]], writes=[bAB[n_]])

        for half in range(2):
            adaln_block(0, half)
        for half in range(2):
            adaln_block(1, half)
        adaln_cols(0)

        def adaln_finish_tasks(TB, bTB):
            combos = [(ty, gi, half) for ty in range(2) for gi in range(2) for half in range(2)]

            def bcast(lst):
                for n_, (ty, gi, half) in enumerate(lst):
                    bank = n_ % 2
                    S.op("pe", lambda e, ty=ty, gi=gi, half=half, bank=bank: e.matmul(
                        G[:, bank, :], SEL[:, ty, :], GROWX[:, gi, half * 512:(half + 1) * 512], start=True, stop=True),
                        reads=[bC, bXT[0], bXT[1]], writes=[bG[bank]])
                    S.op("act", lambda e, ty=ty, gi=gi, half=half, bank=bank: e.copy(
                        GB[:, ty, gi, half * 512:(half + 1) * 512], G[:, bank, :]), reads=[bG[bank]], writes=[bGB])

            def loads(gi):
                m, np_d = ((2, npost1_d), (5, npost2_d))[gi]
                S.dma("sp", lambda e: e.dma_start(out=TB[0], in_=b_ada_d[m * D:(m + 1) * D].partition_broadcast(128)), writes=[bTB[0]])
                S.dma("sp", lambda e: e.dma_start(out=TB[1], in_=np_d.partition_broadcast(128)), writes=[bTB[1]])

            def apply(gi):
                for ty in range(2):
                    S.op("dve", lambda e, ty=ty: e.tensor_tensor(out=GB[:, ty, gi, :], in0=GB[:, ty, gi, :], in1=TB[0], op=ALU.add),
                         reads=[bGB, bTB[0]], writes=[bGB])
                    S.op("dve", lambda e, ty=ty: e.tensor_tensor(out=GB[:, ty, gi, :], in0=GB[:, ty, gi, :], in1=TB[1], op=ALU.mult),
                         reads=[bGB, bTB[1]], writes=[bGB])

            def t0():
                while ada_tasks:
                    ada_step()
                adaln_cols(1)
                bcast(combos[0:4])
                loads(0)

            def t1():
                bcast(combos[4:8])

            def t2():
                apply(0)
                loads(1)

            def t3():
                apply(1)
            return [t0, t1, t2, t3]

        bGB = Buf("GB")

        def norm_stats(src_ap, src_bufs, xn_slot):
            st_ap, st_b = stat_cols(2)
            xn = XN[:, xn_slot, :]
            S.op("act", lambda e: e.activation(out=xn, in_=src_ap, func=AF.Square, accum_out=st_ap[:, 0:1]),
                 reads=src_bufs, writes=[bXN[xn_slot], st_b])
            rstd_from_ss(st_ap[:, 0:1], st_ap[:, 1:2], float(D), [st_b])
            return st_ap, st_b

        def norm_xn_T(src_ap, src_bufs, st, tile_i, xn_slot):
            st_ap, st_b = st
            xn = XN[:, xn_slot, :]
            S.op("dve", lambda e: e.tensor_scalar_mul(xn, src_ap, st_ap[:, 1:2]), reads=src_bufs + [st_b], writes=[bXN[xn_slot]])
            pslot = tile_i % 2
            ptv = PTR[:, pslot, :].rearrange("p (k c) -> p k c", k=8)
            for kc in range(8):
                S.op("pe", lambda e, kc=kc: e.transpose(ptv[:, kc, :], xn[:, kc * 128:(kc + 1) * 128], IDB[:]),
                     reads=[bXN[xn_slot], bC], writes=[bPTR[pslot]])

        def norm_evac(tile_i, ty, which, eng="act"):
            pslot = tile_i % 2
            ptv = PTR[:, pslot, :].rearrange("p (k c) -> p k c", k=8)
            for kc in range(8):
                if eng == "act" or (eng == "mix" and kc < 4):
                    S.op("act", lambda e, kc=kc: e.activation(
                        out=HT[:, kc, tile_i * 128:(tile_i + 1) * 128], in_=ptv[:, kc, :], func=AF.Identity,
                        bias=AB[:, 2 * which + 1, kc, ty:ty + 1], scale=AB[:, 2 * which, kc, ty:ty + 1]),
                        reads=[bPTR[pslot], bAB[which]], writes=[bHT[tile_i]])
                else:
                    S.op("dve", lambda e, kc=kc: e.tensor_scalar(
                        HT[:, kc, tile_i * 128:(tile_i + 1) * 128], ptv[:, kc, :],
                        AB[:, 2 * which, kc, ty:ty + 1], AB[:, 2 * which + 1, kc, ty:ty + 1], ALU.mult, ALU.add),
                        reads=[bPTR[pslot], bAB[which]], writes=[bHT[tile_i]])

        pp_i = [0]

        def proj_fm(wv, wslot, col0, M, half, banks=(0, 1, 2, 3)):
            bank = banks[pp_i[0] % len(banks)]
            pp_i[0] += 1
            for kc in range(8):
                S.op("pe", lambda e, kc=kc: e.matmul(G[0:M, bank, :], wv[:, kc, col0:col0 + M],
                                                     HT[:, kc, half * 512:(half + 1) * 512], start=(kc == 0), stop=(kc == 7)),
                     reads=[bWS[wslot]] + bHT[half * 4:half * 4 + 4], writes=[bG[bank]])
            return bank

        def proj_tm(wv, wslot, col0, N, tile_i, banks=(0, 1, 2, 3)):
            bank = banks[pp_i[0] % len(banks)]
            pp_i[0] += 1
            for kc in range(8):
                S.op("pe", lambda e, kc=kc: e.matmul(G[:, bank, 0:N], HT[:, kc, tile_i * 128:(tile_i + 1) * 128],
                                                     wv[:, kc, col0:col0 + N], start=(kc == 0), stop=(kc == 7)),
                     reads=[bWS[wslot], bHT[tile_i]], writes=[bG[bank]])
            return bank

        def run_sg(sg):
            ty = sg
            row0 = sg * T
            is_sample = (sg == 0)
            pre_w = [ws_load([(0, 512, wcols(w_in_d, c0_, 512))]) for c0_ in (0, 512, 1024)]

            nonlocal R_live
            xp = []
            if sg == 0:
                for i in range(NT + 1):
                    if i < NT:
                        pslot = i % 2
                        ptv = PTR[:, pslot, :].rearrange("p (k c) -> p k c", k=8)
                        for kc in range(8):
                            S.op("pe", lambda e, kc=kc, i=i, ptv=ptv: e.transpose(ptv[:, kc, :], XNA[:, i, kc * 128:(kc + 1) * 128], IDB[:]),
                                 reads=[bXNA[i], bC], writes=[bPTR[pslot]])
                    if i >= 1:
                        norm_evac(i - 1, ty, 0, eng="dve")
            else:
                otf = OT[:].rearrange("p k t -> p (k t)").bitcast(F32).rearrange("p (a b) -> p a b", a=4)
                xp_ot = [Buf("XPOT%d" % i).inherit(bOT) for i in range(4)]
                for i in range(4):
                    xp.append((otf[:, i, :], xp_ot[i]))
                for i in range(2):
                    xp.append((rview(88064 + i * 4096, [128, D], F32), Buf("XPR%d" % i).inherit(R_live)))
                R_live = R_live + [xp[4][1], xp[5][1]]
                for i in range(2):
                    xp.append((XT[:, i, :], bXT[i]))
                for i in range(NT):
                    S.dma("sp", lambda e, i=i: e.dma_start(out=xp[i][0], in_=x_d[row0 + i * 128: row0 + (i + 1) * 128, :]),
                          writes=[xp[i][1]])
                for i in range(NT + 1):
                    if i < NT:
                        st_ = norm_stats(xp[i][0], [xp[i][1]], i % 2)
                        norm_xn_T(xp[i][0], [xp[i][1]], st_, i, i % 2)
                    if i >= 1:
                        norm_evac(i - 1, ty, 0, eng="dve")
            if sg == 1:
                for b_ in bOT:
                    b_.inherit(xp_ot)
            stage(2 + 10 * sg)
            pa = new_phase(["QAT0", "QAT1", "QAT2", "QAT3", "KAT0", "KAT1", "KAT2", "KAT3", "VA", "PT0", "PT1", "PT2",
                            "OA00", "OA01", "OA10", "OA11", "ACCS", "OB16b", "CK32", "T1", "T2", "XB16", "NK0", "NK1", "SQ", "ON", "OB16"])
            QAT = rview(0, [128, 4, T], BF16)
            KAT = [rview(8192, [128, 4, 1280], BF16), rview(59392, [128, 4, 1280], BF16)]
            VA = rview(18432, [128, 10, 4, 129], BF16)
            PT = rview(28800, [128, 3, 512], BF16)
            OA4 = rview(32768, [128, 2, 2, 512], F32)
            CK32 = rview(40960, [128, 2, 512], F32)
            T1 = rview(45056, [128, 512], F32)
            T2 = rview(47104, [128, 512], F32)
            XB16 = rview(49152, [128, 512], BF16)
            NK = rview(50176, [128, 2, 512], F32)
            SQ = rview(54272, [128, 512], F32)
            ON = rview(56320, [128, 512], F32)
            OB16 = rview(58368, [128, 512], BF16)
            bQAT = [pa["QAT%d" % h] for h in range(4)]
            bKAT = [pa["KAT%d" % h] for h in range(4)]
            bPT = [pa["PT%d" % h] for h in range(3)]
            bOAq = [[pa["OA%d%d" % (a, b)] for b in range(2)] for a in range(2)]
            bNK = [pa["NK0"], pa["NK1"]]
            KO = 256 if is_sample else 0
            VO = 2 if is_sample else 0

            S.op("dve", lambda e: e.memset(VA[:, :, :, 128:129], 1.0), writes=[pa["VA"]])
            S.op("dve", lambda e: e.memset(KAT[0][64:128, :, :], 0.0), writes=bKAT)
            S.op("dve", lambda e: e.memset(KAT[1][0:64, :, :], 0.0), writes=bKAT)
            if is_sample:
                for kc in range(2):
                    S.dma("sp", lambda e, kc=kc: e.dma_start(
                        out=CK32[:, kc, :].rearrange("p (h d) -> p h d", h=4),
                        in_=cache_k_d[:, kc * 128:(kc + 1) * 128, :].rearrange("h p d -> p h d")), writes=[pa["CK32"]])
                    S.dma("pool", lambda e, kc=kc: e.dma_start(
                        out=VA[:, kc, :, 0:128],
                        in_=cache_v_d[:, kc * 128:(kc + 1) * 128, :].rearrange("h p d -> p h d")), writes=[pa["VA"]])
            def fm_evac(bank, dst_ap, dst_bufs, half):
                if not is_sample or DBG.get('norope'):
                    for (p0, p1, dap) in dst_ap:
                        S.op("act", lambda e, p0=p0, p1=p1, dap=dap: e.copy(dap, G[p0:p1, bank, :]), reads=[bG[bank]], writes=dst_bufs)
                    return
                S.op("act", lambda e: e.copy(XB16, G[:, bank, :]), reads=[bG[bank]], writes=[pa["XB16"]])
                if not DBG.get('nomm'):
                    S.op("pe", lambda e: e.matmul(G[:, 4, :], PERM[:], XB16, start=True, stop=True),
                         reads=[pa["XB16"], bC], writes=[bG[4]])
                if DBG.get('nodve'):
                    return
                S.op("dve", lambda e: e.tensor_tensor(out=T1, in0=G[:, bank, :], in1=ROPE[:, 0, half * 512:(half + 1) * 512], op=ALU.mult),
                     reads=[bG[bank], bC], writes=[pa["T1"]])
                if DBG.get('nodve2'):
                    return
                b4 = bank if DBG.get('nomm') else 4
                S.op("dve", lambda e: e.tensor_tensor(out=T2, in0=G[:, b4, :], in1=ROPE[:, 1, half * 512:(half + 1) * 512], op=ALU.mult),
                     reads=[bG[b4], bC], writes=[pa["T2"]])
                if DBG.get('nodve3'):
                    return
                for (p0, p1, dap) in dst_ap:
                    S.op("dve", lambda e, p0=p0, p1=p1, dap=dap: e.tensor_tensor(out=dap, in0=T1[p0:p1, :], in1=T2[p0:p1, :], op=ALU.add),
                         reads=[pa["T1"], pa["T2"]], writes=dst_bufs)

            stage(22)
            slot, wv = pre_w[0]
            prev = None
            for h in range(4):
                for half in range(2):
                    bank = proj_fm(wv, slot, h * 128, 128, half)
                    if prev is not None:
                        fm_evac(*prev)
                    prev = (bank, [(0, 128, QAT[:, h, half * 512:(half + 1) * 512])], [bQAT[h]], half)
            fm_evac(*prev)
            stage(23)
            slot, wv = pre_w[1]
            prev = None
            for h in range(4):
                for half in range(2):
                    bank = proj_fm(wv, slot, h * 128, 128, half)
                    if prev is not None:
                        fm_evac(*prev)
                    prev = (bank, [(0, 64, KAT[0][0:64, h, KO + half * 512:KO + (half + 1) * 512]),
                                   (64, 128, KAT[1][64:128, h, KO + half * 512:KO + (half + 1) * 512])], [bKAT[h]], half)
            fm_evac(*prev)
            if not is_sample:
                for i in range(NT):
                    bank = proj_tm(wv, slot, 0, 512, i)
                    ns = i % 2
                    S.op("act", lambda e, bank=bank, ns=ns: e.copy(NK[:, ns, :], G[:, bank, :]), reads=[bG[bank]], writes=[bNK[ns]])
                    S.dma("sp", lambda e, i=i, ns=ns: e.dma_start(
                        out=newk_d[i // 2, :, (i % 2) * 128:(i % 2 + 1) * 128, :].rearrange("h t d -> t h d"),
                        in_=NK[:, ns, :].rearrange("p (h d) -> p h d", h=4)), reads=[bNK[ns]])
            stage(24)
            slot, wv = pre_w[2]
            for i in range(NT):
                bank = proj_tm(wv, slot, 0, 512, i)
                S.op("act", lambda e, bank=bank, i=i: e.copy(VA[:, VO + i, :, 0:128], G[:, bank, :].rearrange("p (h d) -> p h d", h=4)),
                     reads=[bG[bank]], writes=[pa["VA"]])
                if not is_sample:
                    ns = i % 2
                    S.op("dve", lambda e, bank=bank, ns=ns: e.tensor_copy(NK[:, ns, :], G[:, bank, :]), reads=[bG[bank]], writes=[bNK[ns]])
                    S.dma("sp", lambda e, i=i, ns=ns: e.dma_start(
                        out=newv_d[i // 2, :, (i % 2) * 128:(i % 2 + 1) * 128, :].rearrange("h t d -> t h d"),
                        in_=NK[:, ns, :].rearrange("p (h d) -> p h d", h=4)), reads=[bNK[ns]])

            stage(3 + 10 * sg)
            if is_sample:
                stage(21)
                for kc in range(2):
                    bank = 4 + kc
                    for h in range(4):
                        S.op("pe", lambda e, kc=kc, h=h, bank=bank: e.transpose(
                            G[:, bank, h * 128:(h + 1) * 128], CK32[:, kc, h * 128:(h + 1) * 128], IDF[:]),
                            reads=[pa["CK32"], bC], writes=[bG[bank]])
                    for s_ in range(2):
                        S.op("act", lambda e, kc=kc, bank=bank, s_=s_: e.copy(
                            KAT[s_][s_ * 64:(s_ + 1) * 64, :, kc * 128:(kc + 1) * 128],
                            G[s_ * 64:(s_ + 1) * 64, bank, :].rearrange("p (h k) -> p h k", h=4)),
                            reads=[bG[bank]], writes=bKAT)

            if is_sample:
                qblocks = [(qb_ * 256, [(kc * 128, kc) for kc in range(10)]) for qb_ in range(4)]
            else:
                qblocks = [(s_ * 256, [(s_ * 256 + kc * 128, 2 * s_ + kc) for kc in range(2)]) for s_ in range(4)]
            ACCS = rview(69632, [128, 4, 129], F32)
            its = []
            for qbi, (q0, keys) in enumerate(qblocks):
                for h in range(4):
                    for ki, (kcol, vch) in enumerate(keys):
                        its.append((qbi, q0, h, ki, len(keys), kcol, vch))

            head_no = {}
            for (qbi_, q0_, h_, ki_, nk_, kcol_, vch_) in its:
                if (qbi_, h_) not in head_no:
                    head_no[(qbi_, h_)] = len(head_no)

            def att_scores(n):
                qbi, q0, h, ki, nk, kcol, vch = its[n]
                sbank = n % 2
                psc = G[:, sbank, :].rearrange("p (s q) -> p s q", s=2)
                for s_ in range(2):
                    S.op("pe", lambda e, s_=s_: e.matmul(
                        psc[:, s_, :], KAT[s_][:, h, kcol:kcol + 128], QAT[:, h, q0:q0 + 256], start=True, stop=True),
                        reads=[bKAT[h], bQAT[h]], writes=[bG[sbank]])

            def att_exp_pv(n):
                qbi, q0, h, ki, nk, kcol, vch = its[n]
                sbank = n % 2
                pts = n % 3
                oas = qbi % 2
                S.op("act", lambda e: e.activation(out=PT[:, pts, :], in_=G[:, sbank, :], func=AF.Exp, scale=0.125),
                     reads=[bG[sbank]], writes=[bPT[pts]])
                ptv = PT[:, pts, :].rearrange("p (s q) -> p s q", s=2)
                hn = head_no[(qbi, h)]
                ab = 2 + 2 * (hn % 2)
                for qt in range(2):
                    for s_ in range(2):
                        bank = ab + qt
                        S.op("pe", lambda e, qt=qt, s_=s_, bank=bank: e.matmul(
                            G[:, bank, s_ * 129:(s_ + 1) * 129], ptv[:, s_, qt * 128:(qt + 1) * 128], VA[:, vch, h, :],
                            start=(ki == 0 and s_ == 0), stop=(ki == nk - 1), skip_group_check=True),
                            reads=[bPT[pts], pa["VA"]], writes=[bG[bank]])
                if ki != nk - 1:
                    return
                accv = G[:, ab:ab + 2, 0:258].rearrange("p q (s c) -> p q s c", s=2)
                acc_b = [bG[ab], bG[ab + 1]]
                st_ap, st_b = stat_cols(8)
                S.op("dve", lambda e: e.reciprocal(st_ap[:, 0:4].rearrange("p (q s) -> p q s", q=2), accv[:, :, :, 128]),
                     reads=acc_b, writes=[st_b])
                S.op("dve", lambda e: e.tensor_tensor(
                    out=st_ap[:, 4:8].rearrange("p (q s) -> p q s", q=2), in0=st_ap[:, 0:4].rearrange("p (q s) -> p q s", q=2),
                    in1=LAMS[:, 4:6].unsqueeze(1).to_broadcast([128, 2, 2]), op=ALU.mult), reads=[st_b, bC], writes=[st_b])
                for qt in range(2):
                    S.op("dve", lambda e, qt=qt: e.tensor_scalar_mul(T1[:, qt * 128:(qt + 1) * 128], accv[:, qt, 1, 0:128],
                                                                   st_ap[:, 4 + 2 * qt + 1:4 + 2 * qt + 2]),
                         reads=[bG[ab + qt], st_b], writes=[pa["T1"]])
                    S.op("dve", lambda e, qt=qt: e.scalar_tensor_tensor(
                        out=OA4[:, oas, qt, h * 128:(h + 1) * 128], in0=accv[:, qt, 0, 0:128],
                        scalar=st_ap[:, 4 + 2 * qt:4 + 2 * qt + 1], in1=T1[:, qt * 128:(qt + 1) * 128], op0=ALU.mult, op1=ALU.add),
                        reads=[bG[ab + qt], st_b, pa["T1"]], writes=[bOAq[oas][qt]])
                if h != 3:
                    return
                for qt in range(2):
                    tile_i = (q0 // 128) + qt
                    oav = OA4[:, oas, qt, :]
                    st_ap2, st_b2 = SMALL[:, 64 + 8 * (ep_i[0] % 2):72 + 8 * (ep_i[0] % 2)], smb("ep%d" % (ep_i[0] % 2))
                    ep_i[0] += 1
                    ob = qt
                    pslot = tile_i % 2
                    ptv2 = PTR[:, pslot, 0:512].rearrange("p (h c) -> p h c", h=4)
                    bo = bOAq[oas][qt]

                    def t_sq(oav=oav, bo=bo):
                        S.op("act", lambda e: e.activation(out=SQ, in_=oav, func=AF.Square), reads=[bo], writes=[pa["SQ"]])

                    def t_red(st_ap2=st_ap2, st_b2=st_b2):
                        S.op("dve", lambda e: e.reduce_sum(out=st_ap2[:, 0:4], in_=SQ.rearrange("p (h d) -> p h d", h=4), axis=AX.X),
                             reads=[pa["SQ"]], writes=[st_b2])

                    def t_rstd(st_ap2=st_ap2, st_b2=st_b2):
                        rstd_from_ss(st_ap2[:, 0:4], st_ap2[:, 4:8], 128.0, [st_b2])

                    def t_mul(oav=oav, bo=bo, st_ap2=st_ap2, st_b2=st_b2, ob=ob):
                        S.op("dve", lambda e: e.tensor_tensor(
                            out=ON.rearrange("p (h d) -> p h d", h=4), in0=oav.rearrange("p (h d) -> p h d", h=4),
                            in1=st_ap2[:, 4:8].unsqueeze(2).to_broadcast([128, 4, 128]), op=ALU.mult),
                            reads=[bo, st_b2], writes=[pa["ON"]])
                        S.op("dve", lambda e: e.tensor_tensor(out=OB16r[ob], in0=ON, in1=DG[:].rearrange("p h d -> p (h d)"), op=ALU.mult),
                             reads=[pa["ON"], bC], writes=[bOB16r[ob]])

                    def t_tr(ob=ob, ptv2=ptv2, pslot=pslot):
                        for hh_ in range(4):
                            S.op("pe", lambda e, hh_=hh_: e.transpose(ptv2[:, hh_, :], OB16r[ob][:, hh_ * 128:(hh_ + 1) * 128], IDB[:]),
                                 reads=[bOB16r[ob], bC], writes=[bPTR[pslot]])

                    def t_cp(ptv2=ptv2, pslot=pslot, tile_i=tile_i):
                        S.op("act", lambda e: e.copy(OT[:, 0:4, tile_i * 128:(tile_i + 1) * 128], ptv2),
                             reads=[bPTR[pslot]], writes=[bOT[tile_i]])

                    pending.extend([t_sq, t_red, t_rstd, t_mul, t_tr, t_cp])

            ep_i = [0]
            pending = []
            OB16r = [OB16, rview(71936, [128, 512], BF16)]
            bOB16r = [pa["OB16"], pa["OB16b"]]
            for n in range(len(its) + 1):
                if n < len(its):
                    att_scores(n)
                if n >= 1:
                    att_exp_pv(n - 1)
                for _ in range(1 if is_sample else 2):
                    if pending:
                        pending.pop(0)()
                if sg == 0 and n % 18 == 9:
                    ada_step(cb=6, rb=7)
            while pending:
                pending.pop(0)()

            stage(4 + 10 * sg)
            names = (["GT%d" % i for i in range(NT)] + ["QT0", "QT1", "KT0", "KT1", "KH0", "KH1", "RBG", "S160", "S161", "SQ", "ON", "GLT",
                     "EB0", "EB1"] + ["VB%d" % i for i in range(NT)]
                     + ["ETP0", "ETP1", "ETM0", "ETM1", "EH0", "EH1", "XG0", "XG1", "RBT0", "RBT1", "OB0", "OB1"]
                     + ["ATT%d%d" % (r_, d_) for r_ in range(2) for d_ in range(2)]
                     + ["S32_%d%d%d%d" % (d_, r_, pr, hh) for d_ in range(2) for r_ in range(2) for pr in range(2) for hh in range(2)])
            pg = new_phase(names)
            GT = rview(0, [128, 8, 512], F32)
            QTt = [rview(16384 + d_ * 4096, [128, 2, T], BF16) for d_ in range(2)]
            KTt = [rview(24576 + d_ * 4096, [128, 2, T], BF16) for d_ in range(2)]
            KH = [rview(32768 + d_ * 4096, [128, 8, 256], BF16) for d_ in range(2)]
            VB = rview(40960, [128, 8, 512], BF16)
            RBG = rview(49152, [128, 8, 512], BF16)
            S16 = [rview(57344 + d_ * 8192, [128, 2, 16, 128], BF16) for d_ in range(2)]
            RBT = [rview(57344 + r_ * 2048, [128, 512], F32) for r_ in range(2)]
            ETP = [rview(73728 + r_ * 1024, [128, 2, 128], F32) for r_ in range(2)]
            ETM = [rview(75776 + r_ * 1024, [128, 2, 128], F32) for r_ in range(2)]
            EH = [rview(77824 + r_ * 1024, [128, 256], F32) for r_ in range(2)]
            XG = [rview(79872 + r_ * 2048, [128, 512], F32) for r_ in range(2)]
            S32 = [[rview(79872 + (d_ * 2 + r_) * 1024, [128, 2, 128], F32) for r_ in range(2)] for d_ in range(2)]
            ATT = [[rview(83968 + (r_ * 2 + d_) * 1024, [128, 4, 128], BF16) for d_ in range(2)] for r_ in range(2)]
            SQ2 = rview(88064, [128, 512], F32)
            ON2 = rview(90112, [128, 512], F32)
            OB2 = [rview(92160 + r_ * 1024, [128, 512], BF16) for r_ in range(2)]
            GLT = rview(94208, [64, T], BF16)
            EB = [rview(96256 + d_ * 128, [128, 2, 16], F32) for d_ in range(2)]
            bGT = [pg["GT%d" % i] for i in range(NT)]
            bVB = [pg["VB%d" % i] for i in range(NT)]
            bQT = [pg["QT0"], pg["QT1"]]
            bKT = [pg["KT0"], pg["KT1"]]
            bKH = [pg["KH0"], pg["KH1"]]
            bS16 = [pg["S160"], pg["S161"]]
            bEB = [pg["EB0"], pg["EB1"]]
            bETP = [pg["ETP0"], pg["ETP1"]]
            bETM = [pg["ETM0"], pg["ETM1"]]
            bEH = [pg["EH0"], pg["EH1"]]
            bXG = [pg["XG0"], pg["XG1"]]
            bRBT = [pg["RBT0"], pg["RBT1"]]
            bOB = [pg["OB0"], pg["OB1"]]
            bATT = [[pg["ATT%d%d" % (r_, d_)] for d_ in range(2)] for r_ in range(2)]
            bS32 = [[[[pg["S32_%d%d%d%d" % (d_, r_, pr, hh)] for hh in range(2)] for pr in range(2)] for r_ in range(2)] for d_ in range(2)]
            for b_ in bQT + bKT + bKH + bS16 + bEB + [pg["RBG"]]:
                b_.same_ok = True

            slot, wv = ws_load([(0, 16, wcols(w_in_d, 3072, 16)), (32, 16, wcols(w_in_d, 3088, 16))])
            for half in range(2):
                bank = proj_fm(wv, slot, 0, 64, half)
                S.op("act", lambda e, bank=bank, half=half: e.copy(GLT[:, half * 512:(half + 1) * 512], G[0:64, bank, :]),
                     reads=[bG[bank]], writes=[pg["GLT"]])
            for i in range(NT):
                r_ = i % 2
                gb0 = 4 if r_ == 0 else 2
                for d_ in range(2):
                    S.op("pe", lambda e, i=i, d_=d_, gb0=gb0: e.matmul(
                        G[:, gb0 + d_, 0:256], GLT[32 * d_:32 * d_ + 16, i * 128:(i + 1) * 128],
                        WG[32 * d_:32 * d_ + 16, :], start=True, stop=True), reads=[pg["GLT"], bC], writes=[bG[gb0 + d_]])
                S.op("dve", lambda e, r_=r_, gb0=gb0: e.tensor_tensor(out=XG[r_].rearrange("p (a b) -> p a b", a=2),
                                                                      in0=G[:, gb0:gb0 + 2, 0:256], in1=BG[:], op=ALU.add),
                     reads=[bG[gb0], bG[gb0 + 1], bC], writes=[bXG[r_]])
                S.op("act", lambda e, r_=r_: e.activation(out=XG[r_], in_=XG[r_], func=AF.Exp, scale=-1.0), reads=[bXG[r_]], writes=[bXG[r_]])
                S.op("act", lambda e, i=i, r_=r_: e.activation(out=GT[:, i, :], in_=XG[r_], func=AF.Ln, bias=EPSC[:, 1:2]),
                     reads=[bXG[r_], bC], writes=[bGT[i]])
            slot, wv = ws_load([(0, 512, wcols(w_in_d, 2048, 512))])
            for i in range(NT):
                bank = proj_tm(wv, slot, 0, 512, i)
                S.op("act", lambda e, bank=bank, i=i: e.copy(VB[:, i, :], G[:, bank, :]), reads=[bG[bank]], writes=[bVB[i]])
            slot, wv = ws_load([(0, 512, wcols(w_in_d, 2560, 512))])
            for i in range(NT):
                bank = proj_tm(wv, slot, 0, 512, i)
                r_ = i % 2
                S.op("act", lambda e, bank=bank, r_=r_: e.activation(out=RBT[r_], in_=G[:, bank, :], func=AF.Silu), reads=[bG[bank]], writes=[bRBT[r_]])
                S.op("dve", lambda e, i=i, r_=r_: e.tensor_tensor(out=RBG[:, i, :], in0=RBT[r_], in1=GG[:].rearrange("p h d -> p (h d)"), op=ALU.mult),
                     reads=[bRBT[r_], bC], writes=[pg["RBG"]])
            slot, wv = ws_load([(0, 512, wcols(w_in_d, 1536, 512))])
            pi = 0
            for half in range(2):
                for c_ in range(4):
                    for kc in range(8):
                        S.op("pe", lambda e, c_=c_, kc=kc, half=half, wv=wv: e.matmul(
                            G[:, c_, :], wv[:, kc, c_ * 128:(c_ + 1) * 128], HT[:, kc, half * 512:(half + 1) * 512],
                            start=(kc == 0), stop=(kc == 7)), reads=[bWS[slot]] + bHT[half * 4:half * 4 + 4], writes=[bG[c_]])
                for ti in range(4):
                    i = half * 4 + ti
                    kb_bank = 4 + (i % 2)
                    for kc in range(8):
                        S.op("pe", lambda e, kc=kc, i=i, wv=wv, kb_bank=kb_bank: e.matmul(
                            G[:, kb_bank, 0:256], HT[:, kc, i * 128:(i + 1) * 128], wv[:, kc, 256:512], start=(kc == 0), stop=(kc == 7)),
                            reads=[bWS[slot], bHT[i]], writes=[bG[kb_bank]])
                    for d_ in range(2):
                        r_ = pi % 2
                        cb = 6 + r_
                        pi += 1
                        for pr in range(2):
                            S.op("pe", lambda e, i=i, d_=d_, pr=pr, cb=cb: e.matmul(
                                G[:, cb, pr * 128:(pr + 1) * 128], GT[:, i, d_ * 256 + pr * 128:d_ * 256 + (pr + 1) * 128],
                                MASKS[:, 2 + d_, :], start=True, stop=True), reads=[bGT[i], bC], writes=[bG[cb]])
                        S.op("pe", lambda e, i=i, d_=d_, cb=cb: e.matmul(
                            G[:, cb, 256:512], MASKS[:, 4 + d_, :], GT[:, i, d_ * 256:(d_ + 1) * 256], start=True, stop=True),
                            reads=[bGT[i], bC], writes=[bG[cb]])
                        S.op("act", lambda e, r_=r_, cb=cb: e.activation(out=ETP[r_].rearrange("p a b -> p (a b)"), in_=G[:, cb, 0:256], func=AF.Exp),
                             reads=[bG[cb]], writes=[bETP[r_]])
                        S.op("act", lambda e, r_=r_, cb=cb: e.activation(out=ETM[r_].rearrange("p a b -> p (a b)"), in_=G[:, cb, 0:256], func=AF.Exp, scale=-1.0),
                             reads=[bG[cb]], writes=[bETM[r_]])
                        S.op("act", lambda e, r_=r_, cb=cb: e.activation(out=EH[r_], in_=G[:, cb, 256:512], func=AF.Exp), reads=[bG[cb]], writes=[bEH[r_]])
                        S.op("dve", lambda e, d_=d_, i=i, ti=ti, r_=r_: e.scalar_tensor_tensor(
                            out=QTt[d_][:, :, i * 128:(i + 1) * 128], in0=G[:, 0:2, ti * 128:(ti + 1) * 128], scalar=0.125,
                            in1=ETP[r_], op0=ALU.mult, op1=ALU.mult), reads=[bG[0], bG[1], bETP[r_]], writes=[bQT[d_]])
                        S.op("dve", lambda e, d_=d_, i=i, ti=ti, r_=r_: e.tensor_tensor(
                            out=KTt[d_][:, :, i * 128:(i + 1) * 128], in0=G[:, 2:4, ti * 128:(ti + 1) * 128], in1=ETM[r_], op=ALU.mult),
                            reads=[bG[2], bG[3], bETM[r_]], writes=[bKT[d_]])
                        S.op("dve", lambda e, d_=d_, i=i, r_=r_, kb_bank=kb_bank: e.tensor_tensor(
                            out=KH[d_][:, i, :], in0=G[:, kb_bank, 0:256], in1=EH[r_], op=ALU.mult),
                            reads=[bG[kb_bank], bEH[r_]], writes=[bKH[d_]])
                        col = 63 if d_ == 0 else 0
                        S.op("dve", lambda e, d_=d_, i=i, col=col, r_=r_: e.tensor_copy(
                            EB[d_][:, :, 2 * i:2 * i + 2], ETP[r_].rearrange("p a (c t) -> p a c t", c=2)[:, :, :, col]),
                            reads=[bETP[r_]], writes=[bEB[d_]])
            stage(5 + 10 * sg)
            for d_ in range(2):
                bS16[d_].inherit(bRBT)
                for r_ in range(2):
                    for pr in range(2):
                        for hh in range(2):
                            bS32[d_][r_][pr][hh].inherit(bXG)
            nseq = 1 if is_sample else 4
            cps = 16 // nseq
            kv_i = 0
            for k_ in range(16):
                for d_ in range(2):
                    c_ = k_ if d_ == 0 else 15 - k_
                    cur, nxt = k_ % 2, (k_ + 1) % 2
                    seq = c_ // cps
                    first = (c_ % cps == 0) if d_ == 0 else (c_ % cps == cps - 1)
                    last = (c_ % cps == cps - 1) if d_ == 0 else (c_ % cps == 0)
                    cur_b = [bS32[d_][cur][pr][0] for pr in range(2)]
                    nxt_b = [bS32[d_][nxt][pr][0] for pr in range(2)]
                    if first:
                        if is_sample:
                            S.dma("sp", lambda e, d_=d_, cur=cur: e.dma_start(
                                out=S32[d_][cur], in_=state_d[d_].rearrange("(pr hh) k v -> (hh k) pr v", hh=2)), writes=cur_b)
                        else:
                            S.op("dve", lambda e, d_=d_, cur=cur: e.memset(S32[d_][cur], 0.0), writes=cur_b)
                    S.op("act", lambda e, d_=d_, c_=c_, cur=cur: e.copy(S16[d_][:, :, c_, :], S32[d_][cur]), reads=cur_b, writes=[bS16[d_]])
                    i, ch = c_ // 2, c_ % 2
                    bank = kv_i % 4
                    kv_i += 1
                    for pr in range(2):
                        for hh in range(2):
                            h = 2 * pr + hh
                            S.op("pe", lambda e, d_=d_, i=i, ch=ch, pr=pr, hh=hh, h=h, bank=bank: e.matmul(
                                G[hh * 64:(hh + 1) * 64, bank, pr * 128:(pr + 1) * 128],
                                KH[d_][ch * 64:(ch + 1) * 64, i, h * 64:(h + 1) * 64],
                                VB[ch * 64:(ch + 1) * 64, i, h * 128:(h + 1) * 128], start=True, stop=True),
                                reads=[bKH[d_], bVB[i]], writes=[bG[bank]])
                    for pr in range(2):
                        S.op("dve", lambda e, d_=d_, pr=pr, c_=c_, bank=bank, cur=cur, nxt=nxt: e.scalar_tensor_tensor(
                            out=S32[d_][nxt][:, pr, :], in0=S32[d_][cur][:, pr, :], scalar=EB[d_][:, pr, c_:c_ + 1],
                            in1=G[:, bank, pr * 128:(pr + 1) * 128], op0=ALU.mult, op1=ALU.add),
                            reads=[bS32[d_][cur][pr][0], bEB[d_], bG[bank]], writes=[bS32[d_][nxt][pr][0]])
                    if last and not is_sample:
                        S.dma("sp", lambda e, d_=d_, seq=seq, nxt=nxt: e.dma_start(
                            out=news_d[d_][seq].rearrange("(pr hh) k v -> (hh k) pr v", hh=2), in_=S32[d_][nxt]), reads=nxt_b)
            def gla_AT(i):
                for d_ in range(2):
                    for h in (0, 2, 1, 3):
                        pr, hh = h // 2, h % 2
                        S.op("pe", lambda e, d_=d_, pr=pr, hh=hh: e.matmul(
                            G[:, 2 + hh, (d_ * 2 + pr) * 128:(d_ * 2 + pr + 1) * 128],
                            KTt[d_][hh * 64:(hh + 1) * 64, pr, i * 128:(i + 1) * 128],
                            QTt[d_][hh * 64:(hh + 1) * 64, pr, i * 128:(i + 1) * 128], start=True, stop=True),
                            reads=[bKT[d_], bQT[d_]], writes=[bG[2 + hh]])
                r_ = i % 2
                for d_ in range(2):
                    for hh in range(2):
                        S.op("dve", lambda e, d_=d_, hh=hh: e.tensor_tensor(
                            out=ATT[r_][d_].rearrange("p (pr hh) t -> p pr hh t", hh=2)[:, :, hh, :],
                            in0=G[:, 2 + hh, d_ * 256:(d_ + 1) * 256].rearrange("p (pr t) -> p pr t", pr=2),
                            in1=MASKS[:, d_, :].unsqueeze(1).to_broadcast([128, 2, 128]), op=ALU.mult),
                            reads=[bG[2 + hh], bC], writes=[bATT[r_][d_]])

            def gla_O(i):
                r_ = i % 2
                ob = 4 + r_
                for h in range(4):
                    pr, hh = h // 2, h % 2
                    for ch in range(2):
                        c_ = 2 * i + ch
                        outp = G[ch * 64:(ch + 1) * 64, ob, h * 128:(h + 1) * 128]
                        tcols = slice(i * 128 + ch * 64, i * 128 + (ch + 1) * 64)
                        for d_ in range(2):
                            S.op("pe", lambda e, d_=d_, h=h, ch=ch, outp=outp: e.matmul(
                                outp, ATT[r_][d_][:, h, ch * 64:(ch + 1) * 64], VB[:, i, h * 128:(h + 1) * 128],
                                start=(d_ == 0), stop=False), reads=[bATT[r_][d_], bVB[i]], writes=[bG[ob]])
                        for d_ in range(2):
                            S.op("pe", lambda e, d_=d_, pr=pr, hh=hh, c_=c_, outp=outp, tcols=tcols: e.matmul(
                                outp, QTt[d_][hh * 64:(hh + 1) * 64, pr, tcols], S16[d_][hh * 64:(hh + 1) * 64, pr, c_, :],
                                start=False, stop=(d_ == 1)), reads=[bQT[d_], bS16[d_]], writes=[bG[ob]])

            def gla_E1(i):
                r_ = i % 2
                ob = 4 + r_
                st_ap, st_b = stat_cols(8)
                S.op("act", lambda e: e.activation(out=SQ2, in_=G[:, ob, :], func=AF.Square), reads=[bG[ob]], writes=[pg["SQ"]])
                S.op("dve", lambda e: e.reduce_sum(out=st_ap[:, 0:4], in_=SQ2.rearrange("p (h d) -> p h d", h=4), axis=AX.X),
                     reads=[pg["SQ"]], writes=[st_b])
                rstd_from_ss(st_ap[:, 0:4], st_ap[:, 4:8], 128.0, [st_b])
                S.op("dve", lambda e: e.tensor_tensor(
                    out=ON2.rearrange("p (h d) -> p h d", h=4), in0=G[:, ob, :].rearrange("p (h d) -> p h d", h=4),
                    in1=st_ap[:, 4:8].unsqueeze(2).to_broadcast([128, 4, 128]), op=ALU.mult),
                    reads=[bG[ob], st_b], writes=[pg["ON"]])
                S.op("dve", lambda e: e.tensor_tensor(out=OB2[r_], in0=ON2, in1=RBG[:, i, :], op=ALU.mult),
                     reads=[pg["ON"], pg["RBG"]], writes=[bOB[r_]])

            def gla_TR(i):
                r_ = i % 2
                ptv2 = PTR[:, r_, 0:512].rearrange("p (h c) -> p h c", h=4)
                for h in range(4):
                    S.op("pe", lambda e, h=h: e.transpose(ptv2[:, h, :], OB2[r_][:, h * 128:(h + 1) * 128], IDB[:]),
                         reads=[bOB[r_], bC], writes=[bPTR[r_]])
                S.op("act", lambda e: e.copy(OT[:, 4:8, i * 128:(i + 1) * 128], ptv2), reads=[bPTR[r_]], writes=[bOT[i]])

            stage(6 + 10 * sg)
            dead = bGT + bKH + bETP + bETM + bEH + bXG + [pg["GLT"]] + \
                [bS32[d_][r_][pr][0] for d_ in range(2) for r_ in range(2) for pr in range(2)]
            x1_offs = [0, 4096, 8192, 12288, 32768, 36864, 73728, 77824]
            X1t = [rview(o_, [128, D], F32) for o_ in x1_offs]
            bX1 = [Buf("X1_%d" % i).inherit(dead) for i in range(NT)]
            if sg == 0:
                htf = HT[:].rearrange("p k t -> p (k t)").bitcast(F32).rearrange("p (a b) -> p a b", a=4)
                bTB = [Buf("TB0").inherit(bHT), Buf("TB1").inherit(bHT)]
                fin_tasks = adaln_finish_tasks([htf[:, 0, :], htf[:, 1, :]], bTB)
            else:
                fin_tasks = []

            def post_residual(ps_banks, ps_ap, base_ap, base_bufs, gate_i, out_ap, out_bufs, junk_slot):
                st_ap, st_b = stat_cols(2)
                S.op("act", lambda e: e.activation(out=XN[:, junk_slot, :], in_=ps_ap, func=AF.Square, accum_out=st_ap[:, 0:1]),
                     reads=ps_banks, writes=[bXN[junk_slot], st_b])
                rstd_from_ss(st_ap[:, 0:1], st_ap[:, 1:2], float(D), [st_b])
                S.op("dve", lambda e: e.scalar_tensor_tensor(out=out_ap, in0=ps_ap, scalar=st_ap[:, 1:2], in1=GB[:, ty, gate_i, :],
                                                             op0=ALU.mult, op1=ALU.mult), reads=ps_banks + [st_b, bGB], writes=out_bufs)
                S.op("dve", lambda e: e.tensor_tensor(out=out_ap, in0=out_ap, in1=base_ap, op=ALU.add),
                     reads=out_bufs + base_bufs, writes=out_bufs)

            slot_a, wva = ws_load([(0, 512, wcols(w_out_d, 0, 512))])
            slot_b, wvb = ws_load([(0, 512, wcols(w_out_d, 512, 512))])
            st2 = {}

            def p4_M(i):
                for half, (sl, wv_) in enumerate(((slot_a, wva), (slot_b, wvb))):
                    for kc in range(8):
                        S.op("pe", lambda e, kc=kc, half=half, wv_=wv_: e.matmul(
                            G[:, half, :], OT[:, kc, i * 128:(i + 1) * 128], wv_[:, kc, :], start=(kc == 0), stop=(kc == 7)),
                            reads=[bWS[sl], bOT[i]], writes=[bG[half]])
                xs = i % 2
                S.dma("sp", lambda e: e.dma_start(out=XT[:, xs, :], in_=x_d[row0 + i * 128: row0 + (i + 1) * 128, :]),
                      writes=[bXT[xs]])

            def p4_A2(i):
                st2[i] = norm_stats(X1t[i], [bX1[i]], i % 2)
                S.op("dve", lambda e: e.tensor_scalar_mul(XN[:, i % 2, :], X1t[i], st2[i][0][:, 1:2]),
                     reads=[bX1[i], st2[i][1]], writes=[bXN[i % 2]])

            def p4_TR(i):
                pslot = i % 2
                ptv = PTR[:, pslot, :].rearrange("p (k c) -> p k c", k=8)
                for kc in range(8):
                    S.op("pe", lambda e, kc=kc: e.transpose(ptv[:, kc, :], XN[:, i % 2, kc * 128:(kc + 1) * 128], IDB[:]),
                         reads=[bXN[i % 2], bC], writes=[bPTR[pslot]])

            def p4_D1(i):
                xs = i % 2
                post_residual([bG[0], bG[1]], G[:, 0:2, :].rearrange("p a b -> p (a b)"), XT[:, xs, :], [bXT[xs]], 0,
                              X1t[i], [bX1[i]], 2)

            OFF = DBG.get("off", 4)
            for t in range(NT + OFF + 4):
                s_ = t - OFF
                if 0 <= s_ - 3 < NT:
                    norm_evac(s_ - 3, ty, 1, eng="mix")
                if fin_tasks:
                    fin_tasks.pop(0)()
                    if not fin_tasks:
                        for b_ in bHT:
                            b_.inherit(bTB)
                if 0 <= s_ < NT:
                    p4_M(s_)
                if t < NT:
                    gla_AT(t)
                if 1 <= t <= NT:
                    gla_O(t - 1)
                if 0 <= s_ - 1 < NT:
                    p4_A2(s_ - 1)
                if 2 <= t <= NT + 1:
                    gla_E1(t - 2)
                if 0 <= s_ < NT:
                    p4_D1(s_)
                if 3 <= t <= NT + 2:
                    gla_TR(t - 3)
                if 0 <= s_ - 2 < NT:
                    p4_TR(s_ - 2)

            pm = new_phase(["FT%d" % c for c in range(8)] + ["F1_%d" % j for j in range(8)] + ["RL0", "RL1", "RL2"])
            R_live = R_live + bX1
            F1 = rview(16384, [128, 8, T], BF16)
            FT = rview(40960, [128, 8, T], F32)
            RL = rview(81920, [128, 3, 512], F32)
            bFT = [pm["FT%d" % c] for c in range(8)]
            bF1 = [pm["F1_%d" % j] for j in range(8)]
            bRL = [pm["RL0"], pm["RL1"], pm["RL2"]]

            stage(7 + 10 * sg)
            bFTh = [[Buf("FT%d_%d" % (c, hf)).inherit([bFT[c]]) for hf in range(2)] for c in range(8)]
            R_live = R_live + [b_ for pair in bFTh for b_ in pair]

            def phase6_tile(i):
                b0 = 4 + (i % 2) * 2
                hf = i // 4
                for c_ in range(8):
                    bank = b0 + c_ // 4
                    S.op("pe", lambda e, c_=c_, bank=bank: e.transpose(
                        G[:, bank, (c_ % 4) * 128:(c_ % 4 + 1) * 128], FT[:, c_, i * 128:(i + 1) * 128], IDF[:]),
                        reads=[bFTh[c_][hf], bC], writes=[bG[bank]])
                xs = i % 2
                post_residual([bG[b0], bG[b0 + 1]], G[:, b0:b0 + 2, :].rearrange("p a b -> p (a b)"), X1t[i], [bX1[i]], 1,
                              XT[:, xs, :], [bXT[xs]], 2)
                S.dma("sp", lambda e: e.dma_start(out=y_d[row0 + i * 128: row0 + (i + 1) * 128, :], in_=XT[:, xs, :]),
                      reads=[bXT[xs]])

            rl_i = 0
            for q in range(4):
                for blk in range(2):
                    slot, wv = ws_load([(0, 512, wcols(w1_d, q * 1024 + blk * 512, 512))])
                    for j in range(4):
                        jj = blk * 4 + j
                        for half in range(2):
                            bank = proj_fm(wv, slot, j * 128, 128, half)
                            rs = rl_i % 3
                            rl_i += 1
                            S.op("act", lambda e, bank=bank, rs=rs: e.activation(out=RL[:, rs, :], in_=G[:, bank, :], func=AF.Relu),
                                 reads=[bG[bank]], writes=[bRL[rs]])
                            S.op("dve", lambda e, rs=rs, jj=jj, half=half: e.tensor_tensor(
                                out=F1[:, jj, half * 512:(half + 1) * 512], in0=RL[:, rs, :], in1=RL[:, rs, :], op=ALU.mult),
                                reads=[bRL[rs]], writes=[bF1[jj]])
                w2b = [ws_load([(0, 512, wrows(w2_d, q * 1024, blk * 512, 512))]) for blk in range(2)]
                order = [(blk, c4, half) for blk in range(2) for c4 in range(4) for half in range(2)] if q < 3 else \
                        [(blk, c4, half) for half in range(2) for blk in range(2) for c4 in range(4)]
                for gi_, (blk, c4, half) in enumerate(order):
                    slot, wv = w2b[blk]
                    c_ = blk * 4 + c4
                    bank = (pp_i[0]) % 4
                    pp_i[0] += 1
                    for hc in range(8):
                        S.op("pe", lambda e, hc=hc, c4=c4, half=half, bank=bank, wv=wv: e.matmul(
                            G[:, bank, :], wv[:, hc, c4 * 128:(c4 + 1) * 128], F1[:, hc, half * 512:(half + 1) * 512],
                            start=(hc == 0), stop=(hc == 7)), reads=[bWS[slot], bF1[hc]], writes=[bG[bank]])
                    dst = FT[:, c_, half * 512:(half + 1) * 512]
                    if q == 0:
                        S.op("act", lambda e, bank=bank, dst=dst: e.copy(dst, G[:, bank, :]), reads=[bG[bank]], writes=[bFTh[c_][half]])
                    else:
                        S.op("dve", lambda e, bank=bank, dst=dst: e.tensor_tensor(out=dst, in0=dst, in1=G[:, bank, :], op=ALU.add),
                             reads=[bG[bank], bFTh[c_][half]], writes=[bFTh[c_][half]])
                    if q == 3 and half == 1 and gi_ % 2 == 1:
                        phase6_tile((gi_ - 8) // 2)
            stage(8 + 10 * sg)
            for i in range(4, NT):
                phase6_tile(i)

        stage(1)
        run_sg(0)
        stage(9)
        run_sg(1)
        S.stopped = False
        S.finish("sp")
        S.emit(block)
    return nc


_NC_CACHE = {}


def kernel(x_prompt, x_sample, c, cache_k, cache_v, state_fwd, state_bwd, c_ctx,
           w_ada, b_ada, norm_attn_pre, norm_attn_post, norm_mlp_pre, norm_mlp_post,
           w_in, w_gate_fwd, b_gate_fwd, w_gate_bwd, b_gate_bwd,
           lam_q1, lam_k1, lam_q2, lam_k2, diff_norm, gla_norm, w_out, w_mlp1, w_mlp2):
    f = lambda a: np.ascontiguousarray(np.asarray(a, dtype=np.float32))
    x_prompt, x_sample = f(x_prompt), f(x_sample)
    consts = _host_consts()
    shared = {
        "w_ada": f(w_ada)[0], "b_ada": f(b_ada)[0],
        "norm_attn_pre": f(norm_attn_pre)[0], "norm_attn_post": f(norm_attn_post)[0],
        "norm_mlp_pre": f(norm_mlp_pre)[0], "norm_mlp_post": f(norm_mlp_post)[0],
        "w_in": f(w_in)[0], "w_gate_fwd": f(w_gate_fwd)[0], "w_gate_bwd": f(w_gate_bwd)[0],
        "b_gate_fwd": f(b_gate_fwd)[0], "b_gate_bwd": f(b_gate_bwd)[0],
        "lam_q1": f(lam_q1)[0], "lam_k1": f(lam_k1)[0], "lam_q2": f(lam_q2)[0], "lam_k2": f(lam_k2)[0],
        "diff_norm": f(diff_norm)[0], "gla_norm": f(gla_norm)[0],
        "w_out": f(w_out)[0], "w_mlp1": f(w_mlp1)[0], "w_mlp2": f(w_mlp2)[0],
    }
    shared.update(consts)
    in_maps = []
    for i in range(N_CORES):
        m = dict(shared)
        m["x"] = np.concatenate([x_sample[i], x_prompt[4 * i:4 * i + 4].reshape(1024, D)], axis=0)
        m["cvec"] = np.stack([f(c)[i], f(c_ctx)], axis=0)
        m["cache_k"] = f(cache_k)[i, 0]
        m["cache_v"] = f(cache_v)[i, 0]
        m["state_f"] = f(state_fwd)[i, 0]
        m["state_b"] = f(state_bwd)[i, 0]
        in_maps.append(m)
    if "nc" not in _NC_CACHE:
        _NC_CACHE["nc"] = build_nc()
    nc = _NC_CACHE["nc"]
    res = run_bass_kernel_spmd(nc, in_maps, core_ids=list(range(N_CORES)))
    outs = res.results
    y_sample = np.stack([outs[i]["y"][0:T] for i in range(N_CORES)], axis=0)
    y_prompt = np.concatenate([outs[i]["y"][T:2 * T].reshape(4, 256, D) for i in range(N_CORES)], axis=0)
    new_k = np.concatenate([outs[i]["new_k"] for i in range(N_CORES)], axis=0)[:, None]
    new_v = np.concatenate([outs[i]["new_v"] for i in range(N_CORES)], axis=0)[:, None]
    new_sf = np.concatenate([outs[i]["new_sf"] for i in range(N_CORES)], axis=0)[:, None]
    new_sb = np.concatenate([outs[i]["new_sb"] for i in range(N_CORES)], axis=0)[:, None]
    return (y_prompt.astype(np.float32), y_sample.astype(np.float32), new_k.astype(np.float32),
            new_v.astype(np.float32), new_sf.astype(np.float32), new_sb.astype(np.float32))
```
